# Optimizing a Trainium2 kernel written in Bass

```python
import math, functools
import jax, jax.numpy as jnp
from jax import lax
import numpy as np

D_MODEL = 1024
BATCH = 8
SEQ = 2048
DEPTH = 2
DEC_BATCH = 128
DEC_SEQ = 8
PAST_LEN = 2048
PAGE_SIZE = 128

D_CONV = D_MODEL // 2
CONV_WIDTH = 31
HEAD_DIM = 64
N_HEADS = (D_MODEL // 2) // HEAD_DIM
D_ATTN = N_HEADS * HEAD_DIM
KV_HEADS = 2
GROUP = N_HEADS // KV_HEADS
KV_DIM = 2 * KV_HEADS * HEAD_DIM
N_BRANCH = 3
IN_SIZES = (D_CONV, D_CONV, D_CONV, D_ATTN, KV_DIM, KV_DIM, KV_DIM, D_ATTN, N_BRANCH * N_HEADS)
D_IN = sum(IN_SIZES)
CMP_STRIDE = 16
CMP_LEN = 2 * CMP_STRIDE
CMP_HIDDEN = 128
SLC_BLOCK = 64
TOP_N = 8
WINDOW = 512
NUM_BUCKETS = 32
MAX_DISTANCE = 128
PLE_DIM = 256
Q_BLOCK = 128
EPS = 1e-6
NEG_INF = -1e30
FORCED_SCORE = 1e4
MASKED_SCORE = -1e4

kernel_name = 'hymba_conformer_nsa_decoder_step'


def rms_norm(x, g):
    xf = x.astype(jnp.float32)
    y = xf * lax.rsqrt(jnp.mean(xf * xf, axis=-1, keepdims=True) + EPS)
    return (y * g.astype(jnp.float32)).astype(x.dtype)


def layer_norm(x, g, b):
    xf = x.astype(jnp.float32)
    mu = jnp.mean(xf, axis=-1, keepdims=True)
    xc = xf - mu
    var = jnp.mean(xc * xc, axis=-1, keepdims=True)
    y = xc * lax.rsqrt(var + EPS) * g.astype(jnp.float32) + b.astype(jnp.float32)
    return y.astype(x.dtype)


def rel_bucket(dist):
    n = jnp.maximum(dist, 0)
    max_exact = NUM_BUCKETS // 2
    nf = jnp.maximum(n, 1).astype(jnp.float32)
    large = max_exact + (jnp.log(nf / max_exact) / math.log(MAX_DISTANCE / max_exact)
                         * (NUM_BUCKETS - max_exact)).astype(jnp.int32)
    large = jnp.minimum(large, NUM_BUCKETS - 1)
    return jnp.where(n < max_exact, n, large)


def masked_softmax(s, mask):
    p = jax.nn.softmax(jnp.where(mask, s, NEG_INF), axis=-1)
    return p * mask


def compress_blocks(x, pe, w1, w2):
    b, l = x.shape[:2]
    n_half = l // CMP_STRIDE
    halves = x[:, :n_half * CMP_STRIDE].reshape(b, n_half, CMP_STRIDE, KV_HEADS, HEAD_DIM)
    pe = pe.reshape(2, CMP_STRIDE, 1, HEAD_DIM)
    w1 = w1.reshape(2, CMP_STRIDE, HEAD_DIM, CMP_HIDDEN)
    hid = (jnp.einsum('bnsgd,sdh->bngh', halves[:, :-1] + pe[0], w1[0])
           + jnp.einsum('bnsgd,sdh->bngh', halves[:, 1:] + pe[1], w1[1]))
    return jnp.einsum('bngh,hd->bngd', jax.nn.silu(hid), w2)


def _chunks(a, n_chunks):
    b, q = a.shape[:2]
    return jnp.moveaxis(a.reshape(b, n_chunks, q // n_chunks, *a.shape[2:]), 1, 0)


def _unchunk(a):
    a = jnp.moveaxis(a, 0, 1)
    return a.reshape(a.shape[0], -1, *a.shape[3:])


def nsa_attention(q, gate_logits, kv_cmp, kv_slc, kv_win, win_start, cmp_pe, cmp_w1, cmp_w2, rel_bias):
    f32 = jnp.float32
    b, nq = q.shape[:2]
    l = kv_cmp.shape[1]
    scale = HEAD_DIM ** -0.5
    q_pos = (l - nq) + jnp.arange(nq, dtype=jnp.int32)
    qg = q.reshape(b, nq, KV_HEADS, GROUP, HEAD_DIM)
    tbl_g = rel_bias.astype(f32).reshape(NUM_BUCKETS, KV_HEADS, GROUP)
    tbl_t = jnp.transpose(tbl_g, (1, 0, 2))

    kc = compress_blocks(kv_cmp[:, :, 0], cmp_pe[0], cmp_w1[0], cmp_w2[0])
    vc = compress_blocks(kv_cmp[:, :, 1], cmp_pe[1], cmp_w1[1], cmp_w2[1])
    n_cmp = kc.shape[1]
    c_end = jnp.arange(n_cmp, dtype=jnp.int32) * CMP_STRIDE + (CMP_LEN - 1)
    c_dist = q_pos[:, None] - c_end[None, :]
    c_bias = jnp.transpose(tbl_g[rel_bucket(c_dist)], (0, 2, 3, 1))
    s_c = jnp.einsum('bqgrd,bcgd->bqgrc', qg, kc).astype(f32) * scale + c_bias
    p_cmp = masked_softmax(s_c, (c_dist >= 0)[:, None, None, :])
    o_cmp = jnp.einsum('bqgrc,bcgd->bqgrd', p_cmp.astype(vc.dtype), vc)

    n_slc = -(-l // SLC_BLOCK)
    top_n = min(TOP_N, n_slc)
    c_start = c_end - (CMP_LEN - 1)
    s_start = jnp.arange(n_slc, dtype=jnp.int32) * SLC_BLOCK
    cover = ((c_start[:, None] < s_start[None, :] + SLC_BLOCK)
             & (c_end[:, None] >= s_start[None, :])).astype(f32)
    imp = jnp.einsum('bqgrc,cj->bqgj', p_cmp, cover)
    cur = q_pos // SLC_BLOCK
    blk = jnp.arange(n_slc, dtype=jnp.int32)
    forced = (blk[None] == 0) | (blk[None] == cur[:, None]) | (blk[None] == cur[:, None] - 1)
    allowed = blk[None] <= cur[:, None]
    score = jnp.where(allowed[:, None], jnp.where(forced[:, None], FORCED_SCORE, imp), MASKED_SCORE)
    top_val, top_idx = lax.top_k(score, top_n)
    top_ok = top_val > 0.5 * MASKED_SCORE

    pad = n_slc * SLC_BLOCK - l
    kv_blk = jnp.pad(kv_slc, ((0, 0), (0, pad), (0, 0), (0, 0), (0, 0)))
    kv_blk = jnp.transpose(kv_blk.reshape(b, n_slc, SLC_BLOCK, 2, KV_HEADS, HEAD_DIM), (3, 0, 4, 1, 2, 5))
    k_blk, v_blk = kv_blk[0], kv_blk[1]
    kv_win_pad = jnp.pad(kv_win, ((0, 0), (WINDOW, 0), (0, 0), (0, 0), (0, 0)))
    b_ix = jnp.arange(b)[:, None, None, None]
    g_ix = jnp.arange(KV_HEADS)[None, None, :, None]

    def chunk_fn(args):
        qc, pc, ic, okc = args
        kg = k_blk[b_ix, g_ix, ic]
        vg = v_blk[b_ix, g_ix, ic]
        kpos = ic[..., None] * SLC_BLOCK + jnp.arange(SLC_BLOCK, dtype=jnp.int32)
        d = pc[None, :, None, None, None] - kpos
        bias = jnp.transpose(tbl_t[g_ix[..., None], rel_bucket(d)], (0, 1, 2, 5, 3, 4))
        s = jnp.einsum('bcgrd,bcgnkd->bcgrnk', qc, kg).astype(f32) * scale + bias
        m = ((d >= 0) & okc[..., None])[:, :, :, None]
        p = masked_softmax(s.reshape(*s.shape[:4], -1), m.reshape(*m.shape[:4], -1))
        o_s = jnp.einsum('bcgrnk,bcgnkd->bcgrd', p.reshape(s.shape).astype(vg.dtype), vg)
        c = pc.shape[0]
        kw = lax.dynamic_slice_in_dim(kv_win_pad, pc[0] - win_start, WINDOW + c, axis=1)
        wpos = pc[0] - WINDOW + jnp.arange(WINDOW + c, dtype=jnp.int32)
        wd = pc[:, None] - wpos[None, :]
        wm = (wd >= 0) & (wd < WINDOW) & (wpos[None, :] >= win_start)
        w_bias = jnp.transpose(tbl_g[rel_bucket(wd)], (0, 2, 3, 1))
        sw = jnp.einsum('bcgrd,bkgd->bcgrk', qc, kw[:, :, 0]).astype(f32) * scale + w_bias
        pw = masked_softmax(sw, wm[:, None, None, :])
        o_w = jnp.einsum('bcgrk,bkgd->bcgrd', pw.astype(kw.dtype), kw[:, :, 1])
        return o_s, o_w

    chunk = Q_BLOCK if nq % Q_BLOCK == 0 else nq
    n_chunks = nq // chunk
    o_slc, o_win = lax.map(chunk_fn, (_chunks(qg, n_chunks), q_pos.reshape(n_chunks, chunk),
                                      _chunks(top_idx, n_chunks), _chunks(top_ok, n_chunks)))
    o_slc, o_win = _unchunk(o_slc), _unchunk(o_win)

    g = jax.nn.sigmoid(gate_logits.astype(f32)).reshape(b, nq, N_BRANCH, KV_HEADS, GROUP, 1).astype(q.dtype)
    o = g[:, :, 0] * o_cmp + g[:, :, 1] * o_slc + g[:, :, 2] * o_win
    return o.reshape(b, nq, D_ATTN)


def mixer_layer(h, ple, conv_buf, past_cmp, past_slc, past_win, *, norm_g, w_in, conv_w, conv_b,
                conv_ln_g, conv_ln_b, cmp_pe, cmp_w1, cmp_w2, w_out, w_ple, w_ple_gate, rel_bias):
    b, nq, _ = h.shape
    u = rms_norm(h, norm_g)
    proj = jnp.einsum('bqd,de->bqe', u, w_in)
    splits = [int(s) for s in np.cumsum(IN_SIZES)[:-1]]
    glu_a, glu_b, z_conv, q, kv_c, kv_s, kv_w, z_attn, gate_logits = jnp.split(proj, splits, axis=-1)

    glu = glu_a * jax.nn.sigmoid(glu_b)
    xc = jnp.concatenate([conv_buf.astype(glu.dtype), glu], axis=1)
    dw = lax.conv_general_dilated(xc, conv_w[:, None, :].astype(xc.dtype), window_strides=(1,),
                                  padding='VALID', dimension_numbers=('NWC', 'WIO', 'NWC'),
                                  feature_group_count=D_CONV) + conv_b
    conv_out = jax.nn.silu(layer_norm(dw, conv_ln_g, conv_ln_b)) * jax.nn.silu(z_conv)

    def rows(a):
        return a.reshape(b, nq, 2, KV_HEADS, HEAD_DIM)
    kv_c, kv_s, kv_w = rows(kv_c), rows(kv_s), rows(kv_w)
    full_c = jnp.concatenate([past_cmp.astype(kv_c.dtype), kv_c], axis=1)
    full_s = jnp.concatenate([past_slc.astype(kv_s.dtype), kv_s], axis=1)
    win_all = jnp.concatenate([past_win.astype(kv_w.dtype), kv_w], axis=1)
    l = full_c.shape[1]
    win_start = l - win_all.shape[1]
    attn = nsa_attention(q.reshape(b, nq, N_HEADS, HEAD_DIM), gate_logits, full_c, full_s, win_all,
                         win_start, cmp_pe, cmp_w1, cmp_w2, rel_bias)

    mixed = jnp.concatenate([conv_out, attn * jax.nn.silu(z_attn)], axis=-1)
    h = h + jnp.einsum('bqe,ed->bqd', mixed, w_out)
    h = h + jax.nn.sigmoid(jnp.einsum('bqd,de->bqe', h, w_ple_gate)) * jnp.einsum('bqp,pd->bqd', ple, w_ple)
    new_win = win_all[:, -min(WINDOW, l):]
    new_conv = xc[:, -(CONV_WIDTH - 1):]
    return h, kv_c, kv_s, new_win, new_conv


def setup_inputs(seed: int = 0) -> dict:
    key = jax.random.key(seed)
    k = jax.random.split(key, 24)
    n_pages = PAST_LEN // PAGE_SIZE
    n_used = DEC_BATCH * n_pages
    n_pool = n_used + max(1, n_used // 4)
    w_buf = min(WINDOW, PAST_LEN)

    def nrm(kk, shape, scale=1.0):
        return jax.random.normal(kk, shape, jnp.float32) * scale

    page_table = jax.random.permutation(k[4], n_pool)[:n_used].reshape(DEC_BATCH, n_pages).astype(jnp.int32)
    return {
        'x_prompt': nrm(k[0], (BATCH, SEQ, D_MODEL)),
        'x_sample': nrm(k[1], (DEC_BATCH, DEC_SEQ, D_MODEL)),
        'cache_cmp_kv': nrm(k[2], (DEPTH, n_pool, PAGE_SIZE, 2, KV_HEADS, HEAD_DIM)),
        'cache_slc_kv': nrm(k[3], (DEPTH, n_pool, PAGE_SIZE, 2, KV_HEADS, HEAD_DIM)),
        'page_table': page_table,
        'state_win_kv': nrm(k[5], (DEPTH, DEC_BATCH, w_buf, 2, KV_HEADS, HEAD_DIM)),
        'state_conv': nrm(k[6], (DEPTH, DEC_BATCH, CONV_WIDTH - 1, D_CONV), 0.5),
        'p_prompt': nrm(k[7], (DEPTH, BATCH, SEQ, PLE_DIM)),
        'p_sample': nrm(k[8], (DEPTH, DEC_BATCH, DEC_SEQ, PLE_DIM)),
        'norm_g': 1.0 + nrm(k[9], (DEPTH, D_MODEL), 0.05),
        'w_in': nrm(k[10], (DEPTH, D_MODEL, D_IN), D_MODEL ** -0.5),
        'conv_w': nrm(k[11], (DEPTH, CONV_WIDTH, D_CONV), CONV_WIDTH ** -0.5),
        'conv_b': nrm(k[12], (DEPTH, D_CONV), 0.02),
        'conv_ln_g': 1.0 + nrm(k[13], (DEPTH, D_CONV), 0.05),
        'conv_ln_b': nrm(k[14], (DEPTH, D_CONV), 0.02),
        'cmp_pe': nrm(k[15], (DEPTH, 2, CMP_LEN, HEAD_DIM), 0.1),
        'cmp_w1': nrm(k[16], (DEPTH, 2, CMP_LEN, HEAD_DIM, CMP_HIDDEN), (CMP_LEN * HEAD_DIM) ** -0.5),
        'cmp_w2': nrm(k[17], (DEPTH, 2, CMP_HIDDEN, HEAD_DIM), CMP_HIDDEN ** -0.5),
        'w_out': nrm(k[18], (DEPTH, D_CONV + D_ATTN, D_MODEL), (D_CONV + D_ATTN) ** -0.5),
        'w_ple': nrm(k[19], (DEPTH, PLE_DIM, D_MODEL), PLE_DIM ** -0.5),
        'w_ple_gate': nrm(k[20], (DEPTH, D_MODEL, D_MODEL), D_MODEL ** -0.5),
        'rel_bias': nrm(k[21], (NUM_BUCKETS, N_HEADS), 0.5),
        'final_norm_g': 1.0 + nrm(k[22], (D_MODEL,), 0.05),
    }


def reference(x_prompt, x_sample, cache_cmp_kv, cache_slc_kv, page_table, state_win_kv, state_conv,
              p_prompt, p_sample, norm_g, w_in, conv_w, conv_b, conv_ln_g, conv_ln_b, cmp_pe, cmp_w1,
              cmp_w2, w_out, w_ple, w_ple_gate, rel_bias, final_norm_g):
    def paged_rows(pool):
        return pool[page_table].reshape(page_table.shape[0], -1, *pool.shape[2:])

    bp = x_prompt.shape[0]
    empty = jnp.zeros((bp, 0, 2, KV_HEADS, HEAD_DIM), x_prompt.dtype)
    conv_zero = jnp.zeros((bp, CONV_WIDTH - 1, D_CONV), x_prompt.dtype)
    hp, hs = x_prompt, x_sample
    cmp_p, cmp_s, slc_p, slc_s, win_p, win_s, conv_p, conv_s = [], [], [], [], [], [], [], []
    for i in range(DEPTH):
        layer = functools.partial(
            mixer_layer, norm_g=norm_g[i], w_in=w_in[i], conv_w=conv_w[i], conv_b=conv_b[i],
            conv_ln_g=conv_ln_g[i], conv_ln_b=conv_ln_b[i], cmp_pe=cmp_pe[i], cmp_w1=cmp_w1[i],
            cmp_w2=cmp_w2[i], w_out=w_out[i], w_ple=w_ple[i], w_ple_gate=w_ple_gate[i], rel_bias=rel_bias)
        hp, c_p, s_p, w_p, v_p = layer(hp, p_prompt[i], conv_zero, empty, empty, empty)
        hs, c_s, s_s, w_s, v_s = layer(hs, p_sample[i], state_conv[i], paged_rows(cache_cmp_kv[i]),
                                       paged_rows(cache_slc_kv[i]), state_win_kv[i])
        cmp_p.append(c_p); cmp_s.append(c_s); slc_p.append(s_p); slc_s.append(s_s)
        win_p.append(w_p); win_s.append(w_s); conv_p.append(v_p); conv_s.append(v_s)
    y_prompt = rms_norm(hp, final_norm_g)
    y_sample = rms_norm(hs, final_norm_g)
    return (y_prompt, y_sample, jnp.stack(cmp_p), jnp.stack(cmp_s), jnp.stack(slc_p), jnp.stack(slc_s),
            jnp.stack(win_p), jnp.stack(win_s), jnp.stack(conv_p), jnp.stack(conv_s))
```

```python
import contextlib
import numpy as np
import ml_dtypes
import concourse.bass as bass
import concourse.mybir as mybir
from concourse.bass_utils import run_bass_kernel_spmd

F32 = mybir.dt.float32
BF16 = mybir.dt.bfloat16
I32 = mybir.dt.int32
ALU = mybir.AluOpType
AF = mybir.ActivationFunctionType

ENGS = ("pe", "act", "dve", "pool", "sp")
NSLOT = {"sp": 28, "pool": 16, "act": 4}
BIG = 30000.0


def _dsize(dt):
    return mybir.dt.size(dt)


def _region(ap):
    t = ap.tensor
    space = str(ap.space).upper()
    name = t.name
    if "DRAM" in space or "HBM" in space:
        f0 = ap.offset
        ext = 0
        for st_, cnt in ap.ap:
            ext += (cnt - 1) * abs(st_)
        return ("D", name, 0, 1, f0, f0 + ext + 1)
    shp = list(t.shape)
    pstride = 1
    for s in shp[1:]:
        pstride *= s
    off = ap.offset
    p0 = off // pstride
    f0 = off % pstride
    aps = list(ap.ap)
    pcount = 1
    ext = 0
    for i, (st, cnt) in enumerate(aps):
        if i == 0:
            pcount = cnt if st == pstride or cnt == 1 else cnt
            if st != pstride and cnt > 1 and st != 0:
                pcount = 1
                ext += (cnt - 1) * abs(st)
            continue
        ext += (cnt - 1) * abs(st)
    f1 = f0 + ext + 1
    if "PSUM" in space:
        per_bank = 2048 // _dsize(t.dtype)
        f0 = (f0 // per_bank) * per_bank
        f1 = ((f1 + per_bank - 1) // per_bank) * per_bank
        return ("P", name, 0, 128, f0, f1)
    return ("S", name, p0, p0 + pcount, f0, f1)


def _overlap(a, b):
    return a[1] == b[1] and a[2] < b[3] and b[2] < a[3] and a[4] < b[5] and b[4] < a[5]


def _covers(a, b):
    return a[1] == b[1] and a[2] <= b[2] and a[3] >= b[3] and a[4] <= b[4] and a[5] >= b[5]


class Op:
    __slots__ = ("eng", "fn", "idx", "eidx", "deps", "is_dma", "dma_k", "signal", "waits", "sigval")


class Prog:
    def __init__(self, nc):
        self.nc = nc
        self.ops = []
        self.eng_ops = {e: [] for e in ENGS}
        self.track = {}
        self.dma_count = {e: 0 for e in ENGS}

    def add(self, eng, fn, reads=(), writes=(), is_dma=False):
        op = Op()
        op.eng = eng
        op.fn = fn
        op.idx = len(self.ops)
        op.eidx = len(self.eng_ops[eng])
        op.deps = set()
        op.is_dma = is_dma
        op.dma_k = None
        if is_dma:
            op.dma_k = self.dma_count[eng]
            self.dma_count[eng] += 1
        op.signal = False
        self.ops.append(op)
        self.eng_ops[eng].append(op)
        for a in reads:
            if a is None or isinstance(a, (int, float)):
                continue
            r = _region(a)
            self._access(op, r, r[0] == "P")
        for a in writes:
            if a is None:
                continue
            self._access(op, _region(a), True)
        return op

    def _access(self, op, reg, is_write):
        lst = self.track.get(reg[1])
        if lst is None:
            self.track[reg[1]] = [[reg, op, is_write]]
            return
        new = []
        for ent in lst:
            ereg, eop, ew = ent
            if eop is op:
                new.append(ent)
                continue
            if _overlap(ereg, reg):
                if is_write or ew:
                    op.deps.add(eop)
                    if is_write and _covers(reg, ereg):
                        continue
                    new.append(ent)
                else:
                    if (not eop.is_dma) and (not op.is_dma) and eop.eng == op.eng and ereg == reg:
                        continue
                    new.append(ent)
            else:
                new.append(ent)
        new.append([reg, op, is_write])
        self.track[reg[1]] = new

    def matmul(self, out, lhsT, rhs, start=True, stop=True, **kw):
        return self.add("pe", lambda e: e.matmul(out, lhsT, rhs, start=start, stop=stop, **kw),
                        reads=[lhsT, rhs], writes=[out])

    def transpose(self, out, in_, ident):
        return self.add("pe", lambda e: e.transpose(out, in_, ident), reads=[in_, ident], writes=[out])

    def act(self, out, in_, func, bias=None, scale=None, accum_out=None):
        kw = {}
        rd = [in_]
        if bias is not None:
            kw["bias"] = bias
            rd.append(bias)
        if scale is not None:
            kw["scale"] = scale
            rd.append(scale)
        wr = [out]
        if accum_out is not None:
            kw["accum_out"] = accum_out
            wr.append(accum_out)
        return self.add("act", lambda e: e.activation(out, in_, func, **kw), reads=rd, writes=wr)

    def tt(self, eng, out, in0, in1, op):
        return self.add(eng, lambda e: e.tensor_tensor(out, in0, in1, op), reads=[in0, in1], writes=[out])

    def ts(self, eng, out, in0, s1, s2, op0, op1=None):
        kw = {}
        if op1 is not None:
            kw["op1"] = op1
        return self.add(eng, lambda e: e.tensor_scalar(out, in0, s1, s2, op0, **kw),
                        reads=[in0, s1, s2], writes=[out])

    def stt(self, out, in0, scalar, in1, op0, op1):
        return self.add("dve", lambda e: e.scalar_tensor_tensor(out, in0, scalar, in1, op0, op1),
                        reads=[in0, in1, scalar], writes=[out])

    def copy(self, eng, out, in_):
        if eng == "act":
            return self.add(eng, lambda e: e.copy(out, in_), reads=[in_], writes=[out])
        return self.add(eng, lambda e: e.tensor_copy(out, in_), reads=[in_], writes=[out])

    def memset(self, eng, out, val):
        return self.add(eng, lambda e: e.memset(out, val), reads=[], writes=[out])

    def dma(self, queue, out, in_, **kw):
        return self.add(queue, lambda e: e.dma_start(out=out, in_=in_, **kw),
                        reads=[in_], writes=[out], is_dma=True)

    def emit(self):
        nc = self.nc
        waited = {f: {e: -1 for e in ENGS} for f in ENGS}
        for op in self.ops:
            op.waits = []
        for op in self.ops:
            f = op.eng
            need = {}
            for d in op.deps:
                if d.is_dma:
                    op.waits.append(d)
                    continue
                e = d.eng
                if e == f and not op.is_dma:
                    if e in ("pe", "sp"):
                        continue
                    if op.eidx - d.eidx > 3:
                        continue
                if d.eidx > waited[f][e]:
                    if e not in need or need[e].eidx < d.eidx:
                        need[e] = d
            for e, d in need.items():
                waited[f][e] = d.eidx
                d.signal = True
                op.waits.append(d)
        for e in ENGS:
            c = 0
            for op in self.eng_ops[e]:
                if op.is_dma:
                    continue
                if op.signal:
                    c += 1
                    op.sigval = c
        dma_waited = {f: set() for f in ENGS}
        with contextlib.ExitStack() as st:
            sems = {e: st.enter_context(nc.semaphore("sem_" + e)) for e in ENGS}
            dsems = {}
            for q, n in NSLOT.items():
                if self.dma_count[q] > 0:
                    dsems[q] = [st.enter_context(nc.semaphore("dsem_%s_%d" % (q, i))) for i in range(n)]
            block = st.enter_context(nc.Block())

            def dma_slot(d):
                n = NSLOT[d.eng]
                return dsems[d.eng][d.dma_k % n], 16 * (d.dma_k // n + 1)

            def run_engine(ename, eng):
                final_wait = {}
                for op in self.eng_ops[ename]:
                    for d in op.waits:
                        if d.is_dma:
                            if d in dma_waited[ename]:
                                continue
                            dma_waited[ename].add(d)
                            s, v = dma_slot(d)
                            eng.wait_ge(s, v)
                        else:
                            eng.wait_ge(sems[d.eng], d.sigval)
                    if op.is_dma:
                        n = NSLOT[ename]
                        if op.dma_k >= n:
                            eng.wait_ge(dsems[ename][op.dma_k % n], 16 * (op.dma_k // n))
                        ins = op.fn(eng)
                        s, v = dma_slot(op)
                        ins.then_inc(s, 16)
                        final_wait[op.dma_k % n] = (s, v)
                    else:
                        ins = op.fn(eng)
                        if op.signal:
                            ins.then_inc(sems[ename], 1)
                for k, (s, v) in final_wait.items():
                    eng.wait_ge(s, v)

            @block.sync
            def _(eng):
                run_engine("sp", eng)

            @block.tensor
            def _(eng):
                run_engine("pe", eng)

            @block.scalar
            def _(eng):
                run_engine("act", eng)

            @block.vector
            def _(eng):
                run_engine("dve", eng)

            @block.gpsimd
            def _(eng):
                run_engine("pool", eng)


T = 2176
TP = 2048
NT = 17
D = 1024
DIN = 3352
EPS = 1e-6
NB5 = [(0, 512), (512, 512), (1024, 512), (1536, 512), (2048, 128)]


class K:
    pass


def build(nc, npool, nlayers=2, stage=99):
    k = K()
    k.nc = nc
    P = Prog(nc)
    k.P = P
    st = contextlib.ExitStack()
    k.st = st

    def din(name, shape, dt=F32):
        return nc.dram_tensor(name, list(shape), dt, kind="ExternalInput").ap()

    def dout(name, shape, dt=F32):
        return nc.dram_tensor(name, list(shape), dt, kind="ExternalOutput").ap()

    def dscr(name, shape, dt=F32):
        return nc.dram_tensor(name, list(shape), dt, kind="Internal").ap()

    def sb(name, shape, dt):
        return st.enter_context(nc.sbuf_tensor(name, list(shape), dt))

    def ps(name, shape, dt=F32):
        return st.enter_context(nc.psum_tensor(name, list(shape), dt))

    x_tok = din("x_tok", [T, D])
    ple_tok = din("ple_tok", [2, T, 256])
    if stage >= 6:
        cc = din("cc", [2, npool * 128, 256])
        cs = din("cs", [2, npool * 128, 256])
    ptab = din("ptab", [1, 256], I32)
    swin = din("swin", [2, 16, 512, 256])
    sconv = din("sconv", [2, 480, 512])
    norm_g = din("norm_g", [2, D])
    w_in = din("w_in", [2, D, DIN])
    convp = din("convp", [2, 34, 512])
    cmp_pe = din("cmp_pe", [2, 2, 32, 64])
    cmp_w1 = din("cmp_w1", [2, 2, 2048, 128])
    cmp_w2 = din("cmp_w2", [2, 2, 128, 64])
    w_out = din("w_out", [2, D, D])
    w_ple = din("w_ple", [2, 256, D])
    w_pg = din("w_pg", [2, D, D])
    rel_bias = din("rel_bias", [32, 8])
    fng = din("fng", [1, D])
    c_ident = din("c_ident", [128, 128])
    c_oh = din("c_oh", [33, 383])
    c_expand = din("c_expand", [33, T], BF16)
    c_cover = din("c_cover", [127, 34], BF16)
    c_zpad = din("c_zpad", [16, 270], BF16)
    c_m1p = din("c_m1p", [128, 16 * 33])
    c_c1p = din("c_c1p", [128, 16 * 33])
    c_m1s = din("c_m1s", [8, 33])
    c_c1s = din("c_c1s", [8, 33])
    c_iota = din("c_iota", [128, 1])
    c_ac = din("c_ac", [128, 128], BF16)

    y = dout("y", [T, D])
    ncmp = dout("ncmp", [2, T, 256])
    nslc = dout("nslc", [2, T, 256])
    nwin_p = dout("nwin_p", [2, 512, 256])
    nwin_s = dout("nwin_s", [2, 16, 512, 256])
    nconv_p = dout("nconv_p", [2, 30, 512])
    nconv_s = dout("nconv_s", [2, 16, 30, 512])
    h1 = dscr("h1", [T, D])
    G1 = dscr("G1", [8, 128, 512])
    G16 = dscr("G16", [8, 16, 640])

    ident_f = sb("ident_f", [128, 128], F32)
    ident_b = sb("ident_b", [128, 128], BF16)
    ones_f = sb("ones_f", [128, 128], F32)
    epsb = sb("epsb", [128, 1], F32)
    uT = sb("uT", [128, 8, T], BF16)
    WA = sb("WA", [128, 8, 1024], BF16)
    WB = sb("WB", [128, 8, 1024], BF16)
    bigG = sb("bigG", [128, 4, T], BF16)
    bigZ = sb("bigZ", [128, 4 * T], BF16)
    mixc = sb("mixc", [128, 4, T], BF16)
    KT = sb("KT", [128, 3, T], BF16)
    Vt = sb("Vt", [128, 3, 17, 130], BF16)
    gs = sb("gs", [128, 17, 24], F32)
    aF = sb("aF", [128, 7168], F32)
    aB = sb("aB", [128, 8192], BF16)
    ssq = sb("ssq", [128, NT], F32)
    rs = sb("rs", [128, NT], F32)
    cw = sb("cw", [128, 4, 34], F32)
    gluP = bigG
    gluS = KT[:].rearrange("p a t -> p (a t)")[:, 0:2432].rearrange("p (c b r) -> p c b r", c=4, b=16)
    WC = bigG[:, 0, 0:2048].rearrange("p (c n) -> p c n", c=2)
    VcT = aB[:, 0:2048]
    cwraw = aF[0:34, 6656:7168]
    zc = bigZ[:].rearrange("p (c t) -> p c t", c=4)
    sza = bigZ[:].rearrange("p (t e) -> p t e", e=512)
    qT = bigG
    xin = [aF[:, 0:1024], aF[:, 1024:2048]]
    gbc = aF[:, 2048:3072]
    sg = [aF[:, 3072:3584], aF[:, 3584:4096]]
    glu_tail = aF[:, 4096:5120].rearrange("p (c n) -> p c n", c=4)
    stage_t = [aF[:, 5120:5888], aF[:, 5888:6656]]
    dw = aF[:, 0:2048].rearrange("p (c n) -> p c n", c=4)
    sq = aF[:, 2048:2560]
    st_m = aF[:, 2560:3072]
    st_v = aF[:, 3072:3584]
    st_t = aF[:, 3584:4096]
    scraw = aF[:, 5120:5632]
    tailtok = aF[:, 5632:6656].rearrange("p (c n) -> p c n", c=2)
    Dk = aB[:, 0:3968].rearrange("p (k n) -> p k n", k=31)
    scb = aB[:, 3968:6016].rearrange("p (c n) -> p c n", c=4)
    ub = aB[:, 6016:7040]
    junk = aB[:, 7040:8064]

    psA = ps("psA", [128, 512])
    psB = ps("psB", [128, 512])
    psC = ps("psC", [128, 512])
    psT = ps("psT", [128, 1024], BF16)
    psD = ps("psD", [128, 512])
    psE = ps("psE", [128, 512])
    psF = ps("psF", [128, 512])
    psG = ps("psG", [128, 512])
    rot3 = [psA, psB, psC]

    P.dma("sp", ident_f[:], c_ident)
    P.copy("dve", ident_b[:], ident_f[:])
    P.memset("dve", ones_f[:], 1.0)
    P.memset("dve", epsb[:], EPS)
    P.memset("pool", Vt[:], 1.0)

    wrow = lambda w, l: w[l].rearrange("(c p) n -> p c n", p=128)
    cnt = [0]

    def rr():
        cnt[0] += 1
        return rot3[cnt[0] % 3]

    def alt():
        cnt[0] += 1
        return "act" if cnt[0] % 2 else "dve"

    def recip(out, in_):
        return P.add("dve", lambda e: e.reciprocal(out, in_), reads=[in_], writes=[out])

    def phase1(l, src):
        P.dma("sp", gbc[:], norm_g[l:l + 1, :].broadcast_to([128, D]))
        for t in range(NT):
            xt = xin[t % 2]
            P.dma("sp", xt, src[t * 128:(t + 1) * 128, :])
            P.act(junk[:], xt[:], AF.Square, accum_out=ssq[:, t:t + 1])
            P.act(rs[:, t:t + 1], ssq[:, t:t + 1], AF.Sqrt, bias=epsb[:, 0:1], scale=1.0 / D)
            recip(rs[:, t:t + 1], rs[:, t:t + 1])
            P.stt(ub[:], xt[:], rs[:, t:t + 1], gbc[:], ALU.mult, ALU.mult)
            for c in range(8):
                P.transpose(psT[:, c * 128:(c + 1) * 128], ub[:, c * 128:(c + 1) * 128], ident_b[:])
            P.copy(alt(), uT[:, :, t * 128:(t + 1) * 128], psT[:].rearrange("p (c n) -> p c n", c=8))

    def fm_mm(pst, lhs_of_c, nb, m=128):
        n0, nn = NB5[nb]
        for c in range(8):
            P.matmul(pst[0:m, 0:nn], lhs_of_c(c), uT[:, c, n0:n0 + nn], start=(c == 0), stop=(c == 7))

    def phase2a(l):
        wl = wrow(w_in, l)
        P.memset("pool", gluP[:, :, 0:30], 0.0)
        P.dma("pool", WB[:, :, 0:512], wl[:, :, 0:512])
        P.dma("pool", WB[:, :, 512:1024], wl[:, :, 512:1024])
        for ch in range(4):
            for nb in range(5):
                n0, nn = NB5[nb]
                fm_mm(psA, lambda c: WB[:, c, ch * 128:(ch + 1) * 128], nb)
                fm_mm(psB, lambda c: WB[:, c, 512 + ch * 128:512 + (ch + 1) * 128], nb)
                s = sg[nb % 2]
                P.act(s[:, 0:nn], psB[:, 0:nn], AF.Sigmoid)
                if nb < 4:
                    P.tt("dve", gluP[:, ch, 30 + n0:30 + n0 + nn], psA[:, 0:nn], s[:, 0:nn], ALU.mult)
                    if nb == 3:
                        P.tt("dve", glu_tail[:, ch, 0:128], psA[:, 384:512], s[:, 384:512], ALU.mult)
                else:
                    P.tt("dve", gluS[:, ch, :, 30:38], psA[:, 0:128].rearrange("p (b q) -> p b q", q=8),
                         s[:, 0:128].rearrange("p (b q) -> p b q", q=8), ALU.mult)
                    P.tt("dve", glu_tail[:, ch, 128:256], psA[:, 0:128], s[:, 0:128], ALU.mult)
        P.dma("pool", WB[:, :, 0:512], wl[:, :, 1024:1536])
        for ch in range(4):
            for nb in range(5):
                n0, nn = NB5[nb]
                pst = rr()
                fm_mm(pst, lambda c: WB[:, c, ch * 128:(ch + 1) * 128], nb)
                P.act(zc[:, ch, n0:n0 + nn], pst[:, 0:nn], AF.Silu)

    def tm_group(l, half, ncols, evac):
        for t in range(17):
            sl = slice(t * 128, (t + 1) * 128)
            pst = rr()
            for c in range(8):
                P.matmul(pst[:, 0:ncols], uT[:, c, sl], WB[:, c, half * 512:half * 512 + ncols],
                         start=(c == 0), stop=(c == 7))
            evac(t, sl, pst)

    def phase2b(l):
        wl = wrow(w_in, l)
        P.memset("pool", Vs16[:, :, 128:132], 1.0)
        for two in range(2):
            for c in range(8):
                P.dma("pool", WB[:, c, 512:1024].rearrange("p (j two d) -> p j two d", two=2, d=64)[:, :, two, :],
                      wl[:, c, 1536 + two * 256:1536 + two * 256 + 256].rearrange("p (j d) -> p j d", d=64))
        for j in range(4):
            for nb in range(5):
                n0, nn = NB5[nb]
                pst = rr()
                fm_mm(pst, lambda c: WB[:, c, 512 + j * 128:512 + (j + 1) * 128], nb)
                P.act(qT[:, j, n0:n0 + nn], pst[:, 0:nn], AF.Copy, scale=0.125)
        outs = [ncmp, nslc, None]
        for br in range(3):
            half = br % 2
            c0 = 2048 + br * 256
            P.dma("pool", WB[:, :, half * 512:half * 512 + 256], wl[:, :, c0:c0 + 256])
            if br == 0:
                P.dma("pool", WB[:, :, 256:280], wl[:, :, 3328:3352])
            for nb in range(5):
                n0, nn = NB5[nb]
                pst = rr()
                fm_mm(pst, lambda c: WB[:, c, half * 512:half * 512 + 128], nb)
                P.copy(alt(), KT[:, br, n0:n0 + nn], pst[:, 0:nn])
            if br == 0:
                for nb in range(4):
                    n0, nn = NB5[nb]
                    pst = rr()
                    fm_mm(pst, lambda c: WB[:, c, 128:256], nb)
                    P.copy(alt(), VcT[:, n0:n0 + nn], pst[:, 0:nn])

            def evac(t, sl, pst, br=br):
                stg = stage_t[t % 2]
                P.copy("act", stg[:, 0:256], pst[:, 0:256])
                P.copy("pool", Vt[:, br, t, :].rearrange("p (g e) -> p g e", g=2)[:, :, 0:64],
                       stg[:, 128:256].rearrange("p (g d) -> p g d", g=2))
                if t == 16:
                    P.copy("pool", Vs16[:, br, 0:128], stg[:, 128:256])
                if br == 0:
                    P.act(gs[:, t, :], pst[:, 256:280], AF.Sigmoid)
                if br < 2:
                    P.dma("sp", outs[br][l, sl, :], stg[:, 0:256])
                else:
                    if 12 <= t < 16:
                        P.dma("sp", nwin_p[l, (t - 12) * 128:(t - 11) * 128, :], stg[:, 0:256])
                    if t == 16:
                        for b in range(16):
                            P.dma("sp", nwin_s[l, b, 504:512, :], stg[b * 8:(b + 1) * 8, 0:256])
            tm_group(l, half, 280 if br == 0 else 256, evac)
        if stage != 41:
            P.dma("sp", nwin_s[l][:, 0:504, :], swin[l][:, 8:512, :])
        P.dma("pool", WB[:, :, 512:1024], wl[:, :, 2816:3328])
        tm_group(l, 1, 512, lambda t, sl, pst: P.act(sza[:, t, :], pst[:, 0:512], AF.Silu))

    ones_b = sb("ones_b", [128, 128], BF16)
    P.memset("pool", ones_b[:], 1.0)

    def phase3(l):
        for ch in range(4):
            for hlf in range(2):
                P.transpose(psD[:, (hlf * 4 + ch) % 4 * 128:((hlf * 4 + ch) % 4 + 1) * 128],
                            glu_tail[:, ch, hlf * 128:(hlf + 1) * 128], ident_f[:])
                P.copy("dve", tailtok[:, hlf, ch * 128:(ch + 1) * 128],
                       psD[:, (hlf * 4 + ch) % 4 * 128:((hlf * 4 + ch) % 4 + 1) * 128])
        P.dma("sp", nconv_p[l], tailtok[98:128, 0, :])
        for b in range(16):
            P.dma("sp", nconv_s[l, b, 22:30, :], tailtok[b * 8:(b + 1) * 8, 1, :])
        P.dma("sp", nconv_s[l][:, 0:22, :], sconv[l].rearrange("(b r) c -> b r c", r=30)[:, 8:30, :])
        P.dma("sp", cwraw, convp[l])
        for ch in range(4):
            P.transpose(psD[:, 0:34], cwraw[0:34, ch * 128:(ch + 1) * 128], ident_f[0:34, 0:34])
            P.copy("dve", cw[:, ch, :], psD[:, 0:34])
        for r in range(4):
            nr = 128 if r < 3 else 96
            P.dma("sp", scraw[0:nr, :], sconv[l, r * 128:r * 128 + nr, :])
            P.copy("pool", scb[0:nr, r, :], scraw[0:nr, :])
        for ch in range(4):
            for r in range(4):
                nr = 128 if r < 3 else 96
                P.transpose(psT[:, r * 128:r * 128 + nr], scb[0:nr, r, ch * 128:(ch + 1) * 128], ident_b[0:nr, 0:nr])
            P.copy("dve", gluS[:, ch, :, 0:30], psT[:, 0:480].rearrange("p (b r) -> p b r", r=30))
        for ch in range(4):
            for kk in range(31):
                P.ts("dve" if kk % 2 else "pool", Dk[:, kk, :], ident_f[:], cw[:, ch, kk:kk + 1], None, ALU.mult)
            for nb in range(5):
                n0, nn = NB5[nb]
                pst = rr()
                for kk in range(31):
                    if nb < 4:
                        rhs = gluP[:, ch, n0 + kk:n0 + kk + 512]
                    else:
                        rhs = gluS[:, ch, :, kk:kk + 8]
                    P.matmul(pst[:, 0:nn], Dk[:, kk, :], rhs, start=(kk == 0), stop=(kk == 30))
                P.act(mixc[:, ch, n0:n0 + nn], pst[:, 0:nn], AF.Identity, bias=cw[:, ch, 31:32])
        for nb in range(5):
            n0, nn = NB5[nb]
            for ch in range(4):
                P.act(sq[:, 0:nn], mixc[:, ch, n0:n0 + nn], AF.Square)
                P.matmul(psD[:, 0:nn], ones_b[:], mixc[:, ch, n0:n0 + nn], start=(ch == 0), stop=(ch == 3))
                P.matmul(psE[:, 0:nn], ones_f[:], sq[:, 0:nn], start=(ch == 0), stop=(ch == 3))
            P.ts("dve", st_m[:, 0:nn], psD[:, 0:nn], 1.0 / 512, None, ALU.mult)
            P.tt("dve", st_v[:, 0:nn], st_m[:, 0:nn], st_m[:, 0:nn], ALU.mult)
            P.stt(st_v[:, 0:nn], psE[:, 0:nn], 1.0 / 512, st_v[:, 0:nn], ALU.mult, ALU.subtract)
            P.act(st_t[:, 0:nn], st_v[:, 0:nn], AF.Sqrt, bias=epsb[:, 0:1])
            recip(st_t[:, 0:nn], st_t[:, 0:nn])
            for ch in range(4):
                P.tt("dve", sq[:, 0:nn], mixc[:, ch, n0:n0 + nn], st_m[:, 0:nn], ALU.subtract)
                P.tt("dve", sq[:, 0:nn], sq[:, 0:nn], st_t[:, 0:nn], ALU.mult)
                P.act(sq[:, 0:nn], sq[:, 0:nn], AF.Silu, bias=cw[:, ch, 33:34], scale=cw[:, ch, 32:33])
                P.tt("dve", mixc[:, ch, n0:n0 + nn], sq[:, 0:nn], zc[:, ch, n0:n0 + nn], ALU.mult)

    mixa = uT

    def phase5(l, src, last):
        P.dma("pool", WA[:], wrow(w_out, l))
        P.dma("pool", WB[:], wrow(w_pg, l))
        P.dma("pool", WC, wrow(w_ple, l))
        if last:
            P.dma("sp", gbc, fng[0:1, :].broadcast_to([128, D]))
        hm = aF[:, 3072:4096]
        gt = aF[:, 4096:5120]
        plt = aF[:, 5120:5376]
        plb = aB[:, 0:256]
        plT = aB[:, 256:512].rearrange("p (c n) -> p c n", c=2)
        hmT = aB[:, 512:1536].rearrange("p (c n) -> p c n", c=8)
        for t in range(NT):
            sl = slice(t * 128, (t + 1) * 128)
            xt = xin[t % 2]
            P.dma("sp", xt, src[sl, :])
            P.dma("sp", plt, ple_tok[l, sl, :])
            for hf in range(2):
                pst = [psA, psB][hf]
                for e in range(8):
                    lhs = mixc[:, e, sl] if e < 4 else mixa[:, e - 4, sl]
                    P.matmul(pst[:, 0:512], lhs, WA[:, e, hf * 512:(hf + 1) * 512], start=(e == 0), stop=(e == 7))
                P.tt("dve", hm[:, hf * 512:(hf + 1) * 512], pst[:, 0:512], xt[:, hf * 512:(hf + 1) * 512], ALU.add)
            P.copy("pool", ub, hm)
            P.copy("pool", plb, plt)
            for c in range(8):
                P.transpose(psT[:, c * 128:(c + 1) * 128], ub[:, c * 128:(c + 1) * 128], ident_b[:])
            P.copy("act", hmT, psT[:].rearrange("p (c n) -> p c n", c=8))
            for c in range(2):
                P.transpose(psT[:, c * 128:(c + 1) * 128], plb[:, c * 128:(c + 1) * 128], ident_b[:])
            P.copy("act", plT, psT[:, 0:256].rearrange("p (c n) -> p c n", c=2))
            for hf in range(2):
                pg = [psC, psD][hf]
                pp = [psE, psF][hf]
                for c in range(8):
                    P.matmul(pg[:, 0:512], hmT[:, c, :], WB[:, c, hf * 512:(hf + 1) * 512], start=(c == 0), stop=(c == 7))
                for c in range(2):
                    P.matmul(pp[:, 0:512], plT[:, c, :], WC[:, c, hf * 512:(hf + 1) * 512], start=(c == 0), stop=(c == 1))
                P.act(gt[:, hf * 512:(hf + 1) * 512], pg[:, 0:512], AF.Sigmoid)
                P.tt("dve", gt[:, hf * 512:(hf + 1) * 512], gt[:, hf * 512:(hf + 1) * 512], pp[:, 0:512], ALU.mult)
            P.tt("dve", hm, hm, gt, ALU.add)
            if not last:
                P.dma("sp", h1[sl, :], hm)
            else:
                P.act(junk, hm, AF.Square, accum_out=ssq[:, t:t + 1])
                P.act(rs[:, t:t + 1], ssq[:, t:t + 1], AF.Sqrt, bias=epsb[:, 0:1], scale=1.0 / D)
                recip(rs[:, t:t + 1], rs[:, t:t + 1])
                P.stt(gt, hm, rs[:, t:t + 1], gbc, ALU.mult, ALU.mult)
                P.dma("sp", y[sl, :], gt)

    strip = sb("strip", [128, 8, 256], BF16)
    band = sb("band", [128, 8, 128], BF16)
    bands = sb("bands", [128, 8, 8], BF16)
    expand = sb("expand", [128, T], BF16)
    cover = sb("cover", [127, 34], BF16)
    zpad = sb("zpad", [128, 270], BF16)
    m1c1 = aF[:, 5704:5836].rearrange("p (a b) -> p a b", a=2)
    m1s = sb("m1s", [8, 33], F32)
    c1s = sb("c1s", [8, 33], F32)
    acm = sb("acm", [128, 128], BF16)
    tb = sb("tb", [33, 8], F32)
    r31 = sb("r31", [32, 8], F32)
    ohs = aF[0:33, 6000:6383]
    tbh = aF[0:33, 6400:6528]
    w2 = sb("w2", [128, 2, 64], BF16)
    cvec = sb("cvec", [128, 2], F32)
    peraw = aF[0:32, 5576:5704].rearrange("p (kv d) -> p kv d", kv=2)
    peT = sb("peT", [64, 2, 32], BF16)
    Vs16 = aB[:, 7424:7820].rearrange("p (a b) -> p a b", a=3)
    idx = sb("idx", [128, 256], I32)
    idxf = aF[:, 6400:6656]
    iot = sb("iot", [128, 1], F32)
    smallf = aF[:, 5836:5964]

    def AP(t, off, pat):
        return bass.AP(t.tensor, off, pat)

    def setup_attn():
        P.memset("pool", expand[32:64, :], 0.0)
        P.memset("pool", expand[64:128, :], 0.0)
        P.memset("pool", zpad[:], 0.0)
        P.memset("pool", band[:], 0.0)
        P.memset("pool", bands[:], 0.0)
        P.dma("sp", expand[0:33, :], c_expand)
        P.dma("sp", cover[:], c_cover)
        P.dma("sp", zpad[0:16, :], c_zpad)
        P.dma("sp", m1s[:], c_m1s)
        P.dma("sp", c1s[:], c_c1s)
        P.dma("sp", acm[:], c_ac)
        P.dma("sp", ohs, c_oh)
        P.dma("sp", iot[:], c_iota)
        P.dma("sp", tb[0:32, :], rel_bias)
        P.dma("sp", r31[:], rel_bias[31:32, :].broadcast_to([32, 8]))
        P.tt("dve", tb[0:32, :], tb[0:32, :], r31[:], ALU.subtract)
        P.memset("dve", tb[32:33, :], -BIG)
        gsb = aF[:, 3000:3383]
        for h in range(8):
            P.ts("dve", tbh, ones_f[0:33, :], tb[:, h:h + 1], None, ALU.mult)
            P.matmul(psD[:, 0:383], tbh, ohs, start=True, stop=True)
            P.copy("act", gsb, psD[:, 0:383])
            P.dma("sp", AP(G1, h * 128 * 512, [[513, 128], [1, 383]]), gsb)
            P.dma("sp", AP(G16, h * 16 * 640, [[656, 16], [1, 383]]), gsb[0:16, :])
        strip_f = aF[:, 3400:5448].rearrange("p (h x) -> p h x", h=8)
        band_f = aF[0:16, 5448:6472].rearrange("p (h x) -> p h x", h=8)
        bands_f = aF[0:16, 6472:6536].rearrange("p (h x) -> p h x", h=8)
        P.dma("sp", strip_f, AP(G1, 127, [[512, 128], [65536, 8], [1, 256]]))
        P.dma("sp", band_f, AP(G16, 240, [[640, 16], [10240, 8], [1, 128]]))
        P.dma("sp", bands_f, AP(G16, 352, [[640, 16], [10240, 8], [1, 8]]))
        P.copy("dve", strip[:], strip_f)
        P.copy("dve", band[0:16], band_f)
        P.copy("dve", bands[0:16], bands_f)
        P.dma("sp", idx[:], ptab[0:1, :].broadcast_to([128, 256]))
        P.copy("dve", idxf, idx[:])
        P.ts("dve", idxf, idxf, 128.0, iot[:, 0:1], ALU.mult, ALU.add)
        P.copy("dve", idx[:], idxf)

    w1 = WA[:].rearrange("p c n -> p (c n)").rearrange("p (kv pos h) -> p kv pos h", kv=2, pos=32)
    kcT = aB[:, 6400:6528]
    vcaug = aB[0:127, 6528:6724].rearrange("p (g e) -> p g e", g=2)
    hid = aB[:, 6724:6852]

    def load_cmp_weights(l):
        for kv in range(2):
            for cp in range(2):
                P.dma("pool", w1[cp * 64:(cp + 1) * 64, kv, :, :],
                      cmp_w1[l, kv].rearrange("(pos d) h -> d pos h", d=64))
        P.dma("pool", w2[:], cmp_w2[l].rearrange("kv h d -> h kv d"))
        P.dma("sp", peraw, cmp_pe[l].rearrange("kv pos d -> pos kv d"))
        for kv in range(2):
            P.transpose(psD[0:64, kv * 32:(kv + 1) * 32], peraw[0:32, kv, :], ident_f[0:32, 0:32])
        P.copy("dve", peT[:], psD[0:64, 0:64].rearrange("p (kv pos) -> p kv pos", kv=2))
        for kv in range(2):
            for pos in range(32):
                P.matmul(psD[:, 64 + kv:65 + kv], w1[0:64, kv, pos, :], peT[:, kv, pos:pos + 1],
                         start=(pos == 0), stop=(pos == 31))
        P.copy("dve", cvec[:], psD[:, 64:66])

    def compress(Ksrc, Vsrc):
        for g in range(2):
            P.copy("pool", vcaug[:, g, 64:98], cover[:, :])
        for kv in range(2):
            src = Ksrc if kv == 0 else Vsrc
            for g in range(2):
                gb = g * 64
                pst = rr()
                for pos in range(32):
                    P.matmul(pst[:, 0:127], w1[gb:gb + 64, kv, pos, :], src[gb:gb + 64, pos:pos + 16 * 126 + 1:16],
                             start=(pos == 0), stop=(pos == 31))
                P.act(hid[:, 0:127], pst[:, 0:127], AF.Silu, bias=cvec[:, kv:kv + 1])
                if kv == 0:
                    P.matmul(psE[gb:gb + 64, 0:127], w2[:, 0, :], hid[:, 0:127], start=True, stop=True)
                    P.copy("dve", kcT[gb:gb + 64, 0:127], psE[gb:gb + 64, 0:127])
                else:
                    P.matmul(psE[0:127, 128 + g * 64:192 + g * 64], hid[:, 0:127], w2[:, 1, :], start=True, stop=True)
                    P.copy("dve", vcaug[:, g, 0:64], psE[0:127, 128 + g * 64:192 + g * 64])

    es = [aB[:, i * 512:(i + 1) * 512] for i in range(3)]
    ecmp = aB[:, 1536:2048]
    negT = aB[:, 2048:6400].rearrange("p (g t) -> p g t", g=2)
    negb = aB[:, 6852:6885]
    attnb = aB[:, 6912:7424]
    accb = aF[:, 0:2048].rearrange("p (q e) -> p q e", q=4)
    cmpo = aF[:, 2048:2440].rearrange("p (r e) -> p r e", r=4)
    tmpo = aF[:, 2440:2960].rearrange("p (q e) -> p q e", q=4)
    rinv = smallf[:, 0:4]
    coef = smallf[:, 4:8]
    imp = smallf[:, 8:41]
    m8 = smallf[:, 48:56]
    ecnt = [0]
    ecmp_ref = [ecmp]

    def cmp_branch(rows, nc_, g, qcols, band_lhsT, band_rhs_of_h, gate_ap, acc_of_h, m1, c1, negT_out):
        gb = g * 64
        for r in range(4):
            h = g * 4 + r
            P.matmul(psG[0:nc_, r * rows:(r + 1) * rows], kcT[gb:gb + 64, 0:nc_], qcols(r), start=True, stop=False)
            P.matmul(psG[0:nc_, r * rows:(r + 1) * rows], band_lhsT, band_rhs_of_h(h), start=False, stop=True)
        ecm = ecmp_ref[0]
        P.act(ecm[0:nc_, 0:4 * rows], psG[0:nc_, 0:4 * rows], AF.Exp)
        for r in range(4):
            P.matmul(psF[0:rows, r * 98:(r + 1) * 98], ecm[0:nc_, r * rows:(r + 1) * rows], vcaug[0:nc_, g, :],
                     start=True, stop=True)
        P.copy("act", cmpo[0:rows], psF[0:rows, 0:392].rearrange("p (r e) -> p r e", r=4))
        P.ts("dve", rinv[0:rows], cmpo[0:rows, :, 97], 1e-30, None, ALU.add)
        recip(rinv[0:rows], rinv[0:rows])
        P.ts("dve", imp[0:rows], cmpo[0:rows, 0, 64:97], rinv[0:rows, 0:1], None, ALU.mult)
        for r in range(1, 4):
            P.stt(imp[0:rows], cmpo[0:rows, r, 64:97], rinv[0:rows, r:r + 1], imp[0:rows], ALU.mult, ALU.add)
        P.tt("dve", coef[0:rows], rinv[0:rows], gate_ap, ALU.mult)
        for r in range(4):
            P.ts("dve", acc_of_h(g * 4 + r), cmpo[0:rows, r, 0:64], coef[0:rows, r:r + 1], None, ALU.mult)
        P.tt("dve", imp[0:rows], imp[0:rows], m1, ALU.mult)
        P.tt("dve", imp[0:rows], imp[0:rows], c1, ALU.add)
        P.add("dve", lambda e: e.max(out=m8[0:rows], in_=imp[0:rows]), reads=[imp[0:rows]], writes=[m8[0:rows]])
        P.ts("dve", negb[0:rows], imp[0:rows], m8[0:rows, 7:8], -BIG, ALU.is_lt, ALU.mult)
        P.transpose(psT[0:33, 0:rows], negb[0:rows], ident_b[0:rows, 0:rows])
        P.copy("dve", negT_out[0:33], psT[0:33, 0:rows])

    def finish_head(rows, nq, pso, width, ocol, scol, gate_of_q, acc_of_q):
        P.copy("act", tmpo[0:rows, 0:nq, 0:width], pso[0:rows, 0:nq * width].rearrange("p (q e) -> p q e", q=nq))
        P.ts("dve", rinv[0:rows, 0:nq], tmpo[0:rows, 0:nq, scol], 1e-30, None, ALU.add)
        recip(rinv[0:rows, 0:nq], rinv[0:rows, 0:nq])
        P.tt("dve", coef[0:rows, 0:nq], rinv[0:rows, 0:nq], gate_of_q, ALU.mult)
        for qi in range(nq):
            a = acc_of_q(qi)
            P.stt(a, tmpo[0:rows, qi, ocol:ocol + 64], coef[0:rows, qi:qi + 1], a, ALU.mult, ALU.add)

    def attention_prompt(l):
        compress(KT[:, 0, 0:TP], VcT[:, 0:TP])
        P.memset("pool", negT[32:64], 0.0)
        P.memset("pool", negT[64:128], 0.0)
        pso2 = [psD, psE]
        for qb in range(4):
            q0 = qb * 512
            for qi in range(4):
                qt = qb * 4 + qi
                qs = slice(qt * 128, (qt + 1) * 128)
                nc_ = min(127, 8 * qt + 7)
                c_off = 8 * qt - 9
                mc = m1c1[:, qt % 2, :]
                P.dma("sp", mc[:, 0:33], c_m1p[:, qt * 33:(qt + 1) * 33])
                P.dma("sp", mc[:, 33:66], c_c1p[:, qt * 33:(qt + 1) * 33])
                for g in range(2):
                    cmp_branch(128, nc_, g,
                               lambda r, g=g, qs=qs: qT[g * 64:(g + 1) * 64, r, qs],
                               zpad[:, 127 - c_off:127 - c_off + nc_],
                               lambda h: band[:, h, :],
                               gs[:, qt, g * 4:g * 4 + 4],
                               lambda h, qi=qi: accb[:, qi, h * 64:(h + 1) * 64],
                               mc[:, 0:33], mc[:, 33:66],
                               negT[:, g, qs])
            for h in range(8):
                g, r = h // 4, h % 4
                gb = g * 64
                for br in (1, 2):
                    pso = pso2[br - 1]
                    P.memset("dve", pso[:, 0:260], 0.0)
                    kt_lo = 0 if br == 1 else max(0, 4 * qb - 4)
                    for kt in range(kt_lo, 4 * qb + 4):
                        qi_lo = max(0, kt - 4 * qb)
                        qi_hi = 3 if br == 1 else min(3, kt + 4 - 4 * qb)
                        c_lo, c_hi = qi_lo * 128, (qi_hi + 1) * 128
                        pss = rr()
                        ks = slice(kt * 128, (kt + 1) * 128)
                        extra = []
                        if br == 1:
                            extra.append((c_lo, c_hi, expand[:, ks], negT[:, g, q0 + c_lo:q0 + c_hi]))
                        d0 = kt - 4 * qb
                        if 0 <= d0 <= 2:
                            extra.append((d0 * 128, d0 * 128 + 256, ident_b[:], strip[:, h, 0:256]))
                        elif d0 == 3:
                            extra.append((384, 512, ident_b[:], strip[:, h, 0:128]))
                        elif d0 == -1:
                            extra.append((0, 128, ident_b[:], strip[:, h, 128:256]))
                        if br == 2 and 0 <= d0 + 4 <= 3:
                            extra.append(((d0 + 4) * 128, (d0 + 5) * 128, ident_b[:], acm[:]))
                        P.matmul(pss[:, c_lo:c_hi], KT[gb:gb + 64, br, ks], qT[gb:gb + 64, r, q0 + c_lo:q0 + c_hi],
                                 start=True, stop=(len(extra) == 0))
                        for i, (a, b_, lt, rh) in enumerate(extra):
                            P.matmul(pss[:, a:b_], lt, rh, start=False, stop=(i == len(extra) - 1))
                        ecnt[0] += 1
                        e = es[ecnt[0] % 3]
                        P.act(e[:, c_lo:c_hi], pss[:, c_lo:c_hi], AF.Exp)
                        for qi in range(qi_lo, qi_hi + 1):
                            P.matmul(pso[:, qi * 65:(qi + 1) * 65], e[:, qi * 128:(qi + 1) * 128],
                                     Vt[:, br, kt, g * 65:(g + 1) * 65], start=False, stop=False, skip_group_check=True)
                    finish_head(128, 4, pso, 65, 0, 64,
                                gs[:, qb * 4:qb * 4 + 4, br * 8 + h],
                                lambda qi, h=h: accb[:, qi, h * 64:(h + 1) * 64])
            for qi in range(4):
                qt = qb * 4 + qi
                P.tt("dve", attnb, accb[:, qi, :], sza[:, qt, :], ALU.mult)
                for c in range(4):
                    P.transpose(psT[:, c * 128:(c + 1) * 128], attnb[:, c * 128:(c + 1) * 128], ident_b[:])
                P.copy("act", mixa[:, 0:4, qt * 128:(qt + 1) * 128], psT[:, 0:512].rearrange("p (c n) -> p c n", c=4))

    def attention_sample(l):
        WBf = WB[:].rearrange("p c n -> p (c n)")
        cmpc = aB[:, 0:4160].rearrange("p (j c) -> p j c", j=16)
        slcc = WBf[:, 0:4160].rearrange("p (j c) -> p j c", j=16)
        winc = WBf[:, 4160:5200].rearrange("p (j c) -> p j c", j=4)
        KcT_s = uT[:, 4, 0:2048]
        VcT_s = uT[:, 5, 0:2048]
        KsT = uT[:, 6, 0:2048]
        KwT = uT[:, 7, 0:512]
        e_s = aB[:, 4160:4296]
        ecmp_ref[0] = aB[:, 4296:4360]
        negT_s = aB[:, 4400:4416].rearrange("p (g q) -> p g q", g=2)
        Vnew = aB[0:8, 4420:4816].rearrange("p (a b) -> p a b", a=3)
        szab = aB[0:8, 4816:5328]
        attn_s = aB[0:8, 5328:5840]
        accs = aF[0:8, 0:512]
        gsb_b = aF[0:8, 2960:2984]
        if l == 1:
            P.copy("dve", idxf, idx[:])
            P.ts("dve", idxf, idxf, float(npool * 128), None, ALU.add)
            P.copy("dve", idx[:], idxf)
        P.memset("pool", cmpc[:, :, 256:260], 1.0)
        P.memset("pool", slcc[:, :, 256:260], 1.0)
        P.memset("pool", winc[:, :, 256:260], 1.0)
        P.memset("pool", negT_s[32:64], 0.0)
        P.memset("pool", negT_s[64:128], 0.0)
        for b in range(16):
            tk = slice(TP + b * 8, TP + b * 8 + 8)
            for j in range(16):
                col = b * 16 + j
                for (dst, srcd) in ((cmpc, cc), (slcc, cs)):
                    def f(e, dst=dst, srcd=srcd, j=j, col=col):
                        return e.indirect_dma_start(out=dst[:, j, 0:256], out_offset=None,
                                                    in_=srcd.rearrange("l r c -> (l r) c"),
                                                    in_offset=bass.IndirectOffsetOnAxis(idx[:, col:col + 1], 0))
                    P.add("pool", f, reads=[idx[:, col:col + 1]], writes=[dst[:, j, 0:256]], is_dma=True)
            P.dma("pool", winc[:, :, 0:256], swin[l, b].rearrange("(j p) c -> p j c", p=128))
            P.dma("sp", Vnew, Vs16[b * 8:(b + 1) * 8])
            P.dma("sp", szab, sza[b * 8:(b + 1) * 8, 16, :])
            P.dma("sp", gsb_b, gs[b * 8:(b + 1) * 8, 16, :])
            for (src_t, c0, dstT, nj) in ((slcc, 0, KsT, 16), (winc, 0, KwT, 4), (cmpc, 0, KcT_s, 16), (cmpc, 128, VcT_s, 16)):
                for j0 in range(0, nj, 8):
                    n8 = min(8, nj - j0)
                    for j in range(j0, j0 + n8):
                        P.transpose(psT[:, (j - j0) * 128:(j - j0 + 1) * 128], src_t[:, j, c0:c0 + 128], ident_b[:])
                    P.copy(alt(), dstT[:, j0 * 128:(j0 + n8) * 128], psT[:, 0:n8 * 128])
            compress(KcT_s, VcT_s)
            for g in range(2):
                cmp_branch(8, 127, g,
                           lambda r, g=g: qT[g * 64:(g + 1) * 64, r, tk],
                           zpad[:, 15:142],
                           lambda h: bands[:, h, :],
                           gsb_b[:, g * 4:g * 4 + 4],
                           lambda h: accs[:, h * 64:(h + 1) * 64],
                           m1s[:], c1s[:],
                           negT_s[:, g, :])
            for h in range(8):
                g, r = h // 4, h % 4
                gb = g * 64
                q8 = qT[gb:gb + 64, r, tk]
                for br in (1, 2):
                    pss = rr()
                    KpT = KsT if br == 1 else KwT
                    cch = slcc if br == 1 else winc
                    nj = 16 if br == 1 else 4
                    for j in range(nj):
                        cs_ = slice(j * 8, (j + 1) * 8)
                        extra = []
                        if br == 1:
                            extra.append((expand[:, j * 128:(j + 1) * 128], negT_s[:, g, :]))
                        if br == 2 and j == 0:
                            extra.append((ident_b[:], acm[:, 0:8]))
                        if j == nj - 1:
                            extra.append((ident_b[:], strip[:, h, 128:136]))
                        P.matmul(pss[:, cs_], KpT[gb:gb + 64, j * 128:(j + 1) * 128], q8, start=True, stop=(len(extra) == 0))
                        for i, (lt, rh) in enumerate(extra):
                            P.matmul(pss[:, cs_], lt, rh, start=False, stop=(i == len(extra) - 1))
                    cn = slice(nj * 8, nj * 8 + 8)
                    P.matmul(pss[0:8, cn], KT[gb:gb + 64, br, tk], q8, start=True, stop=False)
                    P.matmul(pss[0:8, cn], ident_b[:, 0:8], strip[:, h, 0:8], start=False, stop=True)
                    P.act(e_s[:, 0:nj * 8], pss[:, 0:nj * 8], AF.Exp)
                    P.act(e_s[0:8, cn], pss[0:8, cn], AF.Exp)
                    pso = psD if br == 1 else psE
                    for j in range(nj):
                        P.matmul(pso[0:8, 0:129], e_s[:, j * 8:(j + 1) * 8], cch[:, j, 128:257], start=(j == 0), stop=False)
                    P.matmul(pso[0:8, 0:129], e_s[0:8, cn], Vnew[:, br, 0:129], start=False, stop=True)
                    finish_head(8, 1, pso, 129, g * 64, 128, gsb_b[:, br * 8 + h:br * 8 + h + 1],
                                lambda qi, h=h: accs[:, h * 64:(h + 1) * 64])
            P.tt("dve", attn_s, accs, szab, ALU.mult)
            for c in range(4):
                P.transpose(psT[:, c * 8:(c + 1) * 8], attn_s[:, c * 128:(c + 1) * 128], ident_b[0:8, 0:8])
            P.copy("act", mixa[:, 0:4, tk], psT[:, 0:32].rearrange("p (c n) -> p c n", c=4))
        ecmp_ref[0] = ecmp

    k.phase1 = phase1
    k.__dict__.update(locals())
    return k


def rel_bucket_np(dist):
    n = np.maximum(dist, 0)
    nf = np.maximum(n, 1).astype(np.float32)
    large = 16 + (np.log(nf / np.float32(16)) / np.float32(np.log(128 / 16)) * np.float32(16)).astype(np.int32)
    large = np.minimum(large, 31)
    return np.where(n < 16, n, large)


def make_consts():
    bf = ml_dtypes.bfloat16
    c = {}
    c["c_ident"] = np.eye(128, dtype=np.float32)
    d = np.arange(-127, 256)
    oh = np.zeros((33, 383), np.float32)
    b = rel_bucket_np(d)
    for i, dd in enumerate(d):
        if dd < 0:
            oh[32, i] = 1.0
        else:
            oh[b[i], i] = 1.0
    c["c_oh"] = oh
    pos = np.arange(T)
    ex = np.zeros((33, T), np.float32)
    ex[np.minimum(pos // 64, 32), pos] = 1.0
    c["c_expand"] = ex.astype(bf)
    cidx = np.arange(127)
    c_start = cidx * 16
    c_end = c_start + 31
    s_start = np.arange(33) * 64
    cover = ((c_start[:, None] < s_start[None, :] + 64) & (c_end[:, None] >= s_start[None, :])).astype(np.float32)
    c["c_cover"] = np.concatenate([cover, np.ones((127, 1), np.float32)], axis=1).astype(bf)
    zp = np.zeros((16, 270), np.float32)
    zp[np.arange(16), 127 + np.arange(16)] = 1.0
    c["c_zpad"] = zp.astype(bf)
    qpos = np.arange(2048)
    cur = qpos // 64
    blk = np.arange(33)
    forced = (blk[None] == 0) | (blk[None] == cur[:, None]) | (blk[None] == cur[:, None] - 1)
    allowed = blk[None] <= cur[:, None]
    m1 = (allowed & ~forced).astype(np.float32)
    c1 = np.where(allowed, np.where(forced, 1e4, 0.0), -1e4).astype(np.float32)
    c["c_m1p"] = m1.reshape(16, 128, 33).transpose(1, 0, 2).reshape(128, 16 * 33).copy()
    c["c_c1p"] = c1.reshape(16, 128, 33).transpose(1, 0, 2).reshape(128, 16 * 33).copy()
    qs = 2048 + np.arange(8)
    curs = qs // 64
    forced = (blk[None] == 0) | (blk[None] == curs[:, None]) | (blk[None] == curs[:, None] - 1)
    allowed = blk[None] <= curs[:, None]
    c["c_m1s"] = (allowed & ~forced).astype(np.float32)
    c["c_c1s"] = np.where(allowed, np.where(forced, 1e4, 0.0), -1e4).astype(np.float32)
    c["c_iota"] = np.arange(128, dtype=np.float32).reshape(128, 1)
    kl = np.arange(128)
    c["c_ac"] = np.where(kl[None, :] >= kl[:, None], -BIG, 0.0).astype(np.float32).astype(bf)
    return c


def core_inputs(inp, c, npool_rows=None):
    m = {}
    m["x_tok"] = np.concatenate([inp["x_prompt"][c], inp["x_sample"][16 * c:16 * c + 16].reshape(128, D)], 0)
    m["ple_tok"] = np.concatenate([inp["p_prompt"][:, c], inp["p_sample"][:, 16 * c:16 * c + 16].reshape(2, 128, 256)], 1)
    m["cc"] = inp["cache_cmp_kv"].reshape(2, -1, 256)
    m["cs"] = inp["cache_slc_kv"].reshape(2, -1, 256)
    m["ptab"] = inp["page_table"][16 * c:16 * c + 16].reshape(1, 256).astype(np.int32)
    m["swin"] = inp["state_win_kv"][:, 16 * c:16 * c + 16].reshape(2, 16, 512, 256)
    m["sconv"] = inp["state_conv"][:, 16 * c:16 * c + 16].reshape(2, 480, 512)
    m["norm_g"] = inp["norm_g"]
    m["w_in"] = inp["w_in"]
    m["convp"] = np.concatenate([inp["conv_w"], inp["conv_b"][:, None], inp["conv_ln_g"][:, None],
                                 inp["conv_ln_b"][:, None]], 1)
    m["cmp_pe"] = inp["cmp_pe"]
    m["cmp_w1"] = inp["cmp_w1"].reshape(2, 2, 2048, 128)
    m["cmp_w2"] = inp["cmp_w2"]
    m["w_out"] = inp["w_out"]
    m["w_ple"] = inp["w_ple"]
    m["w_pg"] = inp["w_ple_gate"]
    m["rel_bias"] = inp["rel_bias"]
    m["fng"] = inp["final_norm_g"].reshape(1, D)
    return {k_: np.ascontiguousarray(v) for k_, v in m.items()}


STAGE = 7


def program(k, stage=99):
    srcs = [k.x_tok, k.h1]
    for l in range(2):
        k.phase1(l, srcs[l])
        k.phase2a(l)
        k.phase3(l)
        k.phase2b(l)
        if stage >= 6:
            if l == 0:
                k.setup_attn()
            k.load_cmp_weights(l)
            k.attention_prompt(l)
            if stage >= 7:
                k.attention_sample(l)
            else:
                k.P.memset("pool", k.uT[:, 0:4, 2048:2176], 0.0)
        else:
            k.P.memset("pool", k.uT[:, 0:4, :], 0.0)
        k.phase5(l, srcs[l], l == 1)


_CACHE = {}


def kernel(**inp):
    npool = inp["cache_cmp_kv"].shape[1]
    if "nc" not in _CACHE:
        nc = bass.Bass("TRN2", target_bir_lowering=False)
        k = build(nc, npool, stage=STAGE)
        program(k, STAGE)
        k.P.emit()
        _CACHE["nc"] = nc
    nc = _CACHE["nc"]
    consts = make_consts()
    in_maps = []
    for c in range(8):
        m = core_inputs(inp, c)
        if STAGE < 6:
            m.pop("cc")
            m.pop("cs")
        m.update(consts)
        in_maps.append(m)
    res = run_bass_kernel_spmd(nc, in_maps, core_ids=list(range(8)))
    r = res.results
    y_p = np.stack([r[c]["y"][:TP] for c in range(8)])
    y_s = np.concatenate([r[c]["y"][TP:].reshape(16, 8, D) for c in range(8)])

    def kvp(name):
        return np.stack([r[c][name][:, :TP] for c in range(8)], 1).reshape(2, 8, TP, 2, 2, 64)

    def kvs(name):
        return np.concatenate([r[c][name][:, TP:].reshape(2, 16, 8, 256) for c in range(8)], 1).reshape(2, 128, 8, 2, 2, 64)
    win_p = np.stack([r[c]["nwin_p"] for c in range(8)], 1).reshape(2, 8, 512, 2, 2, 64)
    win_s = np.concatenate([r[c]["nwin_s"] for c in range(8)], 1).reshape(2, 128, 512, 2, 2, 64)
    conv_p = np.stack([r[c]["nconv_p"] for c in range(8)], 1)
    conv_s = np.concatenate([r[c]["nconv_s"] for c in range(8)], 1)
    f = lambda a: np.ascontiguousarray(a, dtype=np.float32)
    return (f(y_p), f(y_s), f(kvp("ncmp")), f(kvs("ncmp")), f(kvp("nslc")), f(kvs("nslc")),
            f(win_p), f(win_s), f(conv_p), f(conv_s))
```

```python
import contextlib
import numpy as np
import ml_dtypes
import concourse.bass as bass
import concourse.mybir as mybir
from concourse.bass_utils import run_bass_kernel_spmd

F32 = mybir.dt.float32
BF16 = mybir.dt.bfloat16
I32 = mybir.dt.int32
ALU = mybir.AluOpType
AF = mybir.ActivationFunctionType

ENGS = ("pe", "act", "dve", "pool", "sp")
NSLOT = {"sp": 28, "pool": 16, "act": 4}
BIG = 30000.0


def _dsize(dt):
    return mybir.dt.size(dt)


def _region(ap):
    t = ap.tensor
    space = str(ap.space).upper()
    name = t.name
    if "DRAM" in space or "HBM" in space:
        f0 = ap.offset
        ext = 0
        for st_, cnt in ap.ap:
            ext += (cnt - 1) * abs(st_)
        return ("D", name, 0, 1, f0, f0 + ext + 1)
    shp = list(t.shape)
    pstride = 1
    for s in shp[1:]:
        pstride *= s
    off = ap.offset
    p0 = off // pstride
    f0 = off % pstride
    aps = list(ap.ap)
    pcount = 1
    ext = 0
    for i, (st, cnt) in enumerate(aps):
        if i == 0:
            pcount = cnt if st == pstride or cnt == 1 else cnt
            if st != pstride and cnt > 1 and st != 0:
                pcount = 1
                ext += (cnt - 1) * abs(st)
            continue
        ext += (cnt - 1) * abs(st)
    f1 = f0 + ext + 1
    if "PSUM" in space:
        per_bank = 2048 // _dsize(t.dtype)
        f0 = (f0 // per_bank) * per_bank
        f1 = ((f1 + per_bank - 1) // per_bank) * per_bank
        return ("P", name, 0, 128, f0, f1)
    return ("S", name, p0, p0 + pcount, f0, f1)


def _overlap(a, b):
    return a[1] == b[1] and a[2] < b[3] and b[2] < a[3] and a[4] < b[5] and b[4] < a[5]


def _covers(a, b):
    return a[1] == b[1] and a[2] <= b[2] and a[3] >= b[3] and a[4] <= b[4] and a[5] >= b[5]


class Op:
    __slots__ = ("eng", "fn", "idx", "eidx", "deps", "is_dma", "dma_k", "signal", "waits", "sigval")


class Prog:
    def __init__(self, nc):
        self.nc = nc
        self.ops = []
        self.eng_ops = {e: [] for e in ENGS}
        self.track = {}
        self.dma_count = {e: 0 for e in ENGS}

    def add(self, eng, fn, reads=(), writes=(), is_dma=False):
        op = Op()
        op.eng = eng
        op.fn = fn
        op.idx = len(self.ops)
        op.eidx = len(self.eng_ops[eng])
        op.deps = set()
        op.is_dma = is_dma
        op.dma_k = None
        if is_dma:
            op.dma_k = self.dma_count[eng]
            self.dma_count[eng] += 1
        op.signal = False
        self.ops.append(op)
        self.eng_ops[eng].append(op)
        for a in reads:
            if a is None or isinstance(a, (int, float)):
                continue
            r = _region(a)
            self._access(op, r, r[0] == "P")
        for a in writes:
            if a is None:
                continue
            self._access(op, _region(a), True)
        return op

    def _access(self, op, reg, is_write):
        lst = self.track.get(reg[1])
        if lst is None:
            self.track[reg[1]] = [[reg, op, is_write]]
            return
        new = []
        for ent in lst:
            ereg, eop, ew = ent
            if eop is op:
                new.append(ent)
                continue
            if _overlap(ereg, reg):
                if is_write or ew:
                    op.deps.add(eop)
                    if is_write and _covers(reg, ereg):
                        continue
                    new.append(ent)
                else:
                    if (not eop.is_dma) and (not op.is_dma) and eop.eng == op.eng and ereg == reg:
                        continue
                    new.append(ent)
            else:
                new.append(ent)
        new.append([reg, op, is_write])
        self.track[reg[1]] = new

    def matmul(self, out, lhsT, rhs, start=True, stop=True, **kw):
        return self.add("pe", lambda e: e.matmul(out, lhsT, rhs, start=start, stop=stop, **kw),
                        reads=[lhsT, rhs], writes=[out])

    def transpose(self, out, in_, ident):
        return self.add("pe", lambda e: e.transpose(out, in_, ident), reads=[in_, ident], writes=[out])

    def act(self, out, in_, func, bias=None, scale=None, accum_out=None):
        kw = {}
        rd = [in_]
        if bias is not None:
            kw["bias"] = bias
            rd.append(bias)
        if scale is not None:
            kw["scale"] = scale
            rd.append(scale)
        wr = [out]
        if accum_out is not None:
            kw["accum_out"] = accum_out
            wr.append(accum_out)
        return self.add("act", lambda e: e.activation(out, in_, func, **kw), reads=rd, writes=wr)

    def tt(self, eng, out, in0, in1, op):
        return self.add(eng, lambda e: e.tensor_tensor(out, in0, in1, op), reads=[in0, in1], writes=[out])

    def ts(self, eng, out, in0, s1, s2, op0, op1=None):
        kw = {}
        if op1 is not None:
            kw["op1"] = op1
        return self.add(eng, lambda e: e.tensor_scalar(out, in0, s1, s2, op0, **kw),
                        reads=[in0, s1, s2], writes=[out])

    def stt(self, out, in0, scalar, in1, op0, op1):
        return self.add("dve", lambda e: e.scalar_tensor_tensor(out, in0, scalar, in1, op0, op1),
                        reads=[in0, in1, scalar], writes=[out])

    def copy(self, eng, out, in_):
        if eng == "act":
            return self.add(eng, lambda e: e.copy(out, in_), reads=[in_], writes=[out])
        return self.add(eng, lambda e: e.tensor_copy(out, in_), reads=[in_], writes=[out])

    def memset(self, eng, out, val):
        return self.add(eng, lambda e: e.memset(out, val), reads=[], writes=[out])

    def dma(self, queue, out, in_, **kw):
        return self.add(queue, lambda e: e.dma_start(out=out, in_=in_, **kw),
                        reads=[in_], writes=[out], is_dma=True)

    def emit(self):
        nc = self.nc
        waited = {f: {e: -1 for e in ENGS} for f in ENGS}
        for op in self.ops:
            op.waits = []
        for op in self.ops:
            f = op.eng
            need = {}
            for d in op.deps:
                if d.is_dma:
                    op.waits.append(d)
                    continue
                e = d.eng
                if e == f and not op.is_dma:
                    if e in ("pe", "sp"):
                        continue
                    if op.eidx - d.eidx > 3:
                        continue
                if d.eidx > waited[f][e]:
                    if e not in need or need[e].eidx < d.eidx:
                        need[e] = d
            for e, d in need.items():
                waited[f][e] = d.eidx
                d.signal = True
                op.waits.append(d)
        for e in ENGS:
            c = 0
            for op in self.eng_ops[e]:
                if op.is_dma:
                    continue
                if op.signal:
                    c += 1
                    op.sigval = c
        dma_waited = {f: set() for f in ENGS}
        with contextlib.ExitStack() as st:
            sems = {e: st.enter_context(nc.semaphore("sem_" + e)) for e in ENGS}
            dsems = {}
            for q, n in NSLOT.items():
                if self.dma_count[q] > 0:
                    dsems[q] = [st.enter_context(nc.semaphore("dsem_%s_%d" % (q, i))) for i in range(n)]
            block = st.enter_context(nc.Block())

            def dma_slot(d):
                n = NSLOT[d.eng]
                return dsems[d.eng][d.dma_k % n], 16 * (d.dma_k // n + 1)

            def run_engine(ename, eng):
                final_wait = {}
                for op in self.eng_ops[ename]:
                    for d in op.waits:
                        if d.is_dma:
                            if d in dma_waited[ename]:
                                continue
                            dma_waited[ename].add(d)
                            s, v = dma_slot(d)
                            eng.wait_ge(s, v)
                        else:
                            eng.wait_ge(sems[d.eng], d.sigval)
                    if op.is_dma:
                        n = NSLOT[ename]
                        if op.dma_k >= n:
                            eng.wait_ge(dsems[ename][op.dma_k % n], 16 * (op.dma_k // n))
                        ins = op.fn(eng)
                        s, v = dma_slot(op)
                        ins.then_inc(s, 16)
                        final_wait[op.dma_k % n] = (s, v)
                    else:
                        ins = op.fn(eng)
                        if op.signal:
                            ins.then_inc(sems[ename], 1)
                for k, (s, v) in final_wait.items():
                    eng.wait_ge(s, v)

            @block.sync
            def _(eng):
                run_engine("sp", eng)

            @block.tensor
            def _(eng):
                run_engine("pe", eng)

            @block.scalar
            def _(eng):
                run_engine("act", eng)

            @block.vector
            def _(eng):
                run_engine("dve", eng)

            @block.gpsimd
            def _(eng):
                run_engine("pool", eng)


T = 2176
TP = 2048
NT = 17
D = 1024
DIN = 3352
EPS = 1e-6
NB5 = [(0, 512), (512, 512), (1024, 512), (1536, 512), (2048, 128)]


class K:
    pass


def build(nc, npool, nlayers=2, stage=99):
    k = K()
    k.nc = nc
    P = Prog(nc)
    k.P = P
    st = contextlib.ExitStack()
    k.st = st

    def din(name, shape, dt=F32):
        return nc.dram_tensor(name, list(shape), dt, kind="ExternalInput").ap()

    def dout(name, shape, dt=F32):
        return nc.dram_tensor(name, list(shape), dt, kind="ExternalOutput").ap()

    def dscr(name, shape, dt=F32):
        return nc.dram_tensor(name, list(shape), dt, kind="Internal").ap()

    def sb(name, shape, dt):
        return st.enter_context(nc.sbuf_tensor(name, list(shape), dt))

    def ps(name, shape, dt=F32):
        return st.enter_context(nc.psum_tensor(name, list(shape), dt))

    x_tok = din("x_tok", [T, D])
    ple_tok = din("ple_tok", [2, T, 256])
    if stage >= 6:
        cc = din("cc", [2, npool * 128, 256])
        cs = din("cs", [2, npool * 128, 256])
    ptab = din("ptab", [1, 256], I32)
    swin = din("swin", [2, 16, 512, 256])
    sconv = din("sconv", [2, 480, 512])
    norm_g = din("norm_g", [2, D])
    w_in = din("w_in", [2, D, DIN])
    convp = din("convp", [2, 34, 512])
    cmp_pe = din("cmp_pe", [2, 2, 32, 64])
    cmp_w1 = din("cmp_w1", [2, 2, 2048, 128])
    cmp_w2 = din("cmp_w2", [2, 2, 128, 64])
    w_out = din("w_out", [2, D, D])
    w_ple = din("w_ple", [2, 256, D])
    w_pg = din("w_pg", [2, D, D])
    rel_bias = din("rel_bias", [32, 8])
    fng = din("fng", [1, D])
    c_ident = din("c_ident", [128, 128])
    c_oh = din("c_oh", [33, 383])
    c_expand = din("c_expand", [33, T], BF16)
    c_cover = din("c_cover", [127, 34], BF16)
    c_zpad = din("c_zpad", [16, 270], BF16)
    c_m1p = din("c_m1p", [128, 16 * 33])
    c_c1p = din("c_c1p", [128, 16 * 33])
    c_m1s = din("c_m1s", [8, 33])
    c_c1s = din("c_c1s", [8, 33])
    c_iota = din("c_iota", [128, 1])
    c_ac = din("c_ac", [128, 128], BF16)

    y = dout("y", [T, D])
    ncmp = dout("ncmp", [2, T, 256])
    nslc = dout("nslc", [2, T, 256])
    nwin_p = dout("nwin_p", [2, 512, 256])
    nwin_s = dout("nwin_s", [2, 16, 512, 256])
    nconv_p = dout("nconv_p", [2, 30, 512])
    nconv_s = dout("nconv_s", [2, 16, 30, 512])
    h1 = dscr("h1", [T, D])
    G1 = dscr("G1", [8, 128, 512])
    G16 = dscr("G16", [8, 16, 640])

    ident_f = sb("ident_f", [128, 128], F32)
    ident_b = sb("ident_b", [128, 128], BF16)
    ones_f = sb("ones_f", [128, 128], F32)
    epsb = sb("epsb", [128, 1], F32)
    uT = sb("uT", [128, 8, T], BF16)
    WA = sb("WA", [128, 8, 1024], BF16)
    WB = sb("WB", [128, 8, 1024], BF16)
    bigG = sb("bigG", [128, 4, T], BF16)
    bigZ = sb("bigZ", [128, 4 * T], BF16)
    mixc = sb("mixc", [128, 4, T], BF16)
    KT = sb("KT", [128, 3, T], BF16)
    Vt = sb("Vt", [128, 3, 17, 130], BF16)
    gs = sb("gs", [128, 17, 24], F32)
    aF = sb("aF", [128, 7168], F32)
    aB = sb("aB", [128, 8192], BF16)
    ssq = sb("ssq", [128, NT], F32)
    rs = sb("rs", [128, NT], F32)
    cw = sb("cw", [128, 4, 34], F32)
    gluP = bigG
    gluS = KT[:].rearrange("p a t -> p (a t)")[:, 0:2432].rearrange("p (c b r) -> p c b r", c=4, b=16)
    WC = bigG[:, 0, 0:2048].rearrange("p (c n) -> p c n", c=2)
    VcT = aB[:, 0:2048]
    cwraw = aF[0:34, 6656:7168]
    zc = bigZ[:].rearrange("p (c t) -> p c t", c=4)
    sza = bigZ[:].rearrange("p (t e) -> p t e", e=512)
    qT = bigG
    xin = [aF[:, 0:1024], aF[:, 1024:2048]]
    gbc = aF[:, 2048:3072]
    sg = [aF[:, 3072:3584], aF[:, 3584:4096]]
    glu_tail = aF[:, 4096:5120].rearrange("p (c n) -> p c n", c=4)
    stage_t = [aF[:, 5120:5888], aF[:, 5888:6656]]
    dw = aF[:, 0:2048].rearrange("p (c n) -> p c n", c=4)
    sq = aF[:, 2048:2560]
    st_m = aF[:, 2560:3072]
    st_v = aF[:, 3072:3584]
    st_t = aF[:, 3584:4096]
    scraw = aF[:, 5120:5632]
    tailtok = aF[:, 5632:6656].rearrange("p (c n) -> p c n", c=2)
    Dk = aB[:, 0:3968].rearrange("p (k n) -> p k n", k=31)
    scb = aB[:, 3968:6016].rearrange("p (c n) -> p c n", c=4)
    ub = aB[:, 6016:7040]
    junk = aB[:, 7040:8064]

    psA = ps("psA", [128, 512])
    psB = ps("psB", [128, 512])
    psC = ps("psC", [128, 512])
    psT = ps("psT", [128, 1024], BF16)
    psD = ps("psD", [128, 512])
    psE = ps("psE", [128, 512])
    psF = ps("psF", [128, 512])
    psG = ps("psG", [128, 512])
    rot3 = [psA, psB, psC]

    P.dma("sp", ident_f[:], c_ident)
    P.copy("dve", ident_b[:], ident_f[:])
    P.memset("dve", ones_f[:], 1.0)
    P.memset("dve", epsb[:], EPS)
    P.memset("pool", Vt[:], 1.0)

    wrow = lambda w, l: w[l].rearrange("(c p) n -> p c n", p=128)
    cnt = [0]

    def rr():
        cnt[0] += 1
        return rot3[cnt[0] % 3]

    def alt():
        cnt[0] += 1
        return "act" if cnt[0] % 2 else "dve"

    def recip(out, in_):
        return P.add("dve", lambda e: e.reciprocal(out, in_), reads=[in_], writes=[out])

    def phase1(l, src):
        P.dma("sp", gbc[:], norm_g[l:l + 1, :].broadcast_to([128, D]))
        for t in range(NT):
            xt = xin[t % 2]
            P.dma("sp", xt, src[t * 128:(t + 1) * 128, :])
            P.act(junk[:], xt[:], AF.Square, accum_out=ssq[:, t:t + 1])
            P.act(rs[:, t:t + 1], ssq[:, t:t + 1], AF.Sqrt, bias=epsb[:, 0:1], scale=1.0 / D)
            recip(rs[:, t:t + 1], rs[:, t:t + 1])
            P.stt(ub[:], xt[:], rs[:, t:t + 1], gbc[:], ALU.mult, ALU.mult)
            for c in range(8):
                P.transpose(psT[:, c * 128:(c + 1) * 128], ub[:, c * 128:(c + 1) * 128], ident_b[:])
            P.copy(alt(), uT[:, :, t * 128:(t + 1) * 128], psT[:].rearrange("p (c n) -> p c n", c=8))

    def fm_mm(pst, lhs_of_c, nb, m=128):
        n0, nn = NB5[nb]
        for c in range(8):
            P.matmul(pst[0:m, 0:nn], lhs_of_c(c), uT[:, c, n0:n0 + nn], start=(c == 0), stop=(c == 7))

    def phase2a(l):
        wl = wrow(w_in, l)
        P.memset("pool", gluP[:, :, 0:30], 0.0)
        P.dma("pool", WB[:, :, 0:512], wl[:, :, 0:512])
        P.dma("pool", WB[:, :, 512:1024], wl[:, :, 512:1024])
        for ch in range(4):
            for nb in range(5):
                n0, nn = NB5[nb]
                fm_mm(psA, lambda c: WB[:, c, ch * 128:(ch + 1) * 128], nb)
                fm_mm(psB, lambda c: WB[:, c, 512 + ch * 128:512 + (ch + 1) * 128], nb)
                s = sg[nb % 2]
                P.act(s[:, 0:nn], psB[:, 0:nn], AF.Sigmoid)
                if nb < 4:
                    P.tt("dve", gluP[:, ch, 30 + n0:30 + n0 + nn], psA[:, 0:nn], s[:, 0:nn], ALU.mult)
                    if nb == 3:
                        P.tt("dve", glu_tail[:, ch, 0:128], psA[:, 384:512], s[:, 384:512], ALU.mult)
                else:
                    P.tt("dve", gluS[:, ch, :, 30:38], psA[:, 0:128].rearrange("p (b q) -> p b q", q=8),
                         s[:, 0:128].rearrange("p (b q) -> p b q", q=8), ALU.mult)
                    P.tt("dve", glu_tail[:, ch, 128:256], psA[:, 0:128], s[:, 0:128], ALU.mult)
        P.dma("pool", WB[:, :, 0:512], wl[:, :, 1024:1536])
        for ch in range(4):
            for nb in range(5):
                n0, nn = NB5[nb]
                pst = rr()
                fm_mm(pst, lambda c: WB[:, c, ch * 128:(ch + 1) * 128], nb)
                P.act(zc[:, ch, n0:n0 + nn], pst[:, 0:nn], AF.Silu)

    def tm_group(l, half, ncols, evac):
        for t in range(17):
            sl = slice(t * 128, (t + 1) * 128)
            pst = rr()
            for c in range(8):
                P.matmul(pst[:, 0:ncols], uT[:, c, sl], WB[:, c, half * 512:half * 512 + ncols],
                         start=(c == 0), stop=(c == 7))
            evac(t, sl, pst)

    def phase2b(l):
        wl = wrow(w_in, l)
        P.memset("pool", Vs16[:, :, 128:132], 1.0)
        for two in range(2):
            for c in range(8):
                P.dma("pool", WB[:, c, 512:1024].rearrange("p (j two d) -> p j two d", two=2, d=64)[:, :, two, :],
                      wl[:, c, 1536 + two * 256:1536 + two * 256 + 256].rearrange("p (j d) -> p j d", d=64))
        for j in range(4):
            for nb in range(5):
                n0, nn = NB5[nb]
                pst = rr()
                fm_mm(pst, lambda c: WB[:, c, 512 + j * 128:512 + (j + 1) * 128], nb)
                P.act(qT[:, j, n0:n0 + nn], pst[:, 0:nn], AF.Copy, scale=0.125)
        outs = [ncmp, nslc, None]
        for br in range(3):
            half = br % 2
            c0 = 2048 + br * 256
            P.dma("pool", WB[:, :, half * 512:half * 512 + 256], wl[:, :, c0:c0 + 256])
            if br == 0:
                P.dma("pool", WB[:, :, 256:280], wl[:, :, 3328:3352])
            for nb in range(5):
                n0, nn = NB5[nb]
                pst = rr()
                fm_mm(pst, lambda c: WB[:, c, half * 512:half * 512 + 128], nb)
                P.copy(alt(), KT[:, br, n0:n0 + nn], pst[:, 0:nn])
            if br == 0:
                for nb in range(4):
                    n0, nn = NB5[nb]
                    pst = rr()
                    fm_mm(pst, lambda c: WB[:, c, 128:256], nb)
                    P.copy(alt(), VcT[:, n0:n0 + nn], pst[:, 0:nn])

            def evac(t, sl, pst, br=br):
                stg = stage_t[t % 2]
                P.copy("act", stg[:, 0:256], pst[:, 0:256])
                P.copy("pool", Vt[:, br, t, :].rearrange("p (g e) -> p g e", g=2)[:, :, 0:64],
                       stg[:, 128:256].rearrange("p (g d) -> p g d", g=2))
                if t == 16:
                    P.copy("pool", Vs16[:, br, 0:128], stg[:, 128:256])
                if br == 0:
                    P.act(gs[:, t, :], pst[:, 256:280], AF.Sigmoid)
                if br < 2:
                    P.dma("sp", outs[br][l, sl, :], stg[:, 0:256])
                else:
                    if 12 <= t < 16:
                        P.dma("sp", nwin_p[l, (t - 12) * 128:(t - 11) * 128, :], stg[:, 0:256])
                    if t == 16:
                        for b in range(16):
                            P.dma("sp", nwin_s[l, b, 504:512, :], stg[b * 8:(b + 1) * 8, 0:256])
            tm_group(l, half, 280 if br == 0 else 256, evac)
        if stage != 41:
            P.dma("sp", nwin_s[l][:, 0:504, :], swin[l][:, 8:512, :])
        P.dma("pool", WB[:, :, 512:1024], wl[:, :, 2816:3328])
        tm_group(l, 1, 512, lambda t, sl, pst: P.act(sza[:, t, :], pst[:, 0:512], AF.Silu))

    ones_b = sb("ones_b", [128, 128], BF16)
    P.memset("pool", ones_b[:], 1.0)

    def phase3(l):
        for ch in range(4):
            for hlf in range(2):
                P.transpose(psD[:, (hlf * 4 + ch) % 4 * 128:((hlf * 4 + ch) % 4 + 1) * 128],
                            glu_tail[:, ch, hlf * 128:(hlf + 1) * 128], ident_f[:])
                P.copy("dve", tailtok[:, hlf, ch * 128:(ch + 1) * 128],
                       psD[:, (hlf * 4 + ch) % 4 * 128:((hlf * 4 + ch) % 4 + 1) * 128])
        P.dma("sp", nconv_p[l], tailtok[98:128, 0, :])
        for b in range(16):
            P.dma("sp", nconv_s[l, b, 22:30, :], tailtok[b * 8:(b + 1) * 8, 1, :])
        P.dma("sp", nconv_s[l][:, 0:22, :], sconv[l].rearrange("(b r) c -> b r c", r=30)[:, 8:30, :])
        P.dma("sp", cwraw, convp[l])
        for ch in range(4):
            P.transpose(psD[:, 0:34], cwraw[0:34, ch * 128:(ch + 1) * 128], ident_f[0:34, 0:34])
            P.copy("dve", cw[:, ch, :], psD[:, 0:34])
        for r in range(4):
            nr = 128 if r < 3 else 96
            P.dma("sp", scraw[0:nr, :], sconv[l, r * 128:r * 128 + nr, :])
            P.copy("pool", scb[0:nr, r, :], scraw[0:nr, :])
        for ch in range(4):
            for r in range(4):
                nr = 128 if r < 3 else 96
                P.transpose(psT[:, r * 128:r * 128 + nr], scb[0:nr, r, ch * 128:(ch + 1) * 128], ident_b[0:nr, 0:nr])
            P.copy("dve", gluS[:, ch, :, 0:30], psT[:, 0:480].rearrange("p (b r) -> p b r", r=30))
        for ch in range(4):
            for kk in range(31):
                P.ts("dve" if kk % 2 else "pool", Dk[:, kk, :], ident_f[:], cw[:, ch, kk:kk + 1], None, ALU.mult)
            for nb in range(5):
                n0, nn = NB5[nb]
                pst = rr()
                for kk in range(31):
                    if nb < 4:
                        rhs = gluP[:, ch, n0 + kk:n0 + kk + 512]
                    else:
                        rhs = gluS[:, ch, :, kk:kk + 8]
                    P.matmul(pst[:, 0:nn], Dk[:, kk, :], rhs, start=(kk == 0), stop=(kk == 30))
                P.act(mixc[:, ch, n0:n0 + nn], pst[:, 0:nn], AF.Identity, bias=cw[:, ch, 31:32])
        for nb in range(5):
            n0, nn = NB5[nb]
            for ch in range(4):
                P.act(sq[:, 0:nn], mixc[:, ch, n0:n0 + nn], AF.Square)
                P.matmul(psD[:, 0:nn], ones_b[:], mixc[:, ch, n0:n0 + nn], start=(ch == 0), stop=(ch == 3))
                P.matmul(psE[:, 0:nn], ones_f[:], sq[:, 0:nn], start=(ch == 0), stop=(ch == 3))
            P.ts("dve", st_m[:, 0:nn], psD[:, 0:nn], 1.0 / 512, None, ALU.mult)
            P.tt("dve", st_v[:, 0:nn], st_m[:, 0:nn], st_m[:, 0:nn], ALU.mult)
            P.stt(st_v[:, 0:nn], psE[:, 0:nn], 1.0 / 512, st_v[:, 0:nn], ALU.mult, ALU.subtract)
            P.act(st_t[:, 0:nn], st_v[:, 0:nn], AF.Sqrt, bias=epsb[:, 0:1])
            recip(st_t[:, 0:nn], st_t[:, 0:nn])
            for ch in range(4):
                P.tt("dve", sq[:, 0:nn], mixc[:, ch, n0:n0 + nn], st_m[:, 0:nn], ALU.subtract)
                P.tt("dve", sq[:, 0:nn], sq[:, 0:nn], st_t[:, 0:nn], ALU.mult)
                P.act(sq[:, 0:nn], sq[:, 0:nn], AF.Silu, bias=cw[:, ch, 33:34], scale=cw[:, ch, 32:33])
                P.tt("dve", mixc[:, ch, n0:n0 + nn], sq[:, 0:nn], zc[:, ch, n0:n0 + nn], ALU.mult)

    mixa = uT

    def phase5(l, src, last):
        P.dma("pool", WA[:], wrow(w_out, l))
        P.dma("pool", WB[:], wrow(w_pg, l))
        P.dma("pool", WC, wrow(w_ple, l))
        if last:
            P.dma("sp", gbc, fng[0:1, :].broadcast_to([128, D]))
        hm = aF[:, 3072:4096]
        gt = aF[:, 4096:5120]
        plt = aF[:, 5120:5376]
        plb = aB[:, 0:256]
        plT = aB[:, 256:512].rearrange("p (c n) -> p c n", c=2)
        hmT = aB[:, 512:1536].rearrange("p (c n) -> p c n", c=8)
        for t in range(NT):
            sl = slice(t * 128, (t + 1) * 128)
            xt = xin[t % 2]
            P.dma("sp", xt, src[sl, :])
            P.dma("sp", plt, ple_tok[l, sl, :])
            for hf in range(2):
                pst = [psA, psB][hf]
                for e in range(8):
                    lhs = mixc[:, e, sl] if e < 4 else mixa[:, e - 4, sl]
                    P.matmul(pst[:, 0:512], lhs, WA[:, e, hf * 512:(hf + 1) * 512], start=(e == 0), stop=(e == 7))
                P.tt("dve", hm[:, hf * 512:(hf + 1) * 512], pst[:, 0:512], xt[:, hf * 512:(hf + 1) * 512], ALU.add)
            P.copy("pool", ub, hm)
            P.copy("pool", plb, plt)
            for c in range(8):
                P.transpose(psT[:, c * 128:(c + 1) * 128], ub[:, c * 128:(c + 1) * 128], ident_b[:])
            P.copy("act", hmT, psT[:].rearrange("p (c n) -> p c n", c=8))
            for c in range(2):
                P.transpose(psT[:, c * 128:(c + 1) * 128], plb[:, c * 128:(c + 1) * 128], ident_b[:])
            P.copy("act", plT, psT[:, 0:256].rearrange("p (c n) -> p c n", c=2))
            for hf in range(2):
                pg = [psC, psD][hf]
                pp = [psE, psF][hf]
                for c in range(8):
                    P.matmul(pg[:, 0:512], hmT[:, c, :], WB[:, c, hf * 512:(hf + 1) * 512], start=(c == 0), stop=(c == 7))
                for c in range(2):
                    P.matmul(pp[:, 0:512], plT[:, c, :], WC[:, c, hf * 512:(hf + 1) * 512], start=(c == 0), stop=(c == 1))
                P.act(gt[:, hf * 512:(hf + 1) * 512], pg[:, 0:512], AF.Sigmoid)
                P.tt("dve", gt[:, hf * 512:(hf + 1) * 512], gt[:, hf * 512:(hf + 1) * 512], pp[:, 0:512], ALU.mult)
            P.tt("dve", hm, hm, gt, ALU.add)
            if not last:
                P.dma("sp", h1[sl, :], hm)
            else:
                P.act(junk, hm, AF.Square, accum_out=ssq[:, t:t + 1])
                P.act(rs[:, t:t + 1], ssq[:, t:t + 1], AF.Sqrt, bias=epsb[:, 0:1], scale=1.0 / D)
                recip(rs[:, t:t + 1], rs[:, t:t + 1])
                P.stt(gt, hm, rs[:, t:t + 1], gbc, ALU.mult, ALU.mult)
                P.dma("sp", y[sl, :], gt)

    strip = sb("strip", [128, 8, 256], BF16)
    band = sb("band", [128, 8, 128], BF16)
    bands = sb("bands", [128, 8, 8], BF16)
    expand = sb("expand", [128, T], BF16)
    cover = sb("cover", [127, 34], BF16)
    zpad = sb("zpad", [128, 270], BF16)
    m1c1 = aF[:, 5704:5836].rearrange("p (a b) -> p a b", a=2)
    m1s = sb("m1s", [8, 33], F32)
    c1s = sb("c1s", [8, 33], F32)
    acm = sb("acm", [128, 128], BF16)
    tb = sb("tb", [33, 8], F32)
    r31 = sb("r31", [32, 8], F32)
    ohs = aF[0:33, 6000:6383]
    tbh = aF[0:33, 6400:6528]
    w2 = sb("w2", [128, 2, 64], BF16)
    cvec = sb("cvec", [128, 2], F32)
    peraw = aF[0:32, 5576:5704].rearrange("p (kv d) -> p kv d", kv=2)
    peT = sb("peT", [64, 2, 32], BF16)
    Vs16 = aB[:, 7424:7820].rearrange("p (a b) -> p a b", a=3)
    idx = sb("idx", [128, 256], I32)
    idxf = aF[:, 6400:6656]
    iot = sb("iot", [128, 1], F32)
    smallf = aF[:, 5836:5964]

    def AP(t, off, pat):
        return bass.AP(t.tensor, off, pat)

    def setup_attn():
        P.memset("pool", expand[32:64, :], 0.0)
        P.memset("pool", expand[64:128, :], 0.0)
        P.memset("pool", zpad[:], 0.0)
        P.memset("pool", band[:], 0.0)
        P.memset("pool", bands[:], 0.0)
        P.dma("sp", expand[0:33, :], c_expand)
        P.dma("sp", cover[:], c_cover)
        P.dma("sp", zpad[0:16, :], c_zpad)
        P.dma("sp", m1s[:], c_m1s)
        P.dma("sp", c1s[:], c_c1s)
        P.dma("sp", acm[:], c_ac)
        P.dma("sp", ohs, c_oh)
        P.dma("sp", iot[:], c_iota)
        P.dma("sp", tb[0:32, :], rel_bias)
        P.dma("sp", r31[:], rel_bias[31:32, :].broadcast_to([32, 8]))
        P.tt("dve", tb[0:32, :], tb[0:32, :], r31[:], ALU.subtract)
        P.memset("dve", tb[32:33, :], -BIG)
        gsb = aF[:, 3000:3383]
        for h in range(8):
            P.ts("dve", tbh, ones_f[0:33, :], tb[:, h:h + 1], None, ALU.mult)
            P.matmul(psD[:, 0:383], tbh, ohs, start=True, stop=True)
            P.copy("act", gsb, psD[:, 0:383])
            P.dma("sp", AP(G1, h * 128 * 512, [[513, 128], [1, 383]]), gsb)
            P.dma("sp", AP(G16, h * 16 * 640, [[656, 16], [1, 383]]), gsb[0:16, :])
        strip_f = aF[:, 3400:5448].rearrange("p (h x) -> p h x", h=8)
        band_f = aF[0:16, 5448:6472].rearrange("p (h x) -> p h x", h=8)
        bands_f = aF[0:16, 6472:6536].rearrange("p (h x) -> p h x", h=8)
        P.dma("sp", strip_f, AP(G1, 127, [[512, 128], [65536, 8], [1, 256]]))
        P.dma("sp", band_f, AP(G16, 240, [[640, 16], [10240, 8], [1, 128]]))
        P.dma("sp", bands_f, AP(G16, 352, [[640, 16], [10240, 8], [1, 8]]))
        P.copy("dve", strip[:], strip_f)
        P.copy("dve", band[0:16], band_f)
        P.copy("dve", bands[0:16], bands_f)
        P.dma("sp", idx[:], ptab[0:1, :].broadcast_to([128, 256]))
        P.copy("dve", idxf, idx[:])
        P.ts("dve", idxf, idxf, 128.0, iot[:, 0:1], ALU.mult, ALU.add)
        P.copy("dve", idx[:], idxf)

    w1 = WA[:].rearrange("p c n -> p (c n)").rearrange("p (kv pos h) -> p kv pos h", kv=2, pos=32)
    kcT = aB[:, 6400:6528]
    vcaug = aB[0:127, 6528:6724].rearrange("p (g e) -> p g e", g=2)
    hid = aB[:, 6724:6852]
    hid = aB[:, 7820:8076]

    def load_cmp_weights(l):
        for kv in range(2):
            for cp in range(2):
                P.dma("pool", w1[cp * 64:(cp + 1) * 64, kv, :, :],
                      cmp_w1[l, kv].rearrange("(pos d) h -> d pos h", d=64))
        P.dma("pool", w2[:], cmp_w2[l].rearrange("kv h d -> h kv d"))
        P.dma("sp", peraw, cmp_pe[l].rearrange("kv pos d -> pos kv d"))
        for kv in range(2):
            P.transpose(psD[0:64, kv * 32:(kv + 1) * 32], peraw[0:32, kv, :], ident_f[0:32, 0:32])
        P.copy("dve", peT[:], psD[0:64, 0:64].rearrange("p (kv pos) -> p kv pos", kv=2))
        for kv in range(2):
            for pos in range(32):
                P.matmul(psD[:, 64 + kv:65 + kv], w1[0:64, kv, pos, :], peT[:, kv, pos:pos + 1],
                         start=(pos == 0), stop=(pos == 31))
        P.copy("dve", cvec[:], psD[:, 64:66])

    cmp_tiles = [kcT, vcaug, hid]

    def compress(Ksrc, Vsrc):
        kcT, vcaug, hid = cmp_tiles
        for g in range(2):
            P.copy("pool", vcaug[:, g, 64:98], cover[:, :])
        for kv in range(2):
            src = Ksrc if kv == 0 else Vsrc
            pg = [psA, psB] if kv == 0 else [psC, psA]
            for pos in range(32):
                for g in range(2):
                    gb = g * 64
                    P.matmul(pg[g][:, 0:127], w1[gb:gb + 64, kv, pos, :], src[gb:gb + 64, pos:pos + 16 * 126 + 1:16],
                             start=(pos == 0), stop=(pos == 31))
            for g in range(2):
                gb = g * 64
                P.act(hid[:, g * 127:(g + 1) * 127], pg[g][:, 0:127], AF.Silu, bias=cvec[:, kv:kv + 1])
                hg = hid[:, g * 127:(g + 1) * 127]
                if kv == 0:
                    P.matmul(psE[gb:gb + 64, 0:127], w2[:, 0, :], hg, start=True, stop=True)
                    P.copy("dve", kcT[gb:gb + 64, 0:127], psE[gb:gb + 64, 0:127])
                else:
                    P.matmul(psE[0:127, 128 + g * 64:192 + g * 64], hg, w2[:, 1, :], start=True, stop=True)
                    P.copy("dve", vcaug[:, g, 0:64], psE[0:127, 128 + g * 64:192 + g * 64])

    es = [aB[:, i * 512:(i + 1) * 512] for i in range(3)]
    ecmp = aB[:, 1536:2048]
    negT = aB[:, 2048:6400].rearrange("p (g t) -> p g t", g=2)
    negb = aB[:, 6852:6885]
    attnb = aB[:, 6912:7424]
    accb = aF[:, 0:2048].rearrange("p (q e) -> p q e", q=4)
    cmpo = aF[:, 2048:2440].rearrange("p (r e) -> p r e", r=4)
    tmpo = aF[:, 2440:2960].rearrange("p (q e) -> p q e", q=4)
    rinv = smallf[:, 0:4]
    coef = smallf[:, 4:8]
    imp = smallf[:, 8:41]
    m8 = smallf[:, 48:56]
    ecnt = [0]
    ecmp_ref = [ecmp]

    def cmp_branch(rows, nc_, g, qcols, band_lhsT, band_rhs_of_h, gate_ap, acc_of_h, m1, c1, negT_out):
        gb = g * 64
        kcT, vcaug, hid = cmp_tiles
        for r in range(4):
            h = g * 4 + r
            P.matmul(psG[0:nc_, r * rows:(r + 1) * rows], kcT[gb:gb + 64, 0:nc_], qcols(r), start=True, stop=False)
            P.matmul(psG[0:nc_, r * rows:(r + 1) * rows], band_lhsT, band_rhs_of_h(h), start=False, stop=True)
        ecm = ecmp_ref[0]
        P.act(ecm[0:nc_, 0:4 * rows], psG[0:nc_, 0:4 * rows], AF.Exp)
        for r in range(4):
            P.matmul(psF[0:rows, r * 98:(r + 1) * 98], ecm[0:nc_, r * rows:(r + 1) * rows], vcaug[0:nc_, g, :],
                     start=True, stop=True)
        P.copy("act", cmpo[0:rows], psF[0:rows, 0:392].rearrange("p (r e) -> p r e", r=4))
        P.ts("dve", rinv[0:rows], cmpo[0:rows, :, 97], 1e-30, None, ALU.add)
        recip(rinv[0:rows], rinv[0:rows])
        P.ts("dve", imp[0:rows], cmpo[0:rows, 0, 64:97], rinv[0:rows, 0:1], None, ALU.mult)
        for r in range(1, 4):
            P.stt(imp[0:rows], cmpo[0:rows, r, 64:97], rinv[0:rows, r:r + 1], imp[0:rows], ALU.mult, ALU.add)
        P.tt("dve", coef[0:rows], rinv[0:rows], gate_ap, ALU.mult)
        for r in range(4):
            P.ts("dve", acc_of_h(g * 4 + r), cmpo[0:rows, r, 0:64], coef[0:rows, r:r + 1], None, ALU.mult)
        P.tt("dve", imp[0:rows], imp[0:rows], m1, ALU.mult)
        P.tt("dve", imp[0:rows], imp[0:rows], c1, ALU.add)
        P.add("dve", lambda e: e.max(out=m8[0:rows], in_=imp[0:rows]), reads=[imp[0:rows]], writes=[m8[0:rows]])
        P.ts("dve", negb[0:rows], imp[0:rows], m8[0:rows, 7:8], -BIG, ALU.is_lt, ALU.mult)
        P.transpose(psT[0:33, 0:rows], negb[0:rows], ident_b[0:rows, 0:rows])
        P.copy("dve", negT_out[0:33], psT[0:33, 0:rows])

    def finish_head(rows, nq, pso, width, ocol, scol, gate_of_q, acc_of_q):
        P.copy("act", tmpo[0:rows, 0:nq, 0:width], pso[0:rows, 0:nq * width].rearrange("p (q e) -> p q e", q=nq))
        P.ts("dve", rinv[0:rows, 0:nq], tmpo[0:rows, 0:nq, scol], 1e-30, None, ALU.add)
        recip(rinv[0:rows, 0:nq], rinv[0:rows, 0:nq])
        P.tt("dve", coef[0:rows, 0:nq], rinv[0:rows, 0:nq], gate_of_q, ALU.mult)
        for qi in range(nq):
            a = acc_of_q(qi)
            P.stt(a, tmpo[0:rows, qi, ocol:ocol + 64], coef[0:rows, qi:qi + 1], a, ALU.mult, ALU.add)

    def attention_prompt(l):
        cmp_tiles[:] = [kcT, vcaug, hid]
        ecmp_ref[0] = ecmp
        compress(KT[:, 0, 0:TP], VcT[:, 0:TP])
        yield
        P.memset("pool", negT[32:64], 0.0)
        P.memset("pool", negT[64:128], 0.0)
        pso2 = [psD, psE]
        for qb in range(4):
            q0 = qb * 512
            for qi in range(4):
                qt = qb * 4 + qi
                qs = slice(qt * 128, (qt + 1) * 128)
                nc_ = min(127, 8 * qt + 7)
                c_off = 8 * qt - 9
                mc = m1c1[:, qt % 2, :]
                P.dma("sp", mc[:, 0:33], c_m1p[:, qt * 33:(qt + 1) * 33])
                P.dma("sp", mc[:, 33:66], c_c1p[:, qt * 33:(qt + 1) * 33])
                for g in range(2):
                    cmp_tiles[:] = [kcT, vcaug, hid]
                    ecmp_ref[0] = ecmp
                    cmp_branch(128, nc_, g,
                               lambda r, g=g, qs=qs: qT[g * 64:(g + 1) * 64, r, qs],
                               zpad[:, 127 - c_off:127 - c_off + nc_],
                               lambda h: band[:, h, :],
                               gs[:, qt, g * 4:g * 4 + 4],
                               lambda h, qi=qi: accb[:, qi, h * 64:(h + 1) * 64],
                               mc[:, 0:33], mc[:, 33:66],
                               negT[:, g, qs])
                yield
            for h in range(8):
                g, r = h // 4, h % 4
                gb = g * 64
                for br in (1, 2):
                    pso = pso2[br - 1]
                    P.memset("dve", pso[:, 0:260], 0.0)
                    kt_lo = 0 if br == 1 else max(0, 4 * qb - 4)
                    for kt in range(kt_lo, 4 * qb + 4):
                        qi_lo = max(0, kt - 4 * qb)
                        qi_hi = 3 if br == 1 else min(3, kt + 4 - 4 * qb)
                        c_lo, c_hi = qi_lo * 128, (qi_hi + 1) * 128
                        pss = rr()
                        ks = slice(kt * 128, (kt + 1) * 128)
                        extra = []
                        if br == 1:
                            extra.append((c_lo, c_hi, expand[:, ks], negT[:, g, q0 + c_lo:q0 + c_hi]))
                        d0 = kt - 4 * qb
                        if 0 <= d0 <= 2:
                            extra.append((d0 * 128, d0 * 128 + 256, ident_b[:], strip[:, h, 0:256]))
                        elif d0 == 3:
                            extra.append((384, 512, ident_b[:], strip[:, h, 0:128]))
                        elif d0 == -1:
                            extra.append((0, 128, ident_b[:], strip[:, h, 128:256]))
                        if br == 2 and 0 <= d0 + 4 <= 3:
                            extra.append(((d0 + 4) * 128, (d0 + 5) * 128, ident_b[:], acm[:]))
                        P.matmul(pss[:, c_lo:c_hi], KT[gb:gb + 64, br, ks], qT[gb:gb + 64, r, q0 + c_lo:q0 + c_hi],
                                 start=True, stop=(len(extra) == 0))
                        for i, (a, b_, lt, rh) in enumerate(extra):
                            P.matmul(pss[:, a:b_], lt, rh, start=False, stop=(i == len(extra) - 1))
                        ecnt[0] += 1
                        e = es[ecnt[0] % 3]
                        P.act(e[:, c_lo:c_hi], pss[:, c_lo:c_hi], AF.Exp)
                        for qi in range(qi_lo, qi_hi + 1):
                            P.matmul(pso[:, qi * 65:(qi + 1) * 65], e[:, qi * 128:(qi + 1) * 128],
                                     Vt[:, br, kt, g * 65:(g + 1) * 65], start=False, stop=False, skip_group_check=True)
                    finish_head(128, 4, pso, 65, 0, 64,
                                gs[:, qb * 4:qb * 4 + 4, br * 8 + h],
                                lambda qi, h=h: accb[:, qi, h * 64:(h + 1) * 64])
                    yield
            for qi in range(4):
                qt = qb * 4 + qi
                P.tt("dve", attnb, accb[:, qi, :], sza[:, qt, :], ALU.mult)
                for c in range(4):
                    P.transpose(psT[:, c * 128:(c + 1) * 128], attnb[:, c * 128:(c + 1) * 128], ident_b[:])
                P.copy("act", mixa[:, 0:4, qt * 128:(qt + 1) * 128], psT[:, 0:512].rearrange("p (c n) -> p c n", c=4))

    def attention_sample(l):
        WBf = WB[:].rearrange("p c n -> p (c n)")
        KT0 = KT[:, 0, :]
        cmpq = [KT0[:, 0:1024].rearrange("p (j c) -> p j c", j=4), KT0[:, 1024:2048].rearrange("p (j c) -> p j c", j=4)]
        slcc = WBf[:, 0:4160].rearrange("p (j c) -> p j c", j=16)
        winc = WBf[:, 4160:5200].rearrange("p (j c) -> p j c", j=4)
        KcT_s = uT[:, 4, 0:2048]
        VcT_s = uT[:, 5, 0:2048]
        KsT = uT[:, 6, 0:2048]
        KwT = uT[:, 7, 0:512]
        e_s = WBf[:, 5200:5744]
        ecmp_s = WBf[:, 5744:5808]
        negT_s = WBf[:, 5808:5872].rearrange("p (g r q) -> p g r q", g=2, r=4)
        ac32 = WBf[:, 5872:5904].rearrange("p (r q) -> p r q", r=4)
        Vnew = WBf[0:8, 5904:6300].rearrange("p (a b) -> p a b", a=3)
        szab = WBf[0:8, 6300:6812]
        attn_s = WBf[0:8, 6812:7324]
        kcT_s = WBf[:, 7324:7452]
        vcaug_s = WBf[0:127, 7452:7648].rearrange("p (g e) -> p g e", g=2)
        hid_s = WBf[:, 7648:7904]
        accs = aF[0:8, 3200:3712]
        gsb_b = aF[0:8, 2960:2984]
        o32 = aF[0:32, 2984:3113]
        on32 = aF[0:32, 3113:3177]
        r32 = aF[0:32, 3177:3178]
        if l == 1:
            P.copy("dve", idxf, idx[:])
            P.ts("dve", idxf, idxf, float(npool * 128), None, ALU.add)
            P.copy("dve", idx[:], idxf)
        P.memset("pool", slcc[:, :, 256:260], 1.0)
        P.memset("pool", winc[:, :, 256:260], 1.0)
        P.memset("pool", negT_s[32:64], 0.0)
        P.memset("pool", negT_s[64:128], 0.0)
        for r in range(4):
            P.copy("pool", ac32[:, r, :], acm[:, 0:8])
        for b in range(16):
            tk = slice(TP + b * 8, TP + b * 8 + 8)
            def gather(dst_ap, srcd, col):
                def f(e):
                    return e.indirect_dma_start(out=dst_ap, out_offset=None, in_=srcd.rearrange("l r c -> (l r) c"),
                                                in_offset=bass.IndirectOffsetOnAxis(idx[:, col:col + 1], 0))
                P.add("pool", f, reads=[idx[:, col:col + 1]], writes=[dst_ap], is_dma=True)
            for j in range(16):
                gather(slcc[:, j, 0:256], cs, b * 16 + j)
            P.dma("pool", winc[:, :, 0:256], swin[l, b].rearrange("(j p) c -> p j c", p=128))
            P.dma("sp", Vnew, Vs16[b * 8:(b + 1) * 8])
            P.dma("sp", szab, sza[b * 8:(b + 1) * 8, 16, :])
            P.dma("sp", gsb_b, gs[b * 8:(b + 1) * 8, 16, :])
            for qtr in range(4):
                cq = cmpq[qtr % 2]
                for jj in range(4):
                    gather(cq[:, jj, :], cc, b * 16 + qtr * 4 + jj)
                for jj in range(4):
                    P.transpose(psT[:, jj * 128:(jj + 1) * 128], cq[:, jj, 0:128], ident_b[:])
                    P.transpose(psT[:, (4 + jj) * 128:(5 + jj) * 128], cq[:, jj, 128:256], ident_b[:])
                P.copy(alt(), KcT_s[:, qtr * 512:(qtr + 1) * 512], psT[:, 0:512])
                P.copy(alt(), VcT_s[:, qtr * 512:(qtr + 1) * 512], psT[:, 512:1024])
            for (src_t, c0, dstT, nj) in ((slcc, 0, KsT, 16), (winc, 0, KwT, 4)):
                for j0 in range(0, nj, 8):
                    n8 = min(8, nj - j0)
                    for j in range(j0, j0 + n8):
                        P.transpose(psT[:, (j - j0) * 128:(j - j0 + 1) * 128], src_t[:, j, c0:c0 + 128], ident_b[:])
                    P.copy(alt(), dstT[:, j0 * 128:(j0 + n8) * 128], psT[:, 0:n8 * 128])
            yield
            cmp_tiles[:] = [kcT_s, vcaug_s, hid_s]
            ecmp_ref[0] = ecmp_s
            compress(KcT_s, VcT_s)
            for g in range(2):
                cmp_branch(8, 127, g,
                           lambda r, g=g: qT[g * 64:(g + 1) * 64, r, tk],
                           zpad[:, 15:142],
                           lambda h: bands[:, h, :],
                           gsb_b[:, g * 4:g * 4 + 4],
                           lambda h: accs[:, h * 64:(h + 1) * 64],
                           m1s[:], c1s[:],
                           negT_s[:, g, 0, :])
                for r in range(1, 4):
                    P.copy("pool", negT_s[0:33, g, r, :], negT_s[0:33, g, 0, :])
            yield
            for g in range(2):
                gb = g * 64
                q32 = qT[gb:gb + 64, 0:4, tk]
                for br in (1, 2):
                    pss = rr()
                    KpT = KsT if br == 1 else KwT
                    cch = slcc if br == 1 else winc
                    nj = 16 if br == 1 else 4
                    for j in range(nj):
                        cs_ = slice(j * 32, (j + 1) * 32)
                        extra = []
                        if br == 1:
                            extra.append((expand[:, j * 128:(j + 1) * 128], negT_s[:, g, :, :]))
                        if br == 2 and j == 0:
                            extra.append((ident_b[:], ac32))
                        if j == nj - 1:
                            extra.append((ident_b[:], strip[:, g * 4:(g + 1) * 4, 128:136]))
                        P.matmul(pss[:, cs_], KpT[gb:gb + 64, j * 128:(j + 1) * 128], q32, start=True, stop=(len(extra) == 0))
                        for i, (lt, rh) in enumerate(extra):
                            P.matmul(pss[:, cs_], lt, rh, start=False, stop=(i == len(extra) - 1))
                    P.matmul(psG[0:8, 0:32], KT[gb:gb + 64, br, tk], q32, start=True, stop=False)
                    P.matmul(psG[0:8, 0:32], ident_b[:, 0:8], strip[:, g * 4:(g + 1) * 4, 0:8], start=False, stop=True)
                    P.act(e_s[:, 0:nj * 32], pss[:, 0:nj * 32], AF.Exp)
                    P.act(e_s[0:8, 512:544], psG[0:8, 0:32], AF.Exp)
                    pso = psD if br == 1 else psE
                    for j in range(nj):
                        P.matmul(pso[0:32, 0:129], e_s[:, j * 32:(j + 1) * 32], cch[:, j, 128:257], start=(j == 0), stop=False)
                    P.matmul(pso[0:32, 0:129], e_s[0:8, 512:544], Vnew[:, br, 0:129], start=False, stop=True)
                    P.copy("act", o32, pso[0:32, 0:129])
                    P.ts("dve", r32, o32[:, 128:129], 1e-30, None, ALU.add)
                    recip(r32, r32)
                    P.ts("dve", on32, o32[:, g * 64:(g + 1) * 64], r32[:, 0:1], None, ALU.mult)
                    for r in range(4):
                        P.matmul(psF[0:8, r * 64:(r + 1) * 64], ident_f[0:32, r * 8:(r + 1) * 8], on32, start=True, stop=True)
                    for r in range(4):
                        h = g * 4 + r
                        a = accs[:, h * 64:(h + 1) * 64]
                        P.stt(a, psF[0:8, r * 64:(r + 1) * 64], gsb_b[:, br * 8 + h:br * 8 + h + 1], a, ALU.mult, ALU.add)
            P.tt("dve", attn_s, accs, szab, ALU.mult)
            for c in range(4):
                P.transpose(psT[:, c * 8:(c + 1) * 8], attn_s[:, c * 128:(c + 1) * 128], ident_b[0:8, 0:8])
            P.copy("act", mixa[:, 0:4, tk], psT[:, 0:32].rearrange("p (c n) -> p c n", c=4))
            yield

    k.phase1 = phase1
    k.__dict__.update(locals())
    return k


def rel_bucket_np(dist):
    n = np.maximum(dist, 0)
    nf = np.maximum(n, 1).astype(np.float32)
    large = 16 + (np.log(nf / np.float32(16)) / np.float32(np.log(128 / 16)) * np.float32(16)).astype(np.int32)
    large = np.minimum(large, 31)
    return np.where(n < 16, n, large)


def make_consts():
    bf = ml_dtypes.bfloat16
    c = {}
    c["c_ident"] = np.eye(128, dtype=np.float32)
    d = np.arange(-127, 256)
    oh = np.zeros((33, 383), np.float32)
    b = rel_bucket_np(d)
    for i, dd in enumerate(d):
        if dd < 0:
            oh[32, i] = 1.0
        else:
            oh[b[i], i] = 1.0
    c["c_oh"] = oh
    pos = np.arange(T)
    ex = np.zeros((33, T), np.float32)
    ex[np.minimum(pos // 64, 32), pos] = 1.0
    c["c_expand"] = ex.astype(bf)
    cidx = np.arange(127)
    c_start = cidx * 16
    c_end = c_start + 31
    s_start = np.arange(33) * 64
    cover = ((c_start[:, None] < s_start[None, :] + 64) & (c_end[:, None] >= s_start[None, :])).astype(np.float32)
    c["c_cover"] = np.concatenate([cover, np.ones((127, 1), np.float32)], axis=1).astype(bf)
    zp = np.zeros((16, 270), np.float32)
    zp[np.arange(16), 127 + np.arange(16)] = 1.0
    c["c_zpad"] = zp.astype(bf)
    qpos = np.arange(2048)
    cur = qpos // 64
    blk = np.arange(33)
    forced = (blk[None] == 0) | (blk[None] == cur[:, None]) | (blk[None] == cur[:, None] - 1)
    allowed = blk[None] <= cur[:, None]
    m1 = (allowed & ~forced).astype(np.float32)
    c1 = np.where(allowed, np.where(forced, 1e4, 0.0), -1e4).astype(np.float32)
    c["c_m1p"] = m1.reshape(16, 128, 33).transpose(1, 0, 2).reshape(128, 16 * 33).copy()
    c["c_c1p"] = c1.reshape(16, 128, 33).transpose(1, 0, 2).reshape(128, 16 * 33).copy()
    qs = 2048 + np.arange(8)
    curs = qs // 64
    forced = (blk[None] == 0) | (blk[None] == curs[:, None]) | (blk[None] == curs[:, None] - 1)
    allowed = blk[None] <= curs[:, None]
    c["c_m1s"] = (allowed & ~forced).astype(np.float32)
    c["c_c1s"] = np.where(allowed, np.where(forced, 1e4, 0.0), -1e4).astype(np.float32)
    c["c_iota"] = np.arange(128, dtype=np.float32).reshape(128, 1)
    kl = np.arange(128)
    c["c_ac"] = np.where(kl[None, :] >= kl[:, None], -BIG, 0.0).astype(np.float32).astype(bf)
    return c


def core_inputs(inp, c, npool_rows=None):
    m = {}
    m["x_tok"] = np.concatenate([inp["x_prompt"][c], inp["x_sample"][16 * c:16 * c + 16].reshape(128, D)], 0)
    m["ple_tok"] = np.concatenate([inp["p_prompt"][:, c], inp["p_sample"][:, 16 * c:16 * c + 16].reshape(2, 128, 256)], 1)
    m["cc"] = inp["cache_cmp_kv"].reshape(2, -1, 256)
    m["cs"] = inp["cache_slc_kv"].reshape(2, -1, 256)
    m["ptab"] = inp["page_table"][16 * c:16 * c + 16].reshape(1, 256).astype(np.int32)
    m["swin"] = inp["state_win_kv"][:, 16 * c:16 * c + 16].reshape(2, 16, 512, 256)
    m["sconv"] = inp["state_conv"][:, 16 * c:16 * c + 16].reshape(2, 480, 512)
    m["norm_g"] = inp["norm_g"]
    m["w_in"] = inp["w_in"]
    m["convp"] = np.concatenate([inp["conv_w"], inp["conv_b"][:, None], inp["conv_ln_g"][:, None],
                                 inp["conv_ln_b"][:, None]], 1)
    m["cmp_pe"] = inp["cmp_pe"]
    m["cmp_w1"] = inp["cmp_w1"].reshape(2, 2, 2048, 128)
    m["cmp_w2"] = inp["cmp_w2"]
    m["w_out"] = inp["w_out"]
    m["w_ple"] = inp["w_ple"]
    m["w_pg"] = inp["w_ple_gate"]
    m["rel_bias"] = inp["rel_bias"]
    m["fng"] = inp["final_norm_g"].reshape(1, D)
    return {k_: np.ascontiguousarray(v) for k_, v in m.items()}


STAGE = 7


def program(k, stage=99):
    srcs = [k.x_tok, k.h1]
    for l in range(2):
        k.phase1(l, srcs[l])
        k.phase2a(l)
        k.phase3(l)
        k.phase2b(l)
        if stage >= 6:
            if l == 0:
                k.setup_attn()
            k.load_cmp_weights(l)
            gp = k.attention_prompt(l)
            if stage >= 7:
                gsm = k.attention_sample(l)
                next(gp)
                alive = [gp, gsm]
                ratio = [3, 2]
                while alive:
                    for gi, gen in enumerate(list(alive)):
                        for _ in range(ratio[0] if gen is gp else ratio[1]):
                            try:
                                next(gen)
                            except StopIteration:
                                alive.remove(gen)
                                break
            else:
                for _ in gp:
                    pass
                k.P.memset("pool", k.uT[:, 0:4, 2048:2176], 0.0)
        else:
            k.P.memset("pool", k.uT[:, 0:4, :], 0.0)
        k.phase5(l, srcs[l], l == 1)


_CACHE = {}


def kernel(**inp):
    npool = inp["cache_cmp_kv"].shape[1]
    if "nc" not in _CACHE:
        nc = bass.Bass("TRN2", target_bir_lowering=False)
        k = build(nc, npool, stage=STAGE)
        program(k, STAGE)
        k.P.emit()
        _CACHE["nc"] = nc
    nc = _CACHE["nc"]
    consts = make_consts()
    in_maps = []
    for c in range(8):
        m = core_inputs(inp, c)
        if STAGE < 6:
            m.pop("cc")
            m.pop("cs")
        m.update(consts)
        in_maps.append(m)
    res = run_bass_kernel_spmd(nc, in_maps, core_ids=list(range(8)))
    r = res.results
    y_p = np.stack([r[c]["y"][:TP] for c in range(8)])
    y_s = np.concatenate([r[c]["y"][TP:].reshape(16, 8, D) for c in range(8)])

    def kvp(name):
        return np.stack([r[c][name][:, :TP] for c in range(8)], 1).reshape(2, 8, TP, 2, 2, 64)

    def kvs(name):
        return np.concatenate([r[c][name][:, TP:].reshape(2, 16, 8, 256) for c in range(8)], 1).reshape(2, 128, 8, 2, 2, 64)
    win_p = np.stack([r[c]["nwin_p"] for c in range(8)], 1).reshape(2, 8, 512, 2, 2, 64)
    win_s = np.concatenate([r[c]["nwin_s"] for c in range(8)], 1).reshape(2, 128, 512, 2, 2, 64)
    conv_p = np.stack([r[c]["nconv_p"] for c in range(8)], 1)
    conv_s = np.concatenate([r[c]["nconv_s"] for c in range(8)], 1)
    f = lambda a: np.ascontiguousarray(a, dtype=np.float32)
    return (f(y_p), f(y_s), f(kvp("ncmp")), f(kvs("ncmp")), f(kvp("nslc")), f(kvs("nslc")),
            f(win_p), f(win_s), f(conv_p), f(conv_s))
```

```python
import contextlib
import numpy as np
import ml_dtypes
import concourse.bass as bass
import concourse.mybir as mybir
from concourse.bass_utils import run_bass_kernel_spmd

F32 = mybir.dt.float32
BF16 = mybir.dt.bfloat16
I32 = mybir.dt.int32
ALU = mybir.AluOpType
AF = mybir.ActivationFunctionType

ENGS = ("pe", "act", "dve", "pool", "sp")
NSLOT = {"sp": 28, "pool": 16, "act": 4}
BIG = 30000.0


def _dsize(dt):
    return mybir.dt.size(dt)


def _region(ap):
    t = ap.tensor
    space = str(ap.space).upper()
    name = t.name
    if "DRAM" in space or "HBM" in space:
        f0 = ap.offset
        ext = 0
        for st_, cnt in ap.ap:
            ext += (cnt - 1) * abs(st_)
        return ("D", name, 0, 1, f0, f0 + ext + 1)
    shp = list(t.shape)
    pstride = 1
    for s in shp[1:]:
        pstride *= s
    off = ap.offset
    p0 = off // pstride
    f0 = off % pstride
    aps = list(ap.ap)
    pcount = 1
    ext = 0
    for i, (st, cnt) in enumerate(aps):
        if i == 0:
            pcount = cnt if st == pstride or cnt == 1 else cnt
            if st != pstride and cnt > 1 and st != 0:
                pcount = 1
                ext += (cnt - 1) * abs(st)
            continue
        ext += (cnt - 1) * abs(st)
    f1 = f0 + ext + 1
    if "PSUM" in space:
        per_bank = 2048 // _dsize(t.dtype)
        f0 = (f0 // per_bank) * per_bank
        f1 = ((f1 + per_bank - 1) // per_bank) * per_bank
        return ("P", name, 0, 128, f0, f1)
    return ("S", name, p0, p0 + pcount, f0, f1)


def _overlap(a, b):
    return a[1] == b[1] and a[2] < b[3] and b[2] < a[3] and a[4] < b[5] and b[4] < a[5]


def _covers(a, b):
    return a[1] == b[1] and a[2] <= b[2] and a[3] >= b[3] and a[4] <= b[4] and a[5] >= b[5]


class Op:
    __slots__ = ("eng", "fn", "idx", "eidx", "deps", "is_dma", "dma_k", "signal", "waits", "sigval")


class Prog:
    def __init__(self, nc):
        self.nc = nc
        self.ops = []
        self.eng_ops = {e: [] for e in ENGS}
        self.track = {}
        self.dma_count = {e: 0 for e in ENGS}

    def add(self, eng, fn, reads=(), writes=(), is_dma=False):
        op = Op()
        op.eng = eng
        op.fn = fn
        op.idx = len(self.ops)
        op.eidx = len(self.eng_ops[eng])
        op.deps = set()
        op.is_dma = is_dma
        op.dma_k = None
        if is_dma:
            op.dma_k = self.dma_count[eng]
            self.dma_count[eng] += 1
        op.signal = False
        self.ops.append(op)
        self.eng_ops[eng].append(op)
        for a in reads:
            if a is None or isinstance(a, (int, float)):
                continue
            r = _region(a)
            self._access(op, r, r[0] == "P")
        for a in writes:
            if a is None:
                continue
            self._access(op, _region(a), True)
        return op

    def _access(self, op, reg, is_write):
        lst = self.track.get(reg[1])
        if lst is None:
            self.track[reg[1]] = [[reg, op, is_write]]
            return
        new = []
        for ent in lst:
            ereg, eop, ew = ent
            if eop is op:
                new.append(ent)
                continue
            if _overlap(ereg, reg):
                if is_write or ew:
                    op.deps.add(eop)
                    if is_write and _covers(reg, ereg):
                        continue
                    new.append(ent)
                else:
                    if (not eop.is_dma) and (not op.is_dma) and eop.eng == op.eng and ereg == reg:
                        continue
                    new.append(ent)
            else:
                new.append(ent)
        new.append([reg, op, is_write])
        self.track[reg[1]] = new

    def matmul(self, out, lhsT, rhs, start=True, stop=True, **kw):
        return self.add("pe", lambda e: e.matmul(out, lhsT, rhs, start=start, stop=stop, **kw),
                        reads=[lhsT, rhs], writes=[out])

    def transpose(self, out, in_, ident):
        return self.add("pe", lambda e: e.transpose(out, in_, ident), reads=[in_, ident], writes=[out])

    def act(self, out, in_, func, bias=None, scale=None, accum_out=None):
        kw = {}
        rd = [in_]
        if bias is not None:
            kw["bias"] = bias
            rd.append(bias)
        if scale is not None:
            kw["scale"] = scale
            rd.append(scale)
        wr = [out]
        if accum_out is not None:
            kw["accum_out"] = accum_out
            wr.append(accum_out)
        return self.add("act", lambda e: e.activation(out, in_, func, **kw), reads=rd, writes=wr)

    def tt(self, eng, out, in0, in1, op):
        return self.add(eng, lambda e: e.tensor_tensor(out, in0, in1, op), reads=[in0, in1], writes=[out])

    def ts(self, eng, out, in0, s1, s2, op0, op1=None):
        kw = {}
        if op1 is not None:
            kw["op1"] = op1
        return self.add(eng, lambda e: e.tensor_scalar(out, in0, s1, s2, op0, **kw),
                        reads=[in0, s1, s2], writes=[out])

    def stt(self, out, in0, scalar, in1, op0, op1):
        return self.add("dve", lambda e: e.scalar_tensor_tensor(out, in0, scalar, in1, op0, op1),
                        reads=[in0, in1, scalar], writes=[out])

    def copy(self, eng, out, in_):
        if eng == "act":
            return self.add(eng, lambda e: e.copy(out, in_), reads=[in_], writes=[out])
        return self.add(eng, lambda e: e.tensor_copy(out, in_), reads=[in_], writes=[out])

    def memset(self, eng, out, val):
        return self.add(eng, lambda e: e.memset(out, val), reads=[], writes=[out])

    def dma(self, queue, out, in_, **kw):
        return self.add(queue, lambda e: e.dma_start(out=out, in_=in_, **kw),
                        reads=[in_], writes=[out], is_dma=True)

    def emit(self):
        nc = self.nc
        waited = {f: {e: -1 for e in ENGS} for f in ENGS}
        for op in self.ops:
            op.waits = []
        for op in self.ops:
            f = op.eng
            need = {}
            for d in op.deps:
                if d.is_dma:
                    op.waits.append(d)
                    continue
                e = d.eng
                if e == f and not op.is_dma:
                    if e in ("pe", "sp"):
                        continue
                    if op.eidx - d.eidx > 3:
                        continue
                if d.eidx > waited[f][e]:
                    if e not in need or need[e].eidx < d.eidx:
                        need[e] = d
            for e, d in need.items():
                waited[f][e] = d.eidx
                d.signal = True
                op.waits.append(d)
        for e in ENGS:
            c = 0
            for op in self.eng_ops[e]:
                if op.is_dma:
                    continue
                if op.signal:
                    c += 1
                    op.sigval = c
        dma_waited = {f: set() for f in ENGS}
        with contextlib.ExitStack() as st:
            sems = {e: st.enter_context(nc.semaphore("sem_" + e)) for e in ENGS}
            dsems = {}
            for q, n in NSLOT.items():
                if self.dma_count[q] > 0:
                    dsems[q] = [st.enter_context(nc.semaphore("dsem_%s_%d" % (q, i))) for i in range(n)]
            block = st.enter_context(nc.Block())

            def dma_slot(d):
                n = NSLOT[d.eng]
                return dsems[d.eng][d.dma_k % n], 16 * (d.dma_k // n + 1)

            def run_engine(ename, eng):
                final_wait = {}
                for op in self.eng_ops[ename]:
                    for d in op.waits:
                        if d.is_dma:
                            if d in dma_waited[ename]:
                                continue
                            dma_waited[ename].add(d)
                            s, v = dma_slot(d)
                            eng.wait_ge(s, v)
                        else:
                            eng.wait_ge(sems[d.eng], d.sigval)
                    if op.is_dma:
                        n = NSLOT[ename]
                        if op.dma_k >= n:
                            eng.wait_ge(dsems[ename][op.dma_k % n], 16 * (op.dma_k // n))
                        ins = op.fn(eng)
                        s, v = dma_slot(op)
                        ins.then_inc(s, 16)
                        final_wait[op.dma_k % n] = (s, v)
                    else:
                        ins = op.fn(eng)
                        if op.signal:
                            ins.then_inc(sems[ename], 1)
                for k, (s, v) in final_wait.items():
                    eng.wait_ge(s, v)

            @block.sync
            def _(eng):
                run_engine("sp", eng)

            @block.tensor
            def _(eng):
                run_engine("pe", eng)

            @block.scalar
            def _(eng):
                run_engine("act", eng)

            @block.vector
            def _(eng):
                run_engine("dve", eng)

            @block.gpsimd
            def _(eng):
                run_engine("pool", eng)


T = 2176
TP = 2048
NT = 17
D = 1024
DIN = 3352
EPS = 1e-6
NB5 = [(0, 512), (512, 512), (1024, 512), (1536, 512), (2048, 128)]


class K:
    pass


def build(nc, npool, nlayers=2, stage=99):
    k = K()
    k.nc = nc
    P = Prog(nc)
    k.P = P
    st = contextlib.ExitStack()
    k.st = st

    def din(name, shape, dt=F32):
        return nc.dram_tensor(name, list(shape), dt, kind="ExternalInput").ap()

    def dout(name, shape, dt=F32):
        return nc.dram_tensor(name, list(shape), dt, kind="ExternalOutput").ap()

    def dscr(name, shape, dt=F32):
        return nc.dram_tensor(name, list(shape), dt, kind="Internal").ap()

    def sb(name, shape, dt):
        return st.enter_context(nc.sbuf_tensor(name, list(shape), dt))

    def ps(name, shape, dt=F32):
        return st.enter_context(nc.psum_tensor(name, list(shape), dt))

    x_tok = din("x_tok", [T, D])
    ple_tok = din("ple_tok", [2, T, 256])
    if stage >= 6:
        cc = din("cc", [2, npool * 128, 256])
        cs = din("cs", [2, npool * 128, 256])
    ptab = din("ptab", [1, 256], I32)
    swin = din("swin", [2, 16, 512, 256])
    sconv = din("sconv", [2, 480, 512])
    norm_g = din("norm_g", [2, D])
    w_in = din("w_in", [2, D, DIN])
    convp = din("convp", [2, 34, 512])
    cmp_pe = din("cmp_pe", [2, 2, 32, 64])
    cmp_w1 = din("cmp_w1", [2, 2, 2048, 128])
    cmp_w2 = din("cmp_w2", [2, 2, 128, 64])
    w_out = din("w_out", [2, D, D])
    w_ple = din("w_ple", [2, 256, D])
    w_pg = din("w_pg", [2, D, D])
    rel_bias = din("rel_bias", [32, 8])
    fng = din("fng", [1, D])
    c_ident = din("c_ident", [128, 128])
    c_oh = din("c_oh", [33, 383])
    c_expand = din("c_expand", [33, T], BF16)
    c_cover = din("c_cover", [127, 34], BF16)
    c_zpad = din("c_zpad", [16, 270], BF16)
    c_m1p = din("c_m1p", [128, 16 * 33])
    c_c1p = din("c_c1p", [128, 16 * 33])
    c_m1s = din("c_m1s", [8, 33])
    c_c1s = din("c_c1s", [8, 33])
    c_iota = din("c_iota", [128, 1])
    c_ac = din("c_ac", [128, 128], BF16)

    y = dout("y", [T, D])
    ncmp = dout("ncmp", [2, T, 256])
    nslc = dout("nslc", [2, T, 256])
    nwin_p = dout("nwin_p", [2, 512, 256])
    nwin_s = dout("nwin_s", [2, 16, 512, 256])
    nconv_p = dout("nconv_p", [2, 30, 512])
    nconv_s = dout("nconv_s", [2, 16, 30, 512])
    h1 = dscr("h1", [T, D])
    G1 = dscr("G1", [8, 128, 512])
    G16 = dscr("G16", [8, 16, 640])

    ident_f = sb("ident_f", [128, 128], F32)
    ident_b = sb("ident_b", [128, 128], BF16)
    ones_f = sb("ones_f", [128, 128], F32)
    epsb = sb("epsb", [128, 1], F32)
    uT = sb("uT", [128, 8, T], BF16)
    WA = sb("WA", [128, 8, 1024], BF16)
    WB = sb("WB", [128, 8, 1024], BF16)
    bigG = sb("bigG", [128, 4, T], BF16)
    bigZ = sb("bigZ", [128, 4 * T], BF16)
    mixc = sb("mixc", [128, 4, T], BF16)
    KT = sb("KT", [128, 3, T], BF16)
    Vt = sb("Vt", [128, 3, 17, 130], BF16)
    gs = sb("gs", [128, 17, 24], F32)
    aF = sb("aF", [128, 7168], F32)
    aB = sb("aB", [128, 8192], BF16)
    ssq = sb("ssq", [128, NT], F32)
    rs = sb("rs", [128, NT], F32)
    cw = sb("cw", [128, 4, 34], F32)
    gluP = bigG
    gluS = KT[:].rearrange("p a t -> p (a t)")[:, 0:2432].rearrange("p (c b r) -> p c b r", c=4, b=16)
    WC = bigG[:, 0, 0:2048].rearrange("p (c n) -> p c n", c=2)
    VcT = aB[:, 0:2048]
    cwraw = aF[0:34, 6656:7168]
    zc = bigZ[:].rearrange("p (c t) -> p c t", c=4)
    sza = bigZ[:].rearrange("p (t e) -> p t e", e=512)
    qT = bigG
    xin = [aF[:, 0:1024], aF[:, 1024:2048]]
    gbc = aF[:, 2048:3072]
    sg = [aF[:, 3072:3584], aF[:, 3584:4096]]
    glu_tail = aF[:, 4096:5120].rearrange("p (c n) -> p c n", c=4)
    stage_t = [aF[:, 5120:5888], aF[:, 5888:6656]]
    dw = aF[:, 0:2048].rearrange("p (c n) -> p c n", c=4)
    sq = aF[:, 2048:2560]
    st_m = aF[:, 2560:3072]
    st_v = aF[:, 3072:3584]
    st_t = aF[:, 3584:4096]
    scraw = aF[:, 5120:5632]
    tailtok = aF[:, 5632:6656].rearrange("p (c n) -> p c n", c=2)
    Dk = aB[:, 0:3968].rearrange("p (k n) -> p k n", k=31)
    scb = aB[:, 3968:6016].rearrange("p (c n) -> p c n", c=4)
    ub = aB[:, 6016:7040]
    junk = aB[:, 7040:8064]

    psA = ps("psA", [128, 512])
    psB = ps("psB", [128, 512])
    psC = ps("psC", [128, 512])
    psT = ps("psT", [128, 1024], BF16)
    psD = ps("psD", [128, 512])
    psE = ps("psE", [128, 512])
    psF = ps("psF", [128, 512])
    psT2 = ps("psT2", [128, 1024], BF16)
    psG = psF
    rot3 = [psA, psB, psC]
    tcnt = [0]

    def ptb():
        tcnt[0] += 1
        return [psT, psT2][tcnt[0] % 2]

    P.dma("sp", ident_f[:], c_ident)
    P.copy("dve", ident_b[:], ident_f[:])
    P.memset("dve", ones_f[:], 1.0)
    P.memset("dve", epsb[:], EPS)
    P.memset("pool", Vt[:], 1.0)

    wrow = lambda w, l: w[l].rearrange("(c p) n -> p c n", p=128)
    cnt = [0]

    def rr():
        cnt[0] += 1
        return rot3[cnt[0] % 3]

    def alt():
        cnt[0] += 1
        return "act" if cnt[0] % 2 else "dve"

    def recip(out, in_):
        return P.add("dve", lambda e: e.reciprocal(out, in_), reads=[in_], writes=[out])

    def phase1(l, src):
        P.dma("sp", gbc[:], norm_g[l:l + 1, :].broadcast_to([128, D]))
        for t in range(NT):
            xt = xin[t % 2]
            P.dma("sp", xt, src[t * 128:(t + 1) * 128, :])
            P.act(junk[:], xt[:], AF.Square, accum_out=ssq[:, t:t + 1])
            P.act(rs[:, t:t + 1], ssq[:, t:t + 1], AF.Sqrt, bias=epsb[:, 0:1], scale=1.0 / D)
            recip(rs[:, t:t + 1], rs[:, t:t + 1])
            P.stt(ub[:], xt[:], rs[:, t:t + 1], gbc[:], ALU.mult, ALU.mult)
            pT = ptb()
            for c in range(8):
                P.transpose(pT[:, c * 128:(c + 1) * 128], ub[:, c * 128:(c + 1) * 128], ident_b[:])
            P.copy(alt(), uT[:, :, t * 128:(t + 1) * 128], pT[:].rearrange("p (c n) -> p c n", c=8))

    def fm_mm(pst, lhs_of_c, nb, m=128):
        n0, nn = NB5[nb]
        for c in range(8):
            P.matmul(pst[0:m, 0:nn], lhs_of_c(c), uT[:, c, n0:n0 + nn], start=(c == 0), stop=(c == 7))

    def phase2a(l):
        wl = wrow(w_in, l)
        P.memset("pool", gluP[:, :, 0:30], 0.0)
        P.dma("pool", WB[:, :, 0:512], wl[:, :, 0:512])
        P.dma("pool", WB[:, :, 512:1024], wl[:, :, 512:1024])
        for ch in range(4):
            for nb in range(5):
                n0, nn = NB5[nb]
                fm_mm(psA, lambda c: WB[:, c, ch * 128:(ch + 1) * 128], nb)
                fm_mm(psB, lambda c: WB[:, c, 512 + ch * 128:512 + (ch + 1) * 128], nb)
                s = sg[nb % 2]
                P.act(s[:, 0:nn], psB[:, 0:nn], AF.Sigmoid)
                if nb < 4:
                    P.tt("dve", gluP[:, ch, 30 + n0:30 + n0 + nn], psA[:, 0:nn], s[:, 0:nn], ALU.mult)
                    if nb == 3:
                        P.tt("dve", glu_tail[:, ch, 0:128], psA[:, 384:512], s[:, 384:512], ALU.mult)
                else:
                    P.tt("dve", gluS[:, ch, :, 30:38], psA[:, 0:128].rearrange("p (b q) -> p b q", q=8),
                         s[:, 0:128].rearrange("p (b q) -> p b q", q=8), ALU.mult)
                    P.tt("dve", glu_tail[:, ch, 128:256], psA[:, 0:128], s[:, 0:128], ALU.mult)
        P.dma("pool", WB[:, :, 0:512], wl[:, :, 1024:1536])
        for ch in range(4):
            for nb in range(5):
                n0, nn = NB5[nb]
                pst = rr()
                fm_mm(pst, lambda c: WB[:, c, ch * 128:(ch + 1) * 128], nb)
                P.act(zc[:, ch, n0:n0 + nn], pst[:, 0:nn], AF.Silu)

    def tm_group(l, half, ncols, evac):
        for t in range(17):
            sl = slice(t * 128, (t + 1) * 128)
            pst = rr()
            for c in range(8):
                P.matmul(pst[:, 0:ncols], uT[:, c, sl], WB[:, c, half * 512:half * 512 + ncols],
                         start=(c == 0), stop=(c == 7))
            evac(t, sl, pst)

    def phase2b(l):
        wl = wrow(w_in, l)
        P.memset("pool", Vs16[:, :, 128:132], 1.0)
        for two in range(2):
            for c in range(8):
                P.dma("pool", WB[:, c, 512:1024].rearrange("p (j two d) -> p j two d", two=2, d=64)[:, :, two, :],
                      wl[:, c, 1536 + two * 256:1536 + two * 256 + 256].rearrange("p (j d) -> p j d", d=64))
        for j in range(4):
            for nb in range(5):
                n0, nn = NB5[nb]
                pst = rr()
                fm_mm(pst, lambda c: WB[:, c, 512 + j * 128:512 + (j + 1) * 128], nb)
                P.act(qT[:, j, n0:n0 + nn], pst[:, 0:nn], AF.Copy, scale=0.125)
        outs = [ncmp, nslc, None]
        for br in range(3):
            half = br % 2
            c0 = 2048 + br * 256
            P.dma("pool", WB[:, :, half * 512:half * 512 + 256], wl[:, :, c0:c0 + 256])
            if br == 0:
                P.dma("pool", WB[:, :, 256:280], wl[:, :, 3328:3352])
            for nb in range(5):
                n0, nn = NB5[nb]
                pst = rr()
                fm_mm(pst, lambda c: WB[:, c, half * 512:half * 512 + 128], nb)
                P.copy(alt(), KT[:, br, n0:n0 + nn], pst[:, 0:nn])
            if br == 0:
                for nb in range(4):
                    n0, nn = NB5[nb]
                    pst = rr()
                    fm_mm(pst, lambda c: WB[:, c, 128:256], nb)
                    P.copy(alt(), VcT[:, n0:n0 + nn], pst[:, 0:nn])

            def evac(t, sl, pst, br=br):
                stg = stage_t[t % 2]
                P.copy("act", stg[:, 0:256], pst[:, 0:256])
                P.copy("pool", Vt[:, br, t, :].rearrange("p (g e) -> p g e", g=2)[:, :, 0:64],
                       stg[:, 128:256].rearrange("p (g d) -> p g d", g=2))
                if t == 16:
                    P.copy("pool", Vs16[:, br, 0:128], stg[:, 128:256])
                if br == 0:
                    P.act(gs[:, t, :], pst[:, 256:280], AF.Sigmoid)
                if br < 2:
                    P.dma("sp", outs[br][l, sl, :], stg[:, 0:256])
                else:
                    if 12 <= t < 16:
                        P.dma("sp", nwin_p[l, (t - 12) * 128:(t - 11) * 128, :], stg[:, 0:256])
                    if t == 16:
                        for b in range(16):
                            P.dma("sp", nwin_s[l, b, 504:512, :], stg[b * 8:(b + 1) * 8, 0:256])
            tm_group(l, half, 280 if br == 0 else 256, evac)
        if stage != 41:
            P.dma("sp", nwin_s[l][:, 0:504, :], swin[l][:, 8:512, :])
        P.dma("pool", WB[:, :, 512:1024], wl[:, :, 2816:3328])
        tm_group(l, 1, 512, lambda t, sl, pst: P.act(sza[:, t, :], pst[:, 0:512], AF.Silu))

    ones_b = sb("ones_b", [128, 128], BF16)
    P.memset("pool", ones_b[:], 1.0)

    def phase3(l):
        for ch in range(4):
            for hlf in range(2):
                P.transpose(psD[:, (hlf * 4 + ch) % 4 * 128:((hlf * 4 + ch) % 4 + 1) * 128],
                            glu_tail[:, ch, hlf * 128:(hlf + 1) * 128], ident_f[:])
                P.copy("dve", tailtok[:, hlf, ch * 128:(ch + 1) * 128],
                       psD[:, (hlf * 4 + ch) % 4 * 128:((hlf * 4 + ch) % 4 + 1) * 128])
        P.dma("sp", nconv_p[l], tailtok[98:128, 0, :])
        for b in range(16):
            P.dma("sp", nconv_s[l, b, 22:30, :], tailtok[b * 8:(b + 1) * 8, 1, :])
        P.dma("sp", nconv_s[l][:, 0:22, :], sconv[l].rearrange("(b r) c -> b r c", r=30)[:, 8:30, :])
        P.dma("sp", cwraw, convp[l])
        for ch in range(4):
            P.transpose(psD[:, 0:34], cwraw[0:34, ch * 128:(ch + 1) * 128], ident_f[0:34, 0:34])
            P.copy("dve", cw[:, ch, :], psD[:, 0:34])
        for r in range(4):
            nr = 128 if r < 3 else 96
            P.dma("sp", scraw[0:nr, :], sconv[l, r * 128:r * 128 + nr, :])
            P.copy("pool", scb[0:nr, r, :], scraw[0:nr, :])
        for ch in range(4):
            pT = ptb()
            for r in range(4):
                nr = 128 if r < 3 else 96
                P.transpose(pT[:, r * 128:r * 128 + nr], scb[0:nr, r, ch * 128:(ch + 1) * 128], ident_b[0:nr, 0:nr])
            P.copy("dve", gluS[:, ch, :, 0:30], pT[:, 0:480].rearrange("p (b r) -> p b r", r=30))
        for ch in range(4):
            for kk in range(31):
                P.ts("dve" if kk % 2 else "pool", Dk[:, kk, :], ident_f[:], cw[:, ch, kk:kk + 1], None, ALU.mult)
            for nb in range(5):
                n0, nn = NB5[nb]
                pst = rr()
                for kk in range(31):
                    if nb < 4:
                        rhs = gluP[:, ch, n0 + kk:n0 + kk + 512]
                    else:
                        rhs = gluS[:, ch, :, kk:kk + 8]
                    P.matmul(pst[:, 0:nn], Dk[:, kk, :], rhs, start=(kk == 0), stop=(kk == 30))
                P.act(mixc[:, ch, n0:n0 + nn], pst[:, 0:nn], AF.Identity, bias=cw[:, ch, 31:32])
        for nb in range(5):
            n0, nn = NB5[nb]
            for ch in range(4):
                P.act(sq[:, 0:nn], mixc[:, ch, n0:n0 + nn], AF.Square)
                P.matmul(psD[:, 0:nn], ones_b[:], mixc[:, ch, n0:n0 + nn], start=(ch == 0), stop=(ch == 3))
                P.matmul(psE[:, 0:nn], ones_f[:], sq[:, 0:nn], start=(ch == 0), stop=(ch == 3))
            P.ts("dve", st_m[:, 0:nn], psD[:, 0:nn], 1.0 / 512, None, ALU.mult)
            P.tt("dve", st_v[:, 0:nn], st_m[:, 0:nn], st_m[:, 0:nn], ALU.mult)
            P.stt(st_v[:, 0:nn], psE[:, 0:nn], 1.0 / 512, st_v[:, 0:nn], ALU.mult, ALU.subtract)
            P.act(st_t[:, 0:nn], st_v[:, 0:nn], AF.Sqrt, bias=epsb[:, 0:1])
            recip(st_t[:, 0:nn], st_t[:, 0:nn])
            for ch in range(4):
                P.tt("dve", sq[:, 0:nn], mixc[:, ch, n0:n0 + nn], st_m[:, 0:nn], ALU.subtract)
                P.tt("dve", sq[:, 0:nn], sq[:, 0:nn], st_t[:, 0:nn], ALU.mult)
                P.act(sq[:, 0:nn], sq[:, 0:nn], AF.Silu, bias=cw[:, ch, 33:34], scale=cw[:, ch, 32:33])
                P.tt("dve", mixc[:, ch, n0:n0 + nn], sq[:, 0:nn], zc[:, ch, n0:n0 + nn], ALU.mult)

    mixa = uT

    def phase5(l, src, last):
        P.dma("pool", WA[:], wrow(w_out, l))
        P.dma("pool", WB[:], wrow(w_pg, l))
        P.dma("pool", WC, wrow(w_ple, l))
        if last:
            P.dma("sp", gbc, fng[0:1, :].broadcast_to([128, D]))
        hms = [aF[:, 3072:4096], aF[:, 5376:6400]]
        plts = [aF[:, 5120:5376], aF[:, 6400:6656]]
        plbs = [aB[:, 0:256], aB[:, 4096:4352]]
        plTs = [aB[:, 256:512].rearrange("p (c n) -> p c n", c=2), aB[:, 4352:4608].rearrange("p (c n) -> p c n", c=2)]
        hmTs = [aB[:, 512:1536].rearrange("p (c n) -> p c n", c=8), aB[:, 3072:4096].rearrange("p (c n) -> p c n", c=8)]
        ubs = [ub, aB[:, 2048:3072]]
        def stageA(t):
            sl = slice(t * 128, (t + 1) * 128)
            p2 = t % 2
            xt = xin[p2]
            hm, plt, plb, plT, hmT, ubp = hms[p2], plts[p2], plbs[p2], plTs[p2], hmTs[p2], ubs[p2]
            P.dma("sp", xt, src[sl, :])
            P.dma("sp", plt, ple_tok[l, sl, :])
            for hf in range(2):
                pst = [psA, psB][hf]
                for e in range(8):
                    lhs = mixc[:, e, sl] if e < 4 else mixa[:, e - 4, sl]
                    P.matmul(pst[:, 0:512], lhs, WA[:, e, hf * 512:(hf + 1) * 512], start=(e == 0), stop=(e == 7))
                P.tt("dve", hm[:, hf * 512:(hf + 1) * 512], pst[:, 0:512], xt[:, hf * 512:(hf + 1) * 512], ALU.add)
            P.copy("act", ubp, hm)
            P.copy("pool", plb, plt)
            pT = ptb()
            for c in range(8):
                P.transpose(pT[:, c * 128:(c + 1) * 128], ubp[:, c * 128:(c + 1) * 128], ident_b[:])
            P.copy("act", hmT, pT[:].rearrange("p (c n) -> p c n", c=8))
            pT = ptb()
            for c in range(2):
                P.transpose(pT[:, c * 128:(c + 1) * 128], plb[:, c * 128:(c + 1) * 128], ident_b[:])
            P.copy("act", plT, pT[:, 0:256].rearrange("p (c n) -> p c n", c=2))

        def stageB(t):
            sl = slice(t * 128, (t + 1) * 128)
            p2 = t % 2
            xt = xin[p2]
            hm, plt, plb, plT, hmT, ubp = hms[p2], plts[p2], plbs[p2], plTs[p2], hmTs[p2], ubs[p2]
            gt = xt
            for hf in range(2):
                pg = [psC, psD][hf]
                pp = [psE, psF][hf]
                for c in range(8):
                    P.matmul(pg[:, 0:512], hmT[:, c, :], WB[:, c, hf * 512:(hf + 1) * 512], start=(c == 0), stop=(c == 7))
                for c in range(2):
                    P.matmul(pp[:, 0:512], plT[:, c, :], WC[:, c, hf * 512:(hf + 1) * 512], start=(c == 0), stop=(c == 1))
                P.act(gt[:, hf * 512:(hf + 1) * 512], pg[:, 0:512], AF.Sigmoid)
                P.tt("dve", gt[:, hf * 512:(hf + 1) * 512], gt[:, hf * 512:(hf + 1) * 512], pp[:, 0:512], ALU.mult)
            P.tt("dve", hm, hm, gt, ALU.add)
            if not last:
                P.dma("sp", h1[sl, :], hm)
            else:
                P.act(junk, hm, AF.Square, accum_out=ssq[:, t:t + 1])
                P.act(rs[:, t:t + 1], ssq[:, t:t + 1], AF.Sqrt, bias=epsb[:, 0:1], scale=1.0 / D)
                recip(rs[:, t:t + 1], rs[:, t:t + 1])
                P.stt(gt, hm, rs[:, t:t + 1], gbc, ALU.mult, ALU.mult)
                P.dma("sp", y[sl, :], gt)

        stageA(0)
        for t in range(NT):
            if t + 1 < NT:
                stageA(t + 1)
            stageB(t)

    strip = sb("strip", [128, 8, 256], BF16)
    band = sb("band", [128, 8, 128], BF16)
    bands = sb("bands", [128, 8, 8], BF16)
    expand = sb("expand", [128, T], BF16)
    cover = sb("cover", [127, 34], BF16)
    zpad = sb("zpad", [128, 270], BF16)
    m1c1 = aF[:, 5704:5836].rearrange("p (a b) -> p a b", a=2)
    m1s = sb("m1s", [8, 33], F32)
    c1s = sb("c1s", [8, 33], F32)
    acm = sb("acm", [128, 128], BF16)
    tb = sb("tb", [33, 8], F32)
    r31 = sb("r31", [32, 8], F32)
    ohs = aF[0:33, 6000:6383]
    tbh = aF[0:33, 6400:6528]
    w2 = sb("w2", [128, 2, 64], BF16)
    cvec = sb("cvec", [128, 2], F32)
    peraw = aF[0:32, 5576:5704].rearrange("p (kv d) -> p kv d", kv=2)
    peT = sb("peT", [64, 2, 32], BF16)
    Vs16 = aB[:, 7424:7820].rearrange("p (a b) -> p a b", a=3)
    idx = sb("idx", [128, 256], I32)
    idxf = aF[:, 6400:6656]
    iot = sb("iot", [128, 1], F32)
    smallf = aF[:, 5836:5964]

    def AP(t, off, pat):
        return bass.AP(t.tensor, off, pat)

    def setup_attn():
        P.memset("pool", expand[32:64, :], 0.0)
        P.memset("pool", expand[64:128, :], 0.0)
        P.memset("pool", zpad[:], 0.0)
        P.memset("pool", band[:], 0.0)
        P.memset("pool", bands[:], 0.0)
        P.dma("sp", expand[0:33, :], c_expand)
        P.dma("sp", cover[:], c_cover)
        P.dma("sp", zpad[0:16, :], c_zpad)
        P.dma("sp", m1s[:], c_m1s)
        P.dma("sp", c1s[:], c_c1s)
        P.dma("sp", acm[:], c_ac)
        P.dma("sp", ohs, c_oh)
        P.dma("sp", iot[:], c_iota)
        P.dma("sp", tb[0:32, :], rel_bias)
        P.dma("sp", r31[:], rel_bias[31:32, :].broadcast_to([32, 8]))
        P.tt("dve", tb[0:32, :], tb[0:32, :], r31[:], ALU.subtract)
        P.memset("dve", tb[32:33, :], -BIG)
        gsb = aF[:, 3000:3383]
        for h in range(8):
            P.ts("dve", tbh, ones_f[0:33, :], tb[:, h:h + 1], None, ALU.mult)
            P.matmul(psD[:, 0:383], tbh, ohs, start=True, stop=True)
            P.copy("act", gsb, psD[:, 0:383])
            P.dma("sp", AP(G1, h * 128 * 512, [[513, 128], [1, 383]]), gsb)
            P.dma("sp", AP(G16, h * 16 * 640, [[656, 16], [1, 383]]), gsb[0:16, :])
        strip_f = aF[:, 3400:5448].rearrange("p (h x) -> p h x", h=8)
        band_f = aF[0:16, 5448:6472].rearrange("p (h x) -> p h x", h=8)
        bands_f = aF[0:16, 6472:6536].rearrange("p (h x) -> p h x", h=8)
        P.dma("sp", strip_f, AP(G1, 127, [[512, 128], [65536, 8], [1, 256]]))
        P.dma("sp", band_f, AP(G16, 240, [[640, 16], [10240, 8], [1, 128]]))
        P.dma("sp", bands_f, AP(G16, 352, [[640, 16], [10240, 8], [1, 8]]))
        P.copy("dve", strip[:], strip_f)
        P.copy("dve", band[0:16], band_f)
        P.copy("dve", bands[0:16], bands_f)
        P.dma("sp", idx[:], ptab[0:1, :].broadcast_to([128, 256]))
        P.copy("dve", idxf, idx[:])
        P.ts("dve", idxf, idxf, 128.0, iot[:, 0:1], ALU.mult, ALU.add)
        P.copy("dve", idx[:], idxf)

    w1 = WA[:].rearrange("p c n -> p (c n)").rearrange("p (kv pos h) -> p kv pos h", kv=2, pos=32)
    kcT = aB[:, 6400:6528]
    vcaug = aB[0:127, 6528:6724].rearrange("p (g e) -> p g e", g=2)
    hid = aB[:, 6724:6852]
    hid = aB[:, 7820:8076]

    def load_cmp_weights(l):
        for kv in range(2):
            for cp in range(2):
                P.dma("pool", w1[cp * 64:(cp + 1) * 64, kv, :, :],
                      cmp_w1[l, kv].rearrange("(pos d) h -> d pos h", d=64))
        P.dma("pool", w2[:], cmp_w2[l].rearrange("kv h d -> h kv d"))
        P.dma("sp", peraw, cmp_pe[l].rearrange("kv pos d -> pos kv d"))
        for kv in range(2):
            P.transpose(psD[0:64, kv * 32:(kv + 1) * 32], peraw[0:32, kv, :], ident_f[0:32, 0:32])
        P.copy("dve", peT[:], psD[0:64, 0:64].rearrange("p (kv pos) -> p kv pos", kv=2))
        for kv in range(2):
            for pos in range(32):
                P.matmul(psD[:, 64 + kv:65 + kv], w1[0:64, kv, pos, :], peT[:, kv, pos:pos + 1],
                         start=(pos == 0), stop=(pos == 31))
        P.copy("dve", cvec[:], psD[:, 64:66])

    cmp_tiles = [kcT, vcaug, hid]

    def compress(Ksrc, Vsrc):
        kcT, vcaug, hid = cmp_tiles
        for g in range(2):
            P.copy("pool", vcaug[:, g, 64:98], cover[:, :])
        for kv in range(2):
            src = Ksrc if kv == 0 else Vsrc
            pg = [psA, psB] if kv == 0 else [psC, psA]
            for pos in range(32):
                for g in range(2):
                    gb = g * 64
                    P.matmul(pg[g][:, 0:127], w1[gb:gb + 64, kv, pos, :], src[gb:gb + 64, pos:pos + 16 * 126 + 1:16],
                             start=(pos == 0), stop=(pos == 31))
            for g in range(2):
                gb = g * 64
                P.act(hid[:, g * 127:(g + 1) * 127], pg[g][:, 0:127], AF.Silu, bias=cvec[:, kv:kv + 1])
                hg = hid[:, g * 127:(g + 1) * 127]
                if kv == 0:
                    P.matmul(psE[gb:gb + 64, 0:127], w2[:, 0, :], hg, start=True, stop=True)
                    P.copy("dve", kcT[gb:gb + 64, 0:127], psE[gb:gb + 64, 0:127])
                else:
                    P.matmul(psE[0:127, 128 + g * 64:192 + g * 64], hg, w2[:, 1, :], start=True, stop=True)
                    P.copy("dve", vcaug[:, g, 0:64], psE[0:127, 128 + g * 64:192 + g * 64])

    es = [aB[:, i * 512:(i + 1) * 512] for i in range(3)]
    ecmp = aB[:, 1536:2048]
    negT = aB[:, 2048:6400].rearrange("p (g t) -> p g t", g=2)
    negb = aB[:, 6852:6885]
    attnb = aB[:, 6912:7424]
    accb = aF[:, 0:2048].rearrange("p (q e) -> p q e", q=4)
    cmpo = aF[:, 2048:2440].rearrange("p (r e) -> p r e", r=4)
    tmpo = aF[:, 2440:2960].rearrange("p (q e) -> p q e", q=4)
    rinv = smallf[:, 0:4]
    coef = smallf[:, 4:8]
    imp = smallf[:, 8:41]
    m8 = smallf[:, 48:56]
    ecnt = [0]
    ecmp_ref = [ecmp]

    def cmp_branch(rows, nc_, g, qcols, band_lhsT, band_rhs_of_h, gate_ap, acc_of_h, m1, c1, negT_out):
        gb = g * 64
        kcT, vcaug, hid = cmp_tiles
        for r in range(4):
            h = g * 4 + r
            P.matmul(psG[0:nc_, r * rows:(r + 1) * rows], kcT[gb:gb + 64, 0:nc_], qcols(r), start=True, stop=False)
            P.matmul(psG[0:nc_, r * rows:(r + 1) * rows], band_lhsT, band_rhs_of_h(h), start=False, stop=True)
        ecm = ecmp_ref[0]
        P.act(ecm[0:nc_, 0:4 * rows], psG[0:nc_, 0:4 * rows], AF.Exp)
        for r in range(4):
            P.matmul(psF[0:rows, r * 98:(r + 1) * 98], ecm[0:nc_, r * rows:(r + 1) * rows], vcaug[0:nc_, g, :],
                     start=True, stop=True)
        P.copy("act", cmpo[0:rows], psF[0:rows, 0:392].rearrange("p (r e) -> p r e", r=4))
        P.ts("dve", rinv[0:rows], cmpo[0:rows, :, 97], 1e-30, None, ALU.add)
        recip(rinv[0:rows], rinv[0:rows])
        P.ts("dve", imp[0:rows], cmpo[0:rows, 0, 64:97], rinv[0:rows, 0:1], None, ALU.mult)
        for r in range(1, 4):
            P.stt(imp[0:rows], cmpo[0:rows, r, 64:97], rinv[0:rows, r:r + 1], imp[0:rows], ALU.mult, ALU.add)
        P.tt("dve", coef[0:rows], rinv[0:rows], gate_ap, ALU.mult)
        for r in range(4):
            P.ts("dve", acc_of_h(g * 4 + r), cmpo[0:rows, r, 0:64], coef[0:rows, r:r + 1], None, ALU.mult)
        P.tt("dve", imp[0:rows], imp[0:rows], m1, ALU.mult)
        P.tt("dve", imp[0:rows], imp[0:rows], c1, ALU.add)
        P.add("dve", lambda e: e.max(out=m8[0:rows], in_=imp[0:rows]), reads=[imp[0:rows]], writes=[m8[0:rows]])
        P.ts("dve", negb[0:rows], imp[0:rows], m8[0:rows, 7:8], -BIG, ALU.is_lt, ALU.mult)
        pT = ptb()
        P.transpose(pT[0:33, 0:rows], negb[0:rows], ident_b[0:rows, 0:rows])
        P.copy("dve", negT_out[0:33], pT[0:33, 0:rows])

    def finish_head(rows, nq, pso, width, ocol, scol, gate_of_q, acc_of_q):
        P.copy("act", tmpo[0:rows, 0:nq, 0:width], pso[0:rows, 0:nq * width].rearrange("p (q e) -> p q e", q=nq))
        P.ts("dve", rinv[0:rows, 0:nq], tmpo[0:rows, 0:nq, scol], 1e-30, None, ALU.add)
        recip(rinv[0:rows, 0:nq], rinv[0:rows, 0:nq])
        P.tt("dve", coef[0:rows, 0:nq], rinv[0:rows, 0:nq], gate_of_q, ALU.mult)
        for qi in range(nq):
            a = acc_of_q(qi)
            P.stt(a, tmpo[0:rows, qi, ocol:ocol + 64], coef[0:rows, qi:qi + 1], a, ALU.mult, ALU.add)

    def attention_prompt(l):
        cmp_tiles[:] = [kcT, vcaug, hid]
        ecmp_ref[0] = ecmp
        compress(KT[:, 0, 0:TP], VcT[:, 0:TP])
        yield
        P.memset("pool", negT[32:64], 0.0)
        P.memset("pool", negT[64:128], 0.0)
        pso2 = [psD, psE]
        for qb in range(4):
            q0 = qb * 512
            for qi in range(4):
                qt = qb * 4 + qi
                qs = slice(qt * 128, (qt + 1) * 128)
                nc_ = min(127, 8 * qt + 7)
                c_off = 8 * qt - 9
                mc = m1c1[:, qt % 2, :]
                P.dma("sp", mc[:, 0:33], c_m1p[:, qt * 33:(qt + 1) * 33])
                P.dma("sp", mc[:, 33:66], c_c1p[:, qt * 33:(qt + 1) * 33])
                for g in range(2):
                    cmp_tiles[:] = [kcT, vcaug, hid]
                    ecmp_ref[0] = ecmp
                    cmp_branch(128, nc_, g,
                               lambda r, g=g, qs=qs: qT[g * 64:(g + 1) * 64, r, qs],
                               zpad[:, 127 - c_off:127 - c_off + nc_],
                               lambda h: band[:, h, :],
                               gs[:, qt, g * 4:g * 4 + 4],
                               lambda h, qi=qi: accb[:, qi, h * 64:(h + 1) * 64],
                               mc[:, 0:33], mc[:, 33:66],
                               negT[:, g, qs])
                yield
            for h in range(8):
                g, r = h // 4, h % 4
                gb = g * 64
                for br in (1, 2):
                    pso = pso2[br - 1]
                    P.memset("dve", pso[:, 0:260], 0.0)
                    kt_lo = 0 if br == 1 else max(0, 4 * qb - 4)
                    for kt in range(kt_lo, 4 * qb + 4):
                        qi_lo = max(0, kt - 4 * qb)
                        qi_hi = 3 if br == 1 else min(3, kt + 4 - 4 * qb)
                        c_lo, c_hi = qi_lo * 128, (qi_hi + 1) * 128
                        pss = rr()
                        ks = slice(kt * 128, (kt + 1) * 128)
                        extra = []
                        if br == 1:
                            extra.append((c_lo, c_hi, expand[:, ks], negT[:, g, q0 + c_lo:q0 + c_hi]))
                        d0 = kt - 4 * qb
                        if 0 <= d0 <= 2:
                            extra.append((d0 * 128, d0 * 128 + 256, ident_b[:], strip[:, h, 0:256]))
                        elif d0 == 3:
                            extra.append((384, 512, ident_b[:], strip[:, h, 0:128]))
                        elif d0 == -1:
                            extra.append((0, 128, ident_b[:], strip[:, h, 128:256]))
                        if br == 2 and 0 <= d0 + 4 <= 3:
                            extra.append(((d0 + 4) * 128, (d0 + 5) * 128, ident_b[:], acm[:]))
                        P.matmul(pss[:, c_lo:c_hi], KT[gb:gb + 64, br, ks], qT[gb:gb + 64, r, q0 + c_lo:q0 + c_hi],
                                 start=True, stop=(len(extra) == 0))
                        for i, (a, b_, lt, rh) in enumerate(extra):
                            P.matmul(pss[:, a:b_], lt, rh, start=False, stop=(i == len(extra) - 1))
                        ecnt[0] += 1
                        e = es[ecnt[0] % 3]
                        P.act(e[:, c_lo:c_hi], pss[:, c_lo:c_hi], AF.Exp)
                        for qi in range(qi_lo, qi_hi + 1):
                            P.matmul(pso[:, qi * 65:(qi + 1) * 65], e[:, qi * 128:(qi + 1) * 128],
                                     Vt[:, br, kt, g * 65:(g + 1) * 65], start=False, stop=False, skip_group_check=True)
                    finish_head(128, 4, pso, 65, 0, 64,
                                gs[:, qb * 4:qb * 4 + 4, br * 8 + h],
                                lambda qi, h=h: accb[:, qi, h * 64:(h + 1) * 64])
                    yield
            for qi in range(4):
                qt = qb * 4 + qi
                P.tt("dve", attnb, accb[:, qi, :], sza[:, qt, :], ALU.mult)
                pT = ptb()
                for c in range(4):
                    P.transpose(pT[:, c * 128:(c + 1) * 128], attnb[:, c * 128:(c + 1) * 128], ident_b[:])
                P.copy("act", mixa[:, 0:4, qt * 128:(qt + 1) * 128], pT[:, 0:512].rearrange("p (c n) -> p c n", c=4))

    def attention_sample(l):
        WBf = WB[:].rearrange("p c n -> p (c n)")
        KT0 = KT[:, 0, :]
        cmpq = [KT0[:, 0:1024].rearrange("p (j c) -> p j c", j=4), KT0[:, 1024:2048].rearrange("p (j c) -> p j c", j=4)]
        slcc = WBf[:, 0:4160].rearrange("p (j c) -> p j c", j=16)
        winc = WBf[:, 4160:5200].rearrange("p (j c) -> p j c", j=4)
        KcT_s = uT[:, 4, 0:2048]
        VcT_s = uT[:, 5, 0:2048]
        KsT = uT[:, 6, 0:2048]
        KwT = uT[:, 7, 0:512]
        e_s = WBf[:, 5200:5744]
        ecmp_s = WBf[:, 5744:5808]
        negT_s = WBf[:, 5808:5872].rearrange("p (g r q) -> p g r q", g=2, r=4)
        ac32 = WBf[:, 5872:5904].rearrange("p (r q) -> p r q", r=4)
        Vnew = WBf[0:8, 5904:6300].rearrange("p (a b) -> p a b", a=3)
        szab = WBf[0:8, 6300:6812]
        attn_s = WBf[0:8, 6812:7324]
        kcT_s = WBf[:, 7324:7452]
        vcaug_s = WBf[0:127, 7452:7648].rearrange("p (g e) -> p g e", g=2)
        hid_s = WBf[:, 7648:7904]
        accs = aF[0:8, 3200:3712]
        gsb_b = aF[0:8, 2960:2984]
        o32 = aF[0:32, 2984:3113]
        on32 = aF[0:32, 3113:3177]
        r32 = aF[0:32, 3177:3178]
        if l == 1:
            P.copy("dve", idxf, idx[:])
            P.ts("dve", idxf, idxf, float(npool * 128), None, ALU.add)
            P.copy("dve", idx[:], idxf)
        P.memset("pool", slcc[:, :, 256:260], 1.0)
        P.memset("pool", winc[:, :, 256:260], 1.0)
        P.memset("pool", negT_s[32:64], 0.0)
        P.memset("pool", negT_s[64:128], 0.0)
        for r in range(4):
            P.copy("pool", ac32[:, r, :], acm[:, 0:8])
        for b in range(16):
            tk = slice(TP + b * 8, TP + b * 8 + 8)
            def gather(dst_ap, srcd, col):
                def f(e):
                    return e.indirect_dma_start(out=dst_ap, out_offset=None, in_=srcd.rearrange("l r c -> (l r) c"),
                                                in_offset=bass.IndirectOffsetOnAxis(idx[:, col:col + 1], 0))
                P.add("pool", f, reads=[idx[:, col:col + 1]], writes=[dst_ap], is_dma=True)
            for j in range(16):
                gather(slcc[:, j, 0:256], cs, b * 16 + j)
            P.dma("pool", winc[:, :, 0:256], swin[l, b].rearrange("(j p) c -> p j c", p=128))
            P.dma("sp", Vnew, Vs16[b * 8:(b + 1) * 8])
            P.dma("sp", szab, sza[b * 8:(b + 1) * 8, 16, :])
            P.dma("sp", gsb_b, gs[b * 8:(b + 1) * 8, 16, :])
            for qtr in range(4):
                cq = cmpq[qtr % 2]
                if not (b > 0 and qtr < 2):
                    for jj in range(4):
                        gather(cq[:, jj, :], cc, b * 16 + qtr * 4 + jj)
                pT = ptb()
                for jj in range(4):
                    P.transpose(pT[:, jj * 128:(jj + 1) * 128], cq[:, jj, 0:128], ident_b[:])
                    P.transpose(pT[:, (4 + jj) * 128:(5 + jj) * 128], cq[:, jj, 128:256], ident_b[:])
                P.copy(alt(), KcT_s[:, qtr * 512:(qtr + 1) * 512], pT[:, 0:512])
                P.copy(alt(), VcT_s[:, qtr * 512:(qtr + 1) * 512], pT[:, 512:1024])
            for (src_t, c0, dstT, nj) in ((slcc, 0, KsT, 16), (winc, 0, KwT, 4)):
                for j0 in range(0, nj, 8):
                    n8 = min(8, nj - j0)
                    pT = ptb()
                    for j in range(j0, j0 + n8):
                        P.transpose(pT[:, (j - j0) * 128:(j - j0 + 1) * 128], src_t[:, j, c0:c0 + 128], ident_b[:])
                    P.copy(alt(), dstT[:, j0 * 128:(j0 + n8) * 128], pT[:, 0:n8 * 128])
            yield
            cmp_tiles[:] = [kcT_s, vcaug_s, hid_s]
            ecmp_ref[0] = ecmp_s
            compress(KcT_s, VcT_s)
            for g in range(2):
                cmp_branch(8, 127, g,
                           lambda r, g=g: qT[g * 64:(g + 1) * 64, r, tk],
                           zpad[:, 15:142],
                           lambda h: bands[:, h, :],
                           gsb_b[:, g * 4:g * 4 + 4],
                           lambda h: accs[:, h * 64:(h + 1) * 64],
                           m1s[:], c1s[:],
                           negT_s[:, g, 0, :])
                for r in range(1, 4):
                    P.copy("pool", negT_s[0:33, g, r, :], negT_s[0:33, g, 0, :])
            if b + 1 < 16:
                for qtr in range(2):
                    for jj in range(4):
                        gather(cmpq[qtr][:, jj, :], cc, (b + 1) * 16 + qtr * 4 + jj)
            yield
            for g in range(2):
                gb = g * 64
                q32 = qT[gb:gb + 64, 0:4, tk]
                for br in (1, 2):
                    pss = rr()
                    KpT = KsT if br == 1 else KwT
                    cch = slcc if br == 1 else winc
                    nj = 16 if br == 1 else 4
                    for j in range(nj):
                        cs_ = slice(j * 32, (j + 1) * 32)
                        extra = []
                        if br == 1:
                            extra.append((expand[:, j * 128:(j + 1) * 128], negT_s[:, g, :, :]))
                        if br == 2 and j == 0:
                            extra.append((ident_b[:], ac32))
                        if j == nj - 1:
                            extra.append((ident_b[:], strip[:, g * 4:(g + 1) * 4, 128:136]))
                        P.matmul(pss[:, cs_], KpT[gb:gb + 64, j * 128:(j + 1) * 128], q32, start=True, stop=(len(extra) == 0))
                        for i, (lt, rh) in enumerate(extra):
                            P.matmul(pss[:, cs_], lt, rh, start=False, stop=(i == len(extra) - 1))
                    P.matmul(psG[0:8, 0:32], KT[gb:gb + 64, br, tk], q32, start=True, stop=False)
                    P.matmul(psG[0:8, 0:32], ident_b[:, 0:8], strip[:, g * 4:(g + 1) * 4, 0:8], start=False, stop=True)
                    P.act(e_s[:, 0:nj * 32], pss[:, 0:nj * 32], AF.Exp)
                    P.act(e_s[0:8, 512:544], psG[0:8, 0:32], AF.Exp)
                    pso = psD if br == 1 else psE
                    for j in range(nj):
                        P.matmul(pso[0:32, 0:129], e_s[:, j * 32:(j + 1) * 32], cch[:, j, 128:257], start=(j == 0), stop=False)
                    P.matmul(pso[0:32, 0:129], e_s[0:8, 512:544], Vnew[:, br, 0:129], start=False, stop=True)
                    P.copy("act", o32, pso[0:32, 0:129])
                    P.ts("dve", r32, o32[:, 128:129], 1e-30, None, ALU.add)
                    recip(r32, r32)
                    P.ts("dve", on32, o32[:, g * 64:(g + 1) * 64], r32[:, 0:1], None, ALU.mult)
                    for r in range(4):
                        P.matmul(psF[0:8, r * 64:(r + 1) * 64], ident_f[0:32, r * 8:(r + 1) * 8], on32, start=True, stop=True)
                    for r in range(4):
                        h = g * 4 + r
                        a = accs[:, h * 64:(h + 1) * 64]
                        P.stt(a, psF[0:8, r * 64:(r + 1) * 64], gsb_b[:, br * 8 + h:br * 8 + h + 1], a, ALU.mult, ALU.add)
            P.tt("dve", attn_s, accs, szab, ALU.mult)
            pT = ptb()
            for c in range(4):
                P.transpose(pT[:, c * 8:(c + 1) * 8], attn_s[:, c * 128:(c + 1) * 128], ident_b[0:8, 0:8])
            P.copy("act", mixa[:, 0:4, tk], pT[:, 0:32].rearrange("p (c n) -> p c n", c=4))
            yield

    k.phase1 = phase1
    k.__dict__.update(locals())
    return k


def rel_bucket_np(dist):
    n = np.maximum(dist, 0)
    nf = np.maximum(n, 1).astype(np.float32)
    large = 16 + (np.log(nf / np.float32(16)) / np.float32(np.log(128 / 16)) * np.float32(16)).astype(np.int32)
    large = np.minimum(large, 31)
    return np.where(n < 16, n, large)


def make_consts():
    bf = ml_dtypes.bfloat16
    c = {}
    c["c_ident"] = np.eye(128, dtype=np.float32)
    d = np.arange(-127, 256)
    oh = np.zeros((33, 383), np.float32)
    b = rel_bucket_np(d)
    for i, dd in enumerate(d):
        if dd < 0:
            oh[32, i] = 1.0
        else:
            oh[b[i], i] = 1.0
    c["c_oh"] = oh
    pos = np.arange(T)
    ex = np.zeros((33, T), np.float32)
    ex[np.minimum(pos // 64, 32), pos] = 1.0
    c["c_expand"] = ex.astype(bf)
    cidx = np.arange(127)
    c_start = cidx * 16
    c_end = c_start + 31
    s_start = np.arange(33) * 64
    cover = ((c_start[:, None] < s_start[None, :] + 64) & (c_end[:, None] >= s_start[None, :])).astype(np.float32)
    c["c_cover"] = np.concatenate([cover, np.ones((127, 1), np.float32)], axis=1).astype(bf)
    zp = np.zeros((16, 270), np.float32)
    zp[np.arange(16), 127 + np.arange(16)] = 1.0
    c["c_zpad"] = zp.astype(bf)
    qpos = np.arange(2048)
    cur = qpos // 64
    blk = np.arange(33)
    forced = (blk[None] == 0) | (blk[None] == cur[:, None]) | (blk[None] == cur[:, None] - 1)
    allowed = blk[None] <= cur[:, None]
    m1 = (allowed & ~forced).astype(np.float32)
    c1 = np.where(allowed, np.where(forced, 1e4, 0.0), -1e4).astype(np.float32)
    c["c_m1p"] = m1.reshape(16, 128, 33).transpose(1, 0, 2).reshape(128, 16 * 33).copy()
    c["c_c1p"] = c1.reshape(16, 128, 33).transpose(1, 0, 2).reshape(128, 16 * 33).copy()
    qs = 2048 + np.arange(8)
    curs = qs // 64
    forced = (blk[None] == 0) | (blk[None] == curs[:, None]) | (blk[None] == curs[:, None] - 1)
    allowed = blk[None] <= curs[:, None]
    c["c_m1s"] = (allowed & ~forced).astype(np.float32)
    c["c_c1s"] = np.where(allowed, np.where(forced, 1e4, 0.0), -1e4).astype(np.float32)
    c["c_iota"] = np.arange(128, dtype=np.float32).reshape(128, 1)
    kl = np.arange(128)
    c["c_ac"] = np.where(kl[None, :] >= kl[:, None], -BIG, 0.0).astype(np.float32).astype(bf)
    return c


def core_inputs(inp, c, npool_rows=None):
    m = {}
    m["x_tok"] = np.concatenate([inp["x_prompt"][c], inp["x_sample"][16 * c:16 * c + 16].reshape(128, D)], 0)
    m["ple_tok"] = np.concatenate([inp["p_prompt"][:, c], inp["p_sample"][:, 16 * c:16 * c + 16].reshape(2, 128, 256)], 1)
    m["cc"] = inp["cache_cmp_kv"].reshape(2, -1, 256)
    m["cs"] = inp["cache_slc_kv"].reshape(2, -1, 256)
    m["ptab"] = inp["page_table"][16 * c:16 * c + 16].reshape(1, 256).astype(np.int32)
    m["swin"] = inp["state_win_kv"][:, 16 * c:16 * c + 16].reshape(2, 16, 512, 256)
    m["sconv"] = inp["state_conv"][:, 16 * c:16 * c + 16].reshape(2, 480, 512)
    m["norm_g"] = inp["norm_g"]
    m["w_in"] = inp["w_in"]
    m["convp"] = np.concatenate([inp["conv_w"], inp["conv_b"][:, None], inp["conv_ln_g"][:, None],
                                 inp["conv_ln_b"][:, None]], 1)
    m["cmp_pe"] = inp["cmp_pe"]
    m["cmp_w1"] = inp["cmp_w1"].reshape(2, 2, 2048, 128)
    m["cmp_w2"] = inp["cmp_w2"]
    m["w_out"] = inp["w_out"]
    m["w_ple"] = inp["w_ple"]
    m["w_pg"] = inp["w_ple_gate"]
    m["rel_bias"] = inp["rel_bias"]
    m["fng"] = inp["final_norm_g"].reshape(1, D)
    return {k_: np.ascontiguousarray(v) for k_, v in m.items()}


STAGE = 7


def program(k, stage=99):
    srcs = [k.x_tok, k.h1]
    for l in range(2):
        k.phase1(l, srcs[l])
        k.phase2a(l)
        k.phase3(l)
        k.phase2b(l)
        if stage >= 6:
            if l == 0:
                k.setup_attn()
            k.load_cmp_weights(l)
            gp = k.attention_prompt(l)
            if stage >= 7:
                gsm = k.attention_sample(l)
                next(gp)
                alive = [gp, gsm]
                ratio = [3, 2]
                while alive:
                    for gi, gen in enumerate(list(alive)):
                        for _ in range(ratio[0] if gen is gp else ratio[1]):
                            try:
                                next(gen)
                            except StopIteration:
                                alive.remove(gen)
                                break
            else:
                for _ in gp:
                    pass
                k.P.memset("pool", k.uT[:, 0:4, 2048:2176], 0.0)
        else:
            k.P.memset("pool", k.uT[:, 0:4, :], 0.0)
        k.phase5(l, srcs[l], l == 1)


_CACHE = {}


def kernel(**inp):
    npool = inp["cache_cmp_kv"].shape[1]
    if "nc" not in _CACHE:
        nc = bass.Bass("TRN2", target_bir_lowering=False)
        k = build(nc, npool, stage=STAGE)
        program(k, STAGE)
        k.P.emit()
        _CACHE["nc"] = nc
    nc = _CACHE["nc"]
    consts = make_consts()
    in_maps = []
    for c in range(8):
        m = core_inputs(inp, c)
        if STAGE < 6:
            m.pop("cc")
            m.pop("cs")
        m.update(consts)
        in_maps.append(m)
    res = run_bass_kernel_spmd(nc, in_maps, core_ids=list(range(8)))
    r = res.results
    y_p = np.stack([r[c]["y"][:TP] for c in range(8)])
    y_s = np.concatenate([r[c]["y"][TP:].reshape(16, 8, D) for c in range(8)])

    def kvp(name):
        return np.stack([r[c][name][:, :TP] for c in range(8)], 1).reshape(2, 8, TP, 2, 2, 64)

    def kvs(name):
        return np.concatenate([r[c][name][:, TP:].reshape(2, 16, 8, 256) for c in range(8)], 1).reshape(2, 128, 8, 2, 2, 64)
    win_p = np.stack([r[c]["nwin_p"] for c in range(8)], 1).reshape(2, 8, 512, 2, 2, 64)
    win_s = np.concatenate([r[c]["nwin_s"] for c in range(8)], 1).reshape(2, 128, 512, 2, 2, 64)
    conv_p = np.stack([r[c]["nconv_p"] for c in range(8)], 1)
    conv_s = np.concatenate([r[c]["nconv_s"] for c in range(8)], 1)
    f = lambda a: np.ascontiguousarray(a, dtype=np.float32)
    return (f(y_p), f(y_s), f(kvp("ncmp")), f(kvs("ncmp")), f(kvp("nslc")), f(kvs("nslc")),
            f(win_p), f(win_s), f(conv_p), f(conv_s))
```

```python
import contextlib
import numpy as np
import ml_dtypes
import concourse.bass as bass
import concourse.mybir as mybir
from concourse.bass_utils import run_bass_kernel_spmd

F32 = mybir.dt.float32
BF16 = mybir.dt.bfloat16
I32 = mybir.dt.int32
ALU = mybir.AluOpType
AF = mybir.ActivationFunctionType

ENGS = ("pe", "act", "dve", "pool", "sp")
NSLOT = {"sp": 28, "pool": 16, "act": 4}
BIG = 30000.0


def _dsize(dt):
    return mybir.dt.size(dt)


def _region(ap):
    t = ap.tensor
    space = str(ap.space).upper()
    name = t.name
    if "DRAM" in space or "HBM" in space:
        f0 = ap.offset
        ext = 0
        for st_, cnt in ap.ap:
            ext += (cnt - 1) * abs(st_)
        return ("D", name, 0, 1, f0, f0 + ext + 1)
    shp = list(t.shape)
    pstride = 1
    for s in shp[1:]:
        pstride *= s
    off = ap.offset
    p0 = off // pstride
    f0 = off % pstride
    aps = list(ap.ap)
    pcount = 1
    ext = 0
    for i, (st, cnt) in enumerate(aps):
        if i == 0:
            pcount = cnt if st == pstride or cnt == 1 else cnt
            if st != pstride and cnt > 1 and st != 0:
                pcount = 1
                ext += (cnt - 1) * abs(st)
            continue
        ext += (cnt - 1) * abs(st)
    f1 = f0 + ext + 1
    if "PSUM" in space:
        per_bank = 2048 // _dsize(t.dtype)
        f0 = (f0 // per_bank) * per_bank
        f1 = ((f1 + per_bank - 1) // per_bank) * per_bank
        return ("P", name, 0, 128, f0, f1)
    return ("S", name, p0, p0 + pcount, f0, f1)


def _overlap(a, b):
    return a[1] == b[1] and a[2] < b[3] and b[2] < a[3] and a[4] < b[5] and b[4] < a[5]


def _covers(a, b):
    return a[1] == b[1] and a[2] <= b[2] and a[3] >= b[3] and a[4] <= b[4] and a[5] >= b[5]


class Op:
    __slots__ = ("eng", "fn", "idx", "eidx", "deps", "is_dma", "dma_k", "signal", "waits", "sigval")


class Prog:
    def __init__(self, nc):
        self.nc = nc
        self.ops = []
        self.eng_ops = {e: [] for e in ENGS}
        self.track = {}
        self.dma_count = {e: 0 for e in ENGS}

    def add(self, eng, fn, reads=(), writes=(), is_dma=False):
        op = Op()
        op.eng = eng
        op.fn = fn
        op.idx = len(self.ops)
        op.eidx = len(self.eng_ops[eng])
        op.deps = set()
        op.is_dma = is_dma
        op.dma_k = None
        if is_dma:
            op.dma_k = self.dma_count[eng]
            self.dma_count[eng] += 1
        op.signal = False
        self.ops.append(op)
        self.eng_ops[eng].append(op)
        for a in reads:
            if a is None or isinstance(a, (int, float)):
                continue
            r = _region(a)
            self._access(op, r, r[0] == "P")
        for a in writes:
            if a is None:
                continue
            self._access(op, _region(a), True)
        return op

    def _access(self, op, reg, is_write):
        lst = self.track.get(reg[1])
        if lst is None:
            self.track[reg[1]] = [[reg, op, is_write]]
            return
        new = []
        for ent in lst:
            ereg, eop, ew = ent
            if eop is op:
                new.append(ent)
                continue
            if _overlap(ereg, reg):
                if is_write or ew:
                    op.deps.add(eop)
                    if is_write and _covers(reg, ereg):
                        continue
                    new.append(ent)
                else:
                    if (not eop.is_dma) and (not op.is_dma) and eop.eng == op.eng and ereg == reg:
                        continue
                    new.append(ent)
            else:
                new.append(ent)
        new.append([reg, op, is_write])
        self.track[reg[1]] = new

    def matmul(self, out, lhsT, rhs, start=True, stop=True, **kw):
        return self.add("pe", lambda e: e.matmul(out, lhsT, rhs, start=start, stop=stop, **kw),
                        reads=[lhsT, rhs], writes=[out])

    def transpose(self, out, in_, ident):
        return self.add("pe", lambda e: e.transpose(out, in_, ident), reads=[in_, ident], writes=[out])

    def act(self, out, in_, func, bias=None, scale=None, accum_out=None):
        kw = {}
        rd = [in_]
        if bias is not None:
            kw["bias"] = bias
            rd.append(bias)
        if scale is not None:
            kw["scale"] = scale
            rd.append(scale)
        wr = [out]
        if accum_out is not None:
            kw["accum_out"] = accum_out
            wr.append(accum_out)
        return self.add("act", lambda e: e.activation(out, in_, func, **kw), reads=rd, writes=wr)

    def tt(self, eng, out, in0, in1, op):
        return self.add(eng, lambda e: e.tensor_tensor(out, in0, in1, op), reads=[in0, in1], writes=[out])

    def ts(self, eng, out, in0, s1, s2, op0, op1=None):
        kw = {}
        if op1 is not None:
            kw["op1"] = op1
        return self.add(eng, lambda e: e.tensor_scalar(out, in0, s1, s2, op0, **kw),
                        reads=[in0, s1, s2], writes=[out])

    def stt(self, out, in0, scalar, in1, op0, op1):
        return self.add("dve", lambda e: e.scalar_tensor_tensor(out, in0, scalar, in1, op0, op1),
                        reads=[in0, in1, scalar], writes=[out])

    def copy(self, eng, out, in_):
        if eng == "act":
            return self.add(eng, lambda e: e.copy(out, in_), reads=[in_], writes=[out])
        return self.add(eng, lambda e: e.tensor_copy(out, in_), reads=[in_], writes=[out])

    def memset(self, eng, out, val):
        return self.add(eng, lambda e: e.memset(out, val), reads=[], writes=[out])

    def dma(self, queue, out, in_, **kw):
        return self.add(queue, lambda e: e.dma_start(out=out, in_=in_, **kw),
                        reads=[in_], writes=[out], is_dma=True)

    def emit(self):
        nc = self.nc
        waited = {f: {e: -1 for e in ENGS} for f in ENGS}
        for op in self.ops:
            op.waits = []
        for op in self.ops:
            f = op.eng
            need = {}
            for d in op.deps:
                if d.is_dma:
                    op.waits.append(d)
                    continue
                e = d.eng
                if e == f and not op.is_dma:
                    if e in ("pe", "sp"):
                        continue
                    if op.eidx - d.eidx > 3:
                        continue
                if d.eidx > waited[f][e]:
                    if e not in need or need[e].eidx < d.eidx:
                        need[e] = d
            for e, d in need.items():
                waited[f][e] = d.eidx
                d.signal = True
                op.waits.append(d)
        for e in ENGS:
            c = 0
            for op in self.eng_ops[e]:
                if op.is_dma:
                    continue
                if op.signal:
                    c += 1
                    op.sigval = c
        dma_waited = {f: set() for f in ENGS}
        with contextlib.ExitStack() as st:
            sems = {e: st.enter_context(nc.semaphore("sem_" + e)) for e in ENGS}
            dsems = {}
            for q, n in NSLOT.items():
                if self.dma_count[q] > 0:
                    dsems[q] = [st.enter_context(nc.semaphore("dsem_%s_%d" % (q, i))) for i in range(n)]
            block = st.enter_context(nc.Block())

            def dma_slot(d):
                n = NSLOT[d.eng]
                return dsems[d.eng][d.dma_k % n], 16 * (d.dma_k // n + 1)

            def run_engine(ename, eng):
                final_wait = {}
                for op in self.eng_ops[ename]:
                    for d in op.waits:
                        if d.is_dma:
                            if d in dma_waited[ename]:
                                continue
                            dma_waited[ename].add(d)
                            s, v = dma_slot(d)
                            eng.wait_ge(s, v)
                        else:
                            eng.wait_ge(sems[d.eng], d.sigval)
                    if op.is_dma:
                        n = NSLOT[ename]
                        if op.dma_k >= n:
                            eng.wait_ge(dsems[ename][op.dma_k % n], 16 * (op.dma_k // n))
                        ins = op.fn(eng)
                        s, v = dma_slot(op)
                        ins.then_inc(s, 16)
                        final_wait[op.dma_k % n] = (s, v)
                    else:
                        ins = op.fn(eng)
                        if op.signal:
                            ins.then_inc(sems[ename], 1)
                for k, (s, v) in final_wait.items():
                    eng.wait_ge(s, v)

            @block.sync
            def _(eng):
                run_engine("sp", eng)

            @block.tensor
            def _(eng):
                run_engine("pe", eng)

            @block.scalar
            def _(eng):
                run_engine("act", eng)

            @block.vector
            def _(eng):
                run_engine("dve", eng)

            @block.gpsimd
            def _(eng):
                run_engine("pool", eng)


T = 2176
TP = 2048
NT = 17
D = 1024
DIN = 3352
EPS = 1e-6
NB5 = [(0, 512), (512, 512), (1024, 512), (1536, 512), (2048, 128)]


class K:
    pass


def build(nc, npool, nlayers=2, stage=99):
    k = K()
    k.nc = nc
    P = Prog(nc)
    k.P = P
    st = contextlib.ExitStack()
    k.st = st

    def din(name, shape, dt=F32):
        return nc.dram_tensor(name, list(shape), dt, kind="ExternalInput").ap()

    def dout(name, shape, dt=F32):
        return nc.dram_tensor(name, list(shape), dt, kind="ExternalOutput").ap()

    def dscr(name, shape, dt=F32):
        return nc.dram_tensor(name, list(shape), dt, kind="Internal").ap()

    def sb(name, shape, dt):
        return st.enter_context(nc.sbuf_tensor(name, list(shape), dt))

    def ps(name, shape, dt=F32):
        return st.enter_context(nc.psum_tensor(name, list(shape), dt))

    x_tok = din("x_tok", [T, D])
    ple_tok = din("ple_tok", [2, T, 256])
    if stage >= 6:
        cc = din("cc", [2, npool * 128, 256])
        cs = din("cs", [2, npool * 128, 256])
    ptab = din("ptab", [1, 256], I32)
    swin = din("swin", [2, 16, 512, 256])
    sconv = din("sconv", [2, 480, 512])
    norm_g = din("norm_g", [2, D])
    w_in = din("w_in", [2, D, DIN])
    convp = din("convp", [2, 34, 512])
    cmp_pe = din("cmp_pe", [2, 2, 32, 64])
    cmp_w1 = din("cmp_w1", [2, 2, 2048, 128])
    cmp_w2 = din("cmp_w2", [2, 2, 128, 64])
    w_out = din("w_out", [2, D, D])
    w_ple = din("w_ple", [2, 256, D])
    w_pg = din("w_pg", [2, D, D])
    rel_bias = din("rel_bias", [32, 8])
    fng = din("fng", [1, D])
    c_ident = din("c_ident", [128, 128])
    c_oh = din("c_oh", [33, 383])
    c_expand = din("c_expand", [33, T], BF16)
    c_cover = din("c_cover", [127, 34], BF16)
    c_zpad = din("c_zpad", [16, 270], BF16)
    c_m1p = din("c_m1p", [128, 16 * 33])
    c_c1p = din("c_c1p", [128, 16 * 33])
    c_m1s = din("c_m1s", [8, 33])
    c_c1s = din("c_c1s", [8, 33])
    c_iota = din("c_iota", [128, 1])
    c_ac = din("c_ac", [128, 128], BF16)

    y = dout("y", [T, D])
    ncmp = dout("ncmp", [2, T, 256])
    nslc = dout("nslc", [2, T, 256])
    nwin_p = dout("nwin_p", [2, 512, 256])
    nwin_s = dout("nwin_s", [2, 16, 512, 256])
    nconv_p = dout("nconv_p", [2, 30, 512])
    nconv_s = dout("nconv_s", [2, 16, 30, 512])
    h1 = dscr("h1", [T, D])
    G1 = dscr("G1", [8, 128, 512])
    G16 = dscr("G16", [8, 16, 640])

    ident_f = sb("ident_f", [128, 128], F32)
    ident_b = sb("ident_b", [128, 128], BF16)
    ones_f = sb("ones_f", [128, 128], F32)
    epsb = sb("epsb", [128, 1], F32)
    uT = sb("uT", [128, 8, T], BF16)
    WA = sb("WA", [128, 8, 1024], BF16)
    WB = sb("WB", [128, 8, 1024], BF16)
    bigG = sb("bigG", [128, 4, T], BF16)
    bigZ = sb("bigZ", [128, 4 * T], BF16)
    mixc = sb("mixc", [128, 4, T], BF16)
    KT = sb("KT", [128, 3, T], BF16)
    Vt = sb("Vt", [128, 3, 17, 130], BF16)
    gs = sb("gs", [128, 17, 24], F32)
    aF = sb("aF", [128, 7168], F32)
    aB = sb("aB", [128, 8192], BF16)
    ssq = sb("ssq", [128, NT], F32)
    rs = sb("rs", [128, NT], F32)
    cw = sb("cw", [128, 4, 34], F32)
    gluP = bigG
    gluS = KT[:].rearrange("p a t -> p (a t)")[:, 0:2432].rearrange("p (c b r) -> p c b r", c=4, b=16)
    WC = bigG[:, 0, 0:2048].rearrange("p (c n) -> p c n", c=2)
    VcT = aB[:, 0:2048]
    cwraw = aF[0:34, 6656:7168]
    zc = bigZ[:].rearrange("p (c t) -> p c t", c=4)
    sza = bigZ[:].rearrange("p (t e) -> p t e", e=512)
    qT = bigG
    xin = [aF[:, 0:1024], aF[:, 1024:2048]]
    gbc = aF[:, 2048:3072]
    sg = [aF[:, 3072:3584], aF[:, 3584:4096]]
    glu_tail = aF[:, 4096:5120].rearrange("p (c n) -> p c n", c=4)
    stage_t = [aF[:, 5120:5888], aF[:, 5888:6656]]
    dw = aF[:, 0:2048].rearrange("p (c n) -> p c n", c=4)
    sq = aF[:, 2048:2560]
    st_m = aF[:, 2560:3072]
    st_v = aF[:, 3072:3584]
    st_t = aF[:, 3584:4096]
    scraw = aF[:, 5120:5632]
    tailtok = aF[:, 5632:6656].rearrange("p (c n) -> p c n", c=2)
    Dk = aB[:, 0:3968].rearrange("p (k n) -> p k n", k=31)
    scb = aB[:, 3968:6016].rearrange("p (c n) -> p c n", c=4)
    ub = aB[:, 6016:7040]
    junk = aB[:, 7040:8064]

    psA = ps("psA", [128, 512])
    psB = ps("psB", [128, 512])
    psC = ps("psC", [128, 512])
    psT = ps("psT", [128, 1024], BF16)
    psD = ps("psD", [128, 512])
    psE = ps("psE", [128, 512])
    psF = ps("psF", [128, 512])
    psT2 = ps("psT2", [128, 1024], BF16)
    psG = psF
    rot3 = [psA, psB, psC]
    tcnt = [0]

    def ptb():
        tcnt[0] += 1
        return [psT, psT2][tcnt[0] % 2]

    P.dma("sp", ident_f[:], c_ident)
    P.copy("dve", ident_b[:], ident_f[:])
    P.memset("dve", ones_f[:], 1.0)
    P.memset("dve", epsb[:], EPS)
    P.memset("pool", Vt[:], 1.0)

    wrow = lambda w, l: w[l].rearrange("(c p) n -> p c n", p=128)
    cnt = [0]

    def rr():
        cnt[0] += 1
        return rot3[cnt[0] % 3]

    def alt():
        cnt[0] += 1
        return "act" if cnt[0] % 2 else "dve"

    def recip(out, in_):
        return P.add("dve", lambda e: e.reciprocal(out, in_), reads=[in_], writes=[out])

    def phase1(l, src):
        P.dma("sp", gbc[:], norm_g[l:l + 1, :].broadcast_to([128, D]))
        for t in range(NT):
            xt = xin[t % 2]
            P.dma("sp", xt, src[t * 128:(t + 1) * 128, :])
            P.act(junk[:], xt[:], AF.Square, accum_out=ssq[:, t:t + 1])
            P.act(rs[:, t:t + 1], ssq[:, t:t + 1], AF.Sqrt, bias=epsb[:, 0:1], scale=1.0 / D)
            recip(rs[:, t:t + 1], rs[:, t:t + 1])
            P.stt(ub[:], xt[:], rs[:, t:t + 1], gbc[:], ALU.mult, ALU.mult)
            pT = ptb()
            for c in range(8):
                P.transpose(pT[:, c * 128:(c + 1) * 128], ub[:, c * 128:(c + 1) * 128], ident_b[:])
            P.copy(alt(), uT[:, :, t * 128:(t + 1) * 128], pT[:].rearrange("p (c n) -> p c n", c=8))

    def fm_mm(pst, lhs_of_c, nb, m=128):
        n0, nn = NB5[nb]
        for c in range(8):
            P.matmul(pst[0:m, 0:nn], lhs_of_c(c), uT[:, c, n0:n0 + nn], start=(c == 0), stop=(c == 7))

    def phase2a(l):
        wl = wrow(w_in, l)
        P.memset("pool", gluP[:, :, 0:30], 0.0)
        P.dma("pool", WB[:, :, 0:512], wl[:, :, 0:512])
        P.dma("pool", WB[:, :, 512:1024], wl[:, :, 512:1024])
        for ch in range(4):
            for nb in range(5):
                n0, nn = NB5[nb]
                fm_mm(psA, lambda c: WB[:, c, ch * 128:(ch + 1) * 128], nb)
                fm_mm(psB, lambda c: WB[:, c, 512 + ch * 128:512 + (ch + 1) * 128], nb)
                s = sg[nb % 2]
                P.act(s[:, 0:nn], psB[:, 0:nn], AF.Sigmoid)
                if nb < 4:
                    P.tt("dve", gluP[:, ch, 30 + n0:30 + n0 + nn], psA[:, 0:nn], s[:, 0:nn], ALU.mult)
                    if nb == 3:
                        P.tt("dve", glu_tail[:, ch, 0:128], psA[:, 384:512], s[:, 384:512], ALU.mult)
                else:
                    P.tt("dve", gluS[:, ch, :, 30:38], psA[:, 0:128].rearrange("p (b q) -> p b q", q=8),
                         s[:, 0:128].rearrange("p (b q) -> p b q", q=8), ALU.mult)
                    P.tt("dve", glu_tail[:, ch, 128:256], psA[:, 0:128], s[:, 0:128], ALU.mult)
        P.dma("pool", WB[:, :, 0:512], wl[:, :, 1024:1536])
        for ch in range(4):
            for nb in range(5):
                n0, nn = NB5[nb]
                pst = rr()
                fm_mm(pst, lambda c: WB[:, c, ch * 128:(ch + 1) * 128], nb)
                P.act(zc[:, ch, n0:n0 + nn], pst[:, 0:nn], AF.Silu)

    def tm_group(l, half, ncols, evac):
        for t in range(17):
            sl = slice(t * 128, (t + 1) * 128)
            pst = rr()
            for c in range(8):
                P.matmul(pst[:, 0:ncols], uT[:, c, sl], WB[:, c, half * 512:half * 512 + ncols],
                         start=(c == 0), stop=(c == 7))
            evac(t, sl, pst)

    def phase2b(l):
        wl = wrow(w_in, l)
        P.memset("pool", Vs16[:, :, 128:132], 1.0)
        for two in range(2):
            for c in range(8):
                P.dma("pool", WB[:, c, 512:1024].rearrange("p (j two d) -> p j two d", two=2, d=64)[:, :, two, :],
                      wl[:, c, 1536 + two * 256:1536 + two * 256 + 256].rearrange("p (j d) -> p j d", d=64))
        for j in range(4):
            for nb in range(5):
                n0, nn = NB5[nb]
                pst = rr()
                fm_mm(pst, lambda c: WB[:, c, 512 + j * 128:512 + (j + 1) * 128], nb)
                P.act(qT[:, j, n0:n0 + nn], pst[:, 0:nn], AF.Copy, scale=0.125)
        outs = [ncmp, nslc, None]
        for br in range(3):
            half = br % 2
            c0 = 2048 + br * 256
            P.dma("pool", WB[:, :, half * 512:half * 512 + 256], wl[:, :, c0:c0 + 256])
            if br == 0:
                P.dma("pool", WB[:, :, 256:280], wl[:, :, 3328:3352])
            for nb in range(5):
                n0, nn = NB5[nb]
                pst = rr()
                fm_mm(pst, lambda c: WB[:, c, half * 512:half * 512 + 128], nb)
                P.copy(alt(), KT[:, br, n0:n0 + nn], pst[:, 0:nn])
            if br == 0:
                for nb in range(4):
                    n0, nn = NB5[nb]
                    pst = rr()
                    fm_mm(pst, lambda c: WB[:, c, 128:256], nb)
                    P.copy(alt(), VcT[:, n0:n0 + nn], pst[:, 0:nn])

            def evac(t, sl, pst, br=br):
                stg = stage_t[t % 2]
                P.copy("act", stg[:, 0:256], pst[:, 0:256])
                P.copy("pool", Vt[:, br, t, :].rearrange("p (g e) -> p g e", g=2)[:, :, 0:64],
                       stg[:, 128:256].rearrange("p (g d) -> p g d", g=2))
                if t == 16:
                    P.copy("pool", Vs16[:, br, 0:128], stg[:, 128:256])
                if br == 0:
                    P.act(gs[:, t, :], pst[:, 256:280], AF.Sigmoid)
                if br < 2:
                    P.dma("sp", outs[br][l, sl, :], stg[:, 0:256])
                else:
                    if 12 <= t < 16:
                        P.dma("sp", nwin_p[l, (t - 12) * 128:(t - 11) * 128, :], stg[:, 0:256])
                    if t == 16:
                        for b in range(16):
                            P.dma("sp", nwin_s[l, b, 504:512, :], stg[b * 8:(b + 1) * 8, 0:256])
            tm_group(l, half, 280 if br == 0 else 256, evac)
        if stage != 41:
            P.dma("sp", nwin_s[l][:, 0:504, :], swin[l][:, 8:512, :])
        P.dma("pool", WB[:, :, 512:1024], wl[:, :, 2816:3328])
        tm_group(l, 1, 512, lambda t, sl, pst: P.act(sza[:, t, :], pst[:, 0:512], AF.Silu))

    ones_b = sb("ones_b", [128, 128], BF16)
    P.memset("pool", ones_b[:], 1.0)

    def phase3(l):
        for ch in range(4):
            for hlf in range(2):
                P.transpose(psD[:, (hlf * 4 + ch) % 4 * 128:((hlf * 4 + ch) % 4 + 1) * 128],
                            glu_tail[:, ch, hlf * 128:(hlf + 1) * 128], ident_f[:])
                P.copy("dve", tailtok[:, hlf, ch * 128:(ch + 1) * 128],
                       psD[:, (hlf * 4 + ch) % 4 * 128:((hlf * 4 + ch) % 4 + 1) * 128])
        P.dma("sp", nconv_p[l], tailtok[98:128, 0, :])
        for b in range(16):
            P.dma("sp", nconv_s[l, b, 22:30, :], tailtok[b * 8:(b + 1) * 8, 1, :])
        P.dma("sp", nconv_s[l][:, 0:22, :], sconv[l].rearrange("(b r) c -> b r c", r=30)[:, 8:30, :])
        P.dma("sp", cwraw, convp[l])
        for ch in range(4):
            P.transpose(psD[:, 0:34], cwraw[0:34, ch * 128:(ch + 1) * 128], ident_f[0:34, 0:34])
            P.copy("dve", cw[:, ch, :], psD[:, 0:34])
        for r in range(4):
            nr = 128 if r < 3 else 96
            P.dma("sp", scraw[0:nr, :], sconv[l, r * 128:r * 128 + nr, :])
            P.copy("pool", scb[0:nr, r, :], scraw[0:nr, :])
        for ch in range(4):
            pT = ptb()
            for r in range(4):
                nr = 128 if r < 3 else 96
                P.transpose(pT[:, r * 128:r * 128 + nr], scb[0:nr, r, ch * 128:(ch + 1) * 128], ident_b[0:nr, 0:nr])
            P.copy("dve", gluS[:, ch, :, 0:30], pT[:, 0:480].rearrange("p (b r) -> p b r", r=30))
        for ch in range(4):
            for kk in range(31):
                P.ts("dve" if kk % 2 else "pool", Dk[:, kk, :], ident_f[:], cw[:, ch, kk:kk + 1], None, ALU.mult)
            for nb in range(5):
                n0, nn = NB5[nb]
                pst = rr()
                for kk in range(31):
                    if nb < 4:
                        rhs = gluP[:, ch, n0 + kk:n0 + kk + 512]
                    else:
                        rhs = gluS[:, ch, :, kk:kk + 8]
                    P.matmul(pst[:, 0:nn], Dk[:, kk, :], rhs, start=(kk == 0), stop=(kk == 30))
                P.act(mixc[:, ch, n0:n0 + nn], pst[:, 0:nn], AF.Identity, bias=cw[:, ch, 31:32])
        for nb in range(5):
            n0, nn = NB5[nb]
            for ch in range(4):
                P.act(sq[:, 0:nn], mixc[:, ch, n0:n0 + nn], AF.Square)
                P.matmul(psD[:, 0:nn], ones_b[:], mixc[:, ch, n0:n0 + nn], start=(ch == 0), stop=(ch == 3))
                P.matmul(psE[:, 0:nn], ones_f[:], sq[:, 0:nn], start=(ch == 0), stop=(ch == 3))
            P.ts("dve", st_m[:, 0:nn], psD[:, 0:nn], 1.0 / 512, None, ALU.mult)
            P.tt("dve", st_v[:, 0:nn], st_m[:, 0:nn], st_m[:, 0:nn], ALU.mult)
            P.stt(st_v[:, 0:nn], psE[:, 0:nn], 1.0 / 512, st_v[:, 0:nn], ALU.mult, ALU.subtract)
            P.act(st_t[:, 0:nn], st_v[:, 0:nn], AF.Sqrt, bias=epsb[:, 0:1])
            recip(st_t[:, 0:nn], st_t[:, 0:nn])
            for ch in range(4):
                P.tt("dve", sq[:, 0:nn], mixc[:, ch, n0:n0 + nn], st_m[:, 0:nn], ALU.subtract)
                P.tt("dve", sq[:, 0:nn], sq[:, 0:nn], st_t[:, 0:nn], ALU.mult)
                P.act(sq[:, 0:nn], sq[:, 0:nn], AF.Silu, bias=cw[:, ch, 33:34], scale=cw[:, ch, 32:33])
                P.tt("dve", mixc[:, ch, n0:n0 + nn], sq[:, 0:nn], zc[:, ch, n0:n0 + nn], ALU.mult)

    mixa = uT

    def phase5(l, src, last):
        P.dma("pool", WA[:], wrow(w_out, l))
        P.dma("pool", WB[:], wrow(w_pg, l))
        P.dma("pool", WC, wrow(w_ple, l))
        if last:
            P.dma("sp", gbc, fng[0:1, :].broadcast_to([128, D]))
        hms = [aF[:, 3072:4096], aF[:, 5376:6400]]
        plts = [aF[:, 5120:5376], aF[:, 6400:6656]]
        plbs = [aB[:, 0:256], aB[:, 4096:4352]]
        plTs = [aB[:, 256:512].rearrange("p (c n) -> p c n", c=2), aB[:, 4352:4608].rearrange("p (c n) -> p c n", c=2)]
        hmTs = [aB[:, 512:1536].rearrange("p (c n) -> p c n", c=8), aB[:, 3072:4096].rearrange("p (c n) -> p c n", c=8)]
        ubs = [ub, aB[:, 2048:3072]]
        def stageA(t):
            sl = slice(t * 128, (t + 1) * 128)
            p2 = t % 2
            xt = xin[p2]
            hm, plt, plb, plT, hmT, ubp = hms[p2], plts[p2], plbs[p2], plTs[p2], hmTs[p2], ubs[p2]
            P.dma("sp", xt, src[sl, :])
            P.dma("sp", plt, ple_tok[l, sl, :])
            for hf in range(2):
                pst = [psA, psB][hf]
                for e in range(8):
                    lhs = mixc[:, e, sl] if e < 4 else mixa[:, e - 4, sl]
                    P.matmul(pst[:, 0:512], lhs, WA[:, e, hf * 512:(hf + 1) * 512], start=(e == 0), stop=(e == 7))
                P.tt("dve", hm[:, hf * 512:(hf + 1) * 512], pst[:, 0:512], xt[:, hf * 512:(hf + 1) * 512], ALU.add)
            P.copy("act", ubp, hm)
            P.copy("pool", plb, plt)

        def stageA2(t):
            p2 = t % 2
            hm, plt, plb, plT, hmT, ubp = hms[p2], plts[p2], plbs[p2], plTs[p2], hmTs[p2], ubs[p2]
            pT = ptb()
            for c in range(8):
                P.transpose(pT[:, c * 128:(c + 1) * 128], ubp[:, c * 128:(c + 1) * 128], ident_b[:])
            P.copy("act", hmT, pT[:].rearrange("p (c n) -> p c n", c=8))
            pT = ptb()
            for c in range(2):
                P.transpose(pT[:, c * 128:(c + 1) * 128], plb[:, c * 128:(c + 1) * 128], ident_b[:])
            P.copy("act", plT, pT[:, 0:256].rearrange("p (c n) -> p c n", c=2))

        def stageB(t):
            sl = slice(t * 128, (t + 1) * 128)
            p2 = t % 2
            xt = xin[p2]
            hm, plt, plb, plT, hmT, ubp = hms[p2], plts[p2], plbs[p2], plTs[p2], hmTs[p2], ubs[p2]
            gt = xt
            for hf in range(2):
                pg = [psC, psD][hf]
                pp = [psE, psF][hf]
                for c in range(8):
                    P.matmul(pg[:, 0:512], hmT[:, c, :], WB[:, c, hf * 512:(hf + 1) * 512], start=(c == 0), stop=(c == 7))
                for c in range(2):
                    P.matmul(pp[:, 0:512], plT[:, c, :], WC[:, c, hf * 512:(hf + 1) * 512], start=(c == 0), stop=(c == 1))
                P.act(gt[:, hf * 512:(hf + 1) * 512], pg[:, 0:512], AF.Sigmoid)
                P.tt("dve", gt[:, hf * 512:(hf + 1) * 512], gt[:, hf * 512:(hf + 1) * 512], pp[:, 0:512], ALU.mult)
            P.tt("dve", hm, hm, gt, ALU.add)
            if not last:
                P.dma("sp", h1[sl, :], hm)
            else:
                P.act(junk, hm, AF.Square, accum_out=ssq[:, t:t + 1])
                P.act(rs[:, t:t + 1], ssq[:, t:t + 1], AF.Sqrt, bias=epsb[:, 0:1], scale=1.0 / D)
                recip(rs[:, t:t + 1], rs[:, t:t + 1])
                P.stt(gt, hm, rs[:, t:t + 1], gbc, ALU.mult, ALU.mult)
                P.dma("sp", y[sl, :], gt)

        stageA(0)
        stageA2(0)
        for t in range(NT):
            if t + 1 < NT:
                stageA(t + 1)
            stageB(t)
            if t + 1 < NT:
                stageA2(t + 1)

    strip = sb("strip", [128, 8, 256], BF16)
    band = sb("band", [128, 8, 128], BF16)
    bands = sb("bands", [128, 8, 8], BF16)
    expand = sb("expand", [128, T], BF16)
    cover = sb("cover", [127, 34], BF16)
    zpad = sb("zpad", [128, 270], BF16)
    m1c1 = aF[:, 5704:5836].rearrange("p (a b) -> p a b", a=2)
    m1s = sb("m1s", [8, 33], F32)
    c1s = sb("c1s", [8, 33], F32)
    acm = sb("acm", [128, 128], BF16)
    tb = sb("tb", [33, 8], F32)
    r31 = sb("r31", [32, 8], F32)
    ohs = aF[0:33, 6000:6383]
    tbh = aF[0:33, 6400:6528]
    w2 = sb("w2", [128, 2, 64], BF16)
    cvec = sb("cvec", [128, 2], F32)
    peraw = aF[0:32, 5576:5704].rearrange("p (kv d) -> p kv d", kv=2)
    peT = sb("peT", [64, 2, 32], BF16)
    Vs16 = aB[:, 7424:7820].rearrange("p (a b) -> p a b", a=3)
    idx = sb("idx", [128, 256], I32)
    idxf = aF[:, 6400:6656]
    iot = sb("iot", [128, 1], F32)
    smallf = aF[:, 5836:5964]

    def AP(t, off, pat):
        return bass.AP(t.tensor, off, pat)

    def setup_attn():
        P.memset("pool", expand[32:64, :], 0.0)
        P.memset("pool", expand[64:128, :], 0.0)
        P.memset("pool", zpad[:], 0.0)
        P.memset("pool", band[:], 0.0)
        P.memset("pool", bands[:], 0.0)
        P.dma("sp", expand[0:33, :], c_expand)
        P.dma("sp", cover[:], c_cover)
        P.dma("sp", zpad[0:16, :], c_zpad)
        P.dma("sp", m1s[:], c_m1s)
        P.dma("sp", c1s[:], c_c1s)
        P.dma("sp", acm[:], c_ac)
        P.dma("sp", ohs, c_oh)
        P.dma("sp", iot[:], c_iota)
        P.dma("sp", tb[0:32, :], rel_bias)
        P.dma("sp", r31[:], rel_bias[31:32, :].broadcast_to([32, 8]))
        P.tt("dve", tb[0:32, :], tb[0:32, :], r31[:], ALU.subtract)
        P.memset("dve", tb[32:33, :], -BIG)
        gsb = aF[:, 3000:3383]
        for h in range(8):
            P.ts("dve", tbh, ones_f[0:33, :], tb[:, h:h + 1], None, ALU.mult)
            P.matmul(psD[:, 0:383], tbh, ohs, start=True, stop=True)
            P.copy("act", gsb, psD[:, 0:383])
            P.dma("sp", AP(G1, h * 128 * 512, [[513, 128], [1, 383]]), gsb)
            P.dma("sp", AP(G16, h * 16 * 640, [[656, 16], [1, 383]]), gsb[0:16, :])
        strip_f = aF[:, 3400:5448].rearrange("p (h x) -> p h x", h=8)
        band_f = aF[0:16, 5448:6472].rearrange("p (h x) -> p h x", h=8)
        bands_f = aF[0:16, 6472:6536].rearrange("p (h x) -> p h x", h=8)
        P.dma("sp", strip_f, AP(G1, 127, [[512, 128], [65536, 8], [1, 256]]))
        P.dma("sp", band_f, AP(G16, 240, [[640, 16], [10240, 8], [1, 128]]))
        P.dma("sp", bands_f, AP(G16, 352, [[640, 16], [10240, 8], [1, 8]]))
        P.copy("dve", strip[:], strip_f)
        P.copy("dve", band[0:16], band_f)
        P.copy("dve", bands[0:16], bands_f)
        P.dma("sp", idx[:], ptab[0:1, :].broadcast_to([128, 256]))
        P.copy("dve", idxf, idx[:])
        P.ts("dve", idxf, idxf, 128.0, iot[:, 0:1], ALU.mult, ALU.add)
        P.copy("dve", idx[:], idxf)

    w1 = WA[:].rearrange("p c n -> p (c n)").rearrange("p (kv pos h) -> p kv pos h", kv=2, pos=32)
    kcT = aB[:, 6400:6528]
    vcaug = aB[0:127, 6528:6724].rearrange("p (g e) -> p g e", g=2)
    hid = aB[:, 6724:6852]
    hid = aB[:, 7820:8076]

    def load_cmp_weights(l):
        for kv in range(2):
            for cp in range(2):
                P.dma("pool", w1[cp * 64:(cp + 1) * 64, kv, :, :],
                      cmp_w1[l, kv].rearrange("(pos d) h -> d pos h", d=64))
        P.dma("pool", w2[:], cmp_w2[l].rearrange("kv h d -> h kv d"))
        P.dma("sp", peraw, cmp_pe[l].rearrange("kv pos d -> pos kv d"))
        for kv in range(2):
            P.transpose(psD[0:64, kv * 32:(kv + 1) * 32], peraw[0:32, kv, :], ident_f[0:32, 0:32])
        P.copy("dve", peT[:], psD[0:64, 0:64].rearrange("p (kv pos) -> p kv pos", kv=2))
        for kv in range(2):
            for pos in range(32):
                P.matmul(psD[:, 64 + kv:65 + kv], w1[0:64, kv, pos, :], peT[:, kv, pos:pos + 1],
                         start=(pos == 0), stop=(pos == 31))
        P.copy("dve", cvec[:], psD[:, 64:66])

    cmp_tiles = [kcT, vcaug, hid]

    def compress(Ksrc, Vsrc):
        kcT, vcaug, hid = cmp_tiles
        for g in range(2):
            P.copy("pool", vcaug[:, g, 64:98], cover[:, :])
        for kv in range(2):
            src = Ksrc if kv == 0 else Vsrc
            pg = [psA, psB] if kv == 0 else [psC, psA]
            for pos in range(32):
                for g in range(2):
                    gb = g * 64
                    P.matmul(pg[g][:, 0:127], w1[gb:gb + 64, kv, pos, :], src[gb:gb + 64, pos:pos + 16 * 126 + 1:16],
                             start=(pos == 0), stop=(pos == 31))
            for g in range(2):
                P.act(hid[:, g * 127:(g + 1) * 127], pg[g][:, 0:127], AF.Silu, bias=cvec[:, kv:kv + 1])
            for g in range(2):
                gb = g * 64
                hg = hid[:, g * 127:(g + 1) * 127]
                pw = rr()
                if kv == 0:
                    P.matmul(pw[gb:gb + 64, 0:127], w2[:, 0, :], hg, start=True, stop=True)
                    P.copy("dve", kcT[gb:gb + 64, 0:127], pw[gb:gb + 64, 0:127])
                else:
                    P.matmul(pw[0:127, 128 + g * 64:192 + g * 64], hg, w2[:, 1, :], start=True, stop=True)
                    P.copy("dve", vcaug[:, g, 0:64], pw[0:127, 128 + g * 64:192 + g * 64])

    es = [aB[:, i * 512:(i + 1) * 512] for i in range(3)]
    ecmp = aB[:, 1536:2048]
    negT = aB[:, 2048:6400].rearrange("p (g t) -> p g t", g=2)
    negb2 = [aB[:, 6852:6885], aB[:, 8076:8109]]
    attnb = aB[:, 6912:7424]
    accb = aF[:, 0:2048].rearrange("p (q e) -> p q e", q=4)
    cmpo = aF[:, 2048:2440].rearrange("p (r e) -> p r e", r=4)
    tmpo = aF[:, 2440:2960].rearrange("p (q e) -> p q e", q=4)
    rinv = smallf[:, 0:4]
    coef = smallf[:, 4:8]
    imp = smallf[:, 8:41]
    m8 = smallf[:, 48:56]
    ecnt = [0]
    ecmp_ref = [ecmp]

    def cmp_branch(rows, nc_, g, qcols, band_lhsT, band_rhs_of_h, gate_ap, acc_of_h, m1, c1, negT_out, negb, deferred):
        gb = g * 64
        kcT, vcaug, hid = cmp_tiles
        for r in range(4):
            h = g * 4 + r
            P.matmul(psG[0:nc_, r * rows:(r + 1) * rows], kcT[gb:gb + 64, 0:nc_], qcols(r), start=True, stop=False)
            P.matmul(psG[0:nc_, r * rows:(r + 1) * rows], band_lhsT, band_rhs_of_h(h), start=False, stop=True)
        ecm = ecmp_ref[0]
        P.act(ecm[0:nc_, 0:4 * rows], psG[0:nc_, 0:4 * rows], AF.Exp)
        for r in range(4):
            P.matmul(psF[0:rows, r * 98:(r + 1) * 98], ecm[0:nc_, r * rows:(r + 1) * rows], vcaug[0:nc_, g, :],
                     start=True, stop=True)
        P.copy("act", cmpo[0:rows], psF[0:rows, 0:392].rearrange("p (r e) -> p r e", r=4))
        P.ts("dve", rinv[0:rows], cmpo[0:rows, :, 97], 1e-30, None, ALU.add)
        recip(rinv[0:rows], rinv[0:rows])
        P.ts("dve", imp[0:rows], cmpo[0:rows, 0, 64:97], rinv[0:rows, 0:1], None, ALU.mult)
        for r in range(1, 4):
            P.stt(imp[0:rows], cmpo[0:rows, r, 64:97], rinv[0:rows, r:r + 1], imp[0:rows], ALU.mult, ALU.add)
        P.tt("dve", coef[0:rows], rinv[0:rows], gate_ap, ALU.mult)
        for r in range(4):
            P.ts("dve", acc_of_h(g * 4 + r), cmpo[0:rows, r, 0:64], coef[0:rows, r:r + 1], None, ALU.mult)
        P.tt("dve", imp[0:rows], imp[0:rows], m1, ALU.mult)
        P.tt("dve", imp[0:rows], imp[0:rows], c1, ALU.add)
        P.add("dve", lambda e: e.max(out=m8[0:rows], in_=imp[0:rows]), reads=[imp[0:rows]], writes=[m8[0:rows]])
        P.ts("dve", negb[0:rows], imp[0:rows], m8[0:rows, 7:8], -BIG, ALU.is_lt, ALU.mult)
        def fin():
            pT = ptb()
            P.transpose(pT[0:33, 0:rows], negb[0:rows], ident_b[0:rows, 0:rows])
            P.copy("dve", negT_out[0:33], pT[0:33, 0:rows])
        deferred.append(fin)

    def finish_head(rows, nq, pso, width, ocol, scol, gate_of_q, acc_of_q):
        P.copy("act", tmpo[0:rows, 0:nq, 0:width], pso[0:rows, 0:nq * width].rearrange("p (q e) -> p q e", q=nq))
        P.ts("dve", rinv[0:rows, 0:nq], tmpo[0:rows, 0:nq, scol], 1e-30, None, ALU.add)
        recip(rinv[0:rows, 0:nq], rinv[0:rows, 0:nq])
        P.tt("dve", coef[0:rows, 0:nq], rinv[0:rows, 0:nq], gate_of_q, ALU.mult)
        for qi in range(nq):
            a = acc_of_q(qi)
            P.stt(a, tmpo[0:rows, qi, ocol:ocol + 64], coef[0:rows, qi:qi + 1], a, ALU.mult, ALU.add)

    def attention_prompt(l):
        cmp_tiles[:] = [kcT, vcaug, hid]
        ecmp_ref[0] = ecmp
        compress(KT[:, 0, 0:TP], VcT[:, 0:TP])
        yield
        P.memset("pool", negT[32:64], 0.0)
        P.memset("pool", negT[64:128], 0.0)
        pso2 = [psD, psE]
        dfr = []
        for qb in range(4):
            q0 = qb * 512
            for qi in range(4):
                qt = qb * 4 + qi
                qs = slice(qt * 128, (qt + 1) * 128)
                nc_ = min(127, 8 * qt + 7)
                c_off = 8 * qt - 9
                mc = m1c1[:, qt % 2, :]
                P.dma("sp", mc[:, 0:33], c_m1p[:, qt * 33:(qt + 1) * 33])
                P.dma("sp", mc[:, 33:66], c_c1p[:, qt * 33:(qt + 1) * 33])
                for g in range(2):
                    cmp_tiles[:] = [kcT, vcaug, hid]
                    ecmp_ref[0] = ecmp
                    cmp_branch(128, nc_, g,
                               lambda r, g=g, qs=qs: qT[g * 64:(g + 1) * 64, r, qs],
                               zpad[:, 127 - c_off:127 - c_off + nc_],
                               lambda h: band[:, h, :],
                               gs[:, qt, g * 4:g * 4 + 4],
                               lambda h, qi=qi: accb[:, qi, h * 64:(h + 1) * 64],
                               mc[:, 0:33], mc[:, 33:66],
                               negT[:, g, qs], negb2[g], dfr)
                yield
                for fn_ in dfr:
                    fn_()
                del dfr[:]
            for h in range(8):
                g, r = h // 4, h % 4
                gb = g * 64
                for br in (1, 2):
                    pso = pso2[br - 1]
                    P.memset("dve", pso[:, 0:260], 0.0)
                    kt_lo = 0 if br == 1 else max(0, 4 * qb - 4)
                    pend = None
                    for kt in range(kt_lo, 4 * qb + 4):
                        qi_lo = max(0, kt - 4 * qb)
                        qi_hi = 3 if br == 1 else min(3, kt + 4 - 4 * qb)
                        c_lo, c_hi = qi_lo * 128, (qi_hi + 1) * 128
                        pss = rr()
                        ks = slice(kt * 128, (kt + 1) * 128)
                        extra = []
                        if br == 1:
                            extra.append((c_lo, c_hi, expand[:, ks], negT[:, g, q0 + c_lo:q0 + c_hi]))
                        d0 = kt - 4 * qb
                        if 0 <= d0 <= 2:
                            extra.append((d0 * 128, d0 * 128 + 256, ident_b[:], strip[:, h, 0:256]))
                        elif d0 == 3:
                            extra.append((384, 512, ident_b[:], strip[:, h, 0:128]))
                        elif d0 == -1:
                            extra.append((0, 128, ident_b[:], strip[:, h, 128:256]))
                        if br == 2 and 0 <= d0 + 4 <= 3:
                            extra.append(((d0 + 4) * 128, (d0 + 5) * 128, ident_b[:], acm[:]))
                        P.matmul(pss[:, c_lo:c_hi], KT[gb:gb + 64, br, ks], qT[gb:gb + 64, r, q0 + c_lo:q0 + c_hi],
                                 start=True, stop=(len(extra) == 0))
                        for i, (a, b_, lt, rh) in enumerate(extra):
                            P.matmul(pss[:, a:b_], lt, rh, start=False, stop=(i == len(extra) - 1))
                        ecnt[0] += 1
                        e = es[ecnt[0] % 3]
                        P.act(e[:, c_lo:c_hi], pss[:, c_lo:c_hi], AF.Exp)

                        def pv(e=e, kt=kt, qi_lo=qi_lo, qi_hi=qi_hi):
                            for qi in range(qi_lo, qi_hi + 1):
                                P.matmul(pso[:, qi * 65:(qi + 1) * 65], e[:, qi * 128:(qi + 1) * 128],
                                         Vt[:, br, kt, g * 65:(g + 1) * 65], start=False, stop=False, skip_group_check=True)
                        if pend is not None:
                            pend()
                        pend = pv
                        if (kt - kt_lo) % 4 == 3:
                            yield
                    pend()
                    finish_head(128, 4, pso, 65, 0, 64,
                                gs[:, qb * 4:qb * 4 + 4, br * 8 + h],
                                lambda qi, h=h: accb[:, qi, h * 64:(h + 1) * 64])
                    yield
            for qi in range(4):
                qt = qb * 4 + qi
                P.tt("dve", attnb, accb[:, qi, :], sza[:, qt, :], ALU.mult)
                pT = ptb()
                for c in range(4):
                    P.transpose(pT[:, c * 128:(c + 1) * 128], attnb[:, c * 128:(c + 1) * 128], ident_b[:])
                P.copy("act", mixa[:, 0:4, qt * 128:(qt + 1) * 128], pT[:, 0:512].rearrange("p (c n) -> p c n", c=4))

    def attention_sample(l):
        WBf = WB[:].rearrange("p c n -> p (c n)")
        KT0 = KT[:, 0, :]
        cmpq = [KT0[:, 0:1024].rearrange("p (j c) -> p j c", j=4), KT0[:, 1024:2048].rearrange("p (j c) -> p j c", j=4)]
        slcc = WBf[:, 0:4160].rearrange("p (j c) -> p j c", j=16)
        winc = WBf[:, 4160:5200].rearrange("p (j c) -> p j c", j=4)
        KcT_s = uT[:, 4, 0:2048]
        VcT_s = uT[:, 5, 0:2048]
        KsT = uT[:, 6, 0:2048]
        KwT = uT[:, 7, 0:512]
        e_s = WBf[:, 5200:5744]
        ecmp_s = WBf[:, 5744:5808]
        negT_s = WBf[:, 5808:5872].rearrange("p (g r q) -> p g r q", g=2, r=4)
        ac32 = WBf[:, 5872:5904].rearrange("p (r q) -> p r q", r=4)
        Vnew = WBf[0:8, 5904:6300].rearrange("p (a b) -> p a b", a=3)
        szab = WBf[0:8, 6300:6812]
        attn_s = WBf[0:8, 6812:7324]
        kcT_s = WBf[:, 7324:7452]
        vcaug_s = WBf[0:127, 7452:7648].rearrange("p (g e) -> p g e", g=2)
        hid_s = WBf[:, 7648:7904]
        negb_s = [WBf[:, 7904:7937], WBf[:, 7940:7973]]
        dfs = []
        accs = aF[0:8, 3200:3712]
        gsb_b = aF[0:8, 2960:2984]
        o32 = aF[0:32, 2984:3113]
        on32 = aF[0:32, 3113:3177]
        r32 = aF[0:32, 3177:3178]
        if l == 1:
            P.copy("dve", idxf, idx[:])
            P.ts("dve", idxf, idxf, float(npool * 128), None, ALU.add)
            P.copy("dve", idx[:], idxf)
        P.memset("pool", slcc[:, :, 256:260], 1.0)
        P.memset("pool", winc[:, :, 256:260], 1.0)
        P.memset("pool", negT_s[32:64], 0.0)
        P.memset("pool", negT_s[64:128], 0.0)
        for r in range(4):
            P.copy("pool", ac32[:, r, :], acm[:, 0:8])
        for b in range(16):
            tk = slice(TP + b * 8, TP + b * 8 + 8)
            def gather(dst_ap, srcd, col):
                def f(e):
                    return e.indirect_dma_start(out=dst_ap, out_offset=None, in_=srcd.rearrange("l r c -> (l r) c"),
                                                in_offset=bass.IndirectOffsetOnAxis(idx[:, col:col + 1], 0))
                P.add("pool", f, reads=[idx[:, col:col + 1]], writes=[dst_ap], is_dma=True)
            for j in range(16):
                gather(slcc[:, j, 0:256], cs, b * 16 + j)
            P.dma("pool", winc[:, :, 0:256], swin[l, b].rearrange("(j p) c -> p j c", p=128))
            P.dma("sp", Vnew, Vs16[b * 8:(b + 1) * 8])
            P.dma("sp", szab, sza[b * 8:(b + 1) * 8, 16, :])
            P.dma("sp", gsb_b, gs[b * 8:(b + 1) * 8, 16, :])
            for qtr in range(4):
                cq = cmpq[qtr % 2]
                if not (b > 0 and qtr < 2):
                    for jj in range(4):
                        gather(cq[:, jj, :], cc, b * 16 + qtr * 4 + jj)
                pT = ptb()
                for jj in range(4):
                    P.transpose(pT[:, jj * 128:(jj + 1) * 128], cq[:, jj, 0:128], ident_b[:])
                    P.transpose(pT[:, (4 + jj) * 128:(5 + jj) * 128], cq[:, jj, 128:256], ident_b[:])
                P.copy(alt(), KcT_s[:, qtr * 512:(qtr + 1) * 512], pT[:, 0:512])
                P.copy(alt(), VcT_s[:, qtr * 512:(qtr + 1) * 512], pT[:, 512:1024])
            for (src_t, c0, dstT, nj) in ((slcc, 0, KsT, 16), (winc, 0, KwT, 4)):
                for j0 in range(0, nj, 8):
                    n8 = min(8, nj - j0)
                    pT = ptb()
                    for j in range(j0, j0 + n8):
                        P.transpose(pT[:, (j - j0) * 128:(j - j0 + 1) * 128], src_t[:, j, c0:c0 + 128], ident_b[:])
                    P.copy(alt(), dstT[:, j0 * 128:(j0 + n8) * 128], pT[:, 0:n8 * 128])
            yield
            cmp_tiles[:] = [kcT_s, vcaug_s, hid_s]
            ecmp_ref[0] = ecmp_s
            compress(KcT_s, VcT_s)
            yield
            cmp_tiles[:] = [kcT_s, vcaug_s, hid_s]
            ecmp_ref[0] = ecmp_s
            for g in range(2):
                cmp_branch(8, 127, g,
                           lambda r, g=g: qT[g * 64:(g + 1) * 64, r, tk],
                           zpad[:, 15:142],
                           lambda h: bands[:, h, :],
                           gsb_b[:, g * 4:g * 4 + 4],
                           lambda h: accs[:, h * 64:(h + 1) * 64],
                           m1s[:], c1s[:],
                           negT_s[:, g, 0, :], negb_s[g], dfs)
            if b + 1 < 16:
                for qtr in range(2):
                    for jj in range(4):
                        gather(cmpq[qtr][:, jj, :], cc, (b + 1) * 16 + qtr * 4 + jj)
            yield
            for fn_ in dfs:
                fn_()
            del dfs[:]
            for g in range(2):
                for r in range(1, 4):
                    P.copy("pool", negT_s[0:33, g, r, :], negT_s[0:33, g, 0, :])
            for g in range(2):
                gb = g * 64
                q32 = qT[gb:gb + 64, 0:4, tk]
                for br in (1, 2):
                    pss = rr()
                    KpT = KsT if br == 1 else KwT
                    cch = slcc if br == 1 else winc
                    nj = 16 if br == 1 else 4
                    for j in range(nj):
                        cs_ = slice(j * 32, (j + 1) * 32)
                        extra = []
                        if br == 1:
                            extra.append((expand[:, j * 128:(j + 1) * 128], negT_s[:, g, :, :]))
                        if br == 2 and j == 0:
                            extra.append((ident_b[:], ac32))
                        if j == nj - 1:
                            extra.append((ident_b[:], strip[:, g * 4:(g + 1) * 4, 128:136]))
                        P.matmul(pss[:, cs_], KpT[gb:gb + 64, j * 128:(j + 1) * 128], q32, start=True, stop=(len(extra) == 0))
                        for i, (lt, rh) in enumerate(extra):
                            P.matmul(pss[:, cs_], lt, rh, start=False, stop=(i == len(extra) - 1))
                    P.matmul(psG[0:8, 0:32], KT[gb:gb + 64, br, tk], q32, start=True, stop=False)
                    P.matmul(psG[0:8, 0:32], ident_b[:, 0:8], strip[:, g * 4:(g + 1) * 4, 0:8], start=False, stop=True)
                    P.act(e_s[:, 0:nj * 32], pss[:, 0:nj * 32], AF.Exp)
                    P.act(e_s[0:8, 512:544], psG[0:8, 0:32], AF.Exp)
                    pso = rr()
                    for j in range(nj):
                        P.matmul(pso[0:32, 0:129], e_s[:, j * 32:(j + 1) * 32], cch[:, j, 128:257], start=(j == 0), stop=False)
                    P.matmul(pso[0:32, 0:129], e_s[0:8, 512:544], Vnew[:, br, 0:129], start=False, stop=True)
                    P.copy("act", o32, pso[0:32, 0:129])
                    P.ts("dve", r32, o32[:, 128:129], 1e-30, None, ALU.add)
                    recip(r32, r32)
                    P.ts("dve", on32, o32[:, g * 64:(g + 1) * 64], r32[:, 0:1], None, ALU.mult)
                    for r in range(4):
                        P.matmul(psF[0:8, r * 64:(r + 1) * 64], ident_f[0:32, r * 8:(r + 1) * 8], on32, start=True, stop=True)
                    for r in range(4):
                        h = g * 4 + r
                        a = accs[:, h * 64:(h + 1) * 64]
                        P.stt(a, psF[0:8, r * 64:(r + 1) * 64], gsb_b[:, br * 8 + h:br * 8 + h + 1], a, ALU.mult, ALU.add)
                    yield
            P.tt("dve", attn_s, accs, szab, ALU.mult)
            pT = ptb()
            for c in range(4):
                P.transpose(pT[:, c * 8:(c + 1) * 8], attn_s[:, c * 128:(c + 1) * 128], ident_b[0:8, 0:8])
            P.copy("act", mixa[:, 0:4, tk], pT[:, 0:32].rearrange("p (c n) -> p c n", c=4))
            yield

    k.phase1 = phase1
    k.__dict__.update(locals())
    return k


def rel_bucket_np(dist):
    n = np.maximum(dist, 0)
    nf = np.maximum(n, 1).astype(np.float32)
    large = 16 + (np.log(nf / np.float32(16)) / np.float32(np.log(128 / 16)) * np.float32(16)).astype(np.int32)
    large = np.minimum(large, 31)
    return np.where(n < 16, n, large)


def make_consts():
    bf = ml_dtypes.bfloat16
    c = {}
    c["c_ident"] = np.eye(128, dtype=np.float32)
    d = np.arange(-127, 256)
    oh = np.zeros((33, 383), np.float32)
    b = rel_bucket_np(d)
    for i, dd in enumerate(d):
        if dd < 0:
            oh[32, i] = 1.0
        else:
            oh[b[i], i] = 1.0
    c["c_oh"] = oh
    pos = np.arange(T)
    ex = np.zeros((33, T), np.float32)
    ex[np.minimum(pos // 64, 32), pos] = 1.0
    c["c_expand"] = ex.astype(bf)
    cidx = np.arange(127)
    c_start = cidx * 16
    c_end = c_start + 31
    s_start = np.arange(33) * 64
    cover = ((c_start[:, None] < s_start[None, :] + 64) & (c_end[:, None] >= s_start[None, :])).astype(np.float32)
    c["c_cover"] = np.concatenate([cover, np.ones((127, 1), np.float32)], axis=1).astype(bf)
    zp = np.zeros((16, 270), np.float32)
    zp[np.arange(16), 127 + np.arange(16)] = 1.0
    c["c_zpad"] = zp.astype(bf)
    qpos = np.arange(2048)
    cur = qpos // 64
    blk = np.arange(33)
    forced = (blk[None] == 0) | (blk[None] == cur[:, None]) | (blk[None] == cur[:, None] - 1)
    allowed = blk[None] <= cur[:, None]
    m1 = (allowed & ~forced).astype(np.float32)
    c1 = np.where(allowed, np.where(forced, 1e4, 0.0), -1e4).astype(np.float32)
    c["c_m1p"] = m1.reshape(16, 128, 33).transpose(1, 0, 2).reshape(128, 16 * 33).copy()
    c["c_c1p"] = c1.reshape(16, 128, 33).transpose(1, 0, 2).reshape(128, 16 * 33).copy()
    qs = 2048 + np.arange(8)
    curs = qs // 64
    forced = (blk[None] == 0) | (blk[None] == curs[:, None]) | (blk[None] == curs[:, None] - 1)
    allowed = blk[None] <= curs[:, None]
    c["c_m1s"] = (allowed & ~forced).astype(np.float32)
    c["c_c1s"] = np.where(allowed, np.where(forced, 1e4, 0.0), -1e4).astype(np.float32)
    c["c_iota"] = np.arange(128, dtype=np.float32).reshape(128, 1)
    kl = np.arange(128)
    c["c_ac"] = np.where(kl[None, :] >= kl[:, None], -BIG, 0.0).astype(np.float32).astype(bf)
    return c


def core_inputs(inp, c, npool_rows=None):
    m = {}
    m["x_tok"] = np.concatenate([inp["x_prompt"][c], inp["x_sample"][16 * c:16 * c + 16].reshape(128, D)], 0)
    m["ple_tok"] = np.concatenate([inp["p_prompt"][:, c], inp["p_sample"][:, 16 * c:16 * c + 16].reshape(2, 128, 256)], 1)
    m["cc"] = inp["cache_cmp_kv"].reshape(2, -1, 256)
    m["cs"] = inp["cache_slc_kv"].reshape(2, -1, 256)
    m["ptab"] = inp["page_table"][16 * c:16 * c + 16].reshape(1, 256).astype(np.int32)
    m["swin"] = inp["state_win_kv"][:, 16 * c:16 * c + 16].reshape(2, 16, 512, 256)
    m["sconv"] = inp["state_conv"][:, 16 * c:16 * c + 16].reshape(2, 480, 512)
    m["norm_g"] = inp["norm_g"]
    m["w_in"] = inp["w_in"]
    m["convp"] = np.concatenate([inp["conv_w"], inp["conv_b"][:, None], inp["conv_ln_g"][:, None],
                                 inp["conv_ln_b"][:, None]], 1)
    m["cmp_pe"] = inp["cmp_pe"]
    m["cmp_w1"] = inp["cmp_w1"].reshape(2, 2, 2048, 128)
    m["cmp_w2"] = inp["cmp_w2"]
    m["w_out"] = inp["w_out"]
    m["w_ple"] = inp["w_ple"]
    m["w_pg"] = inp["w_ple_gate"]
    m["rel_bias"] = inp["rel_bias"]
    m["fng"] = inp["final_norm_g"].reshape(1, D)
    return {k_: np.ascontiguousarray(v) for k_, v in m.items()}


STAGE = 7


def program(k, stage=99):
    srcs = [k.x_tok, k.h1]
    for l in range(2):
        k.phase1(l, srcs[l])
        k.phase2a(l)
        k.phase3(l)
        k.phase2b(l)
        if stage >= 6:
            if l == 0:
                k.setup_attn()
            k.load_cmp_weights(l)
            gp = k.attention_prompt(l)
            if stage >= 7:
                gsm = k.attention_sample(l)
                next(gp)
                gens = [gp, gsm]
                weight = [4500.0, 5800.0]
                cum = [0.0, 0.0]
                alive = [True, True]
                while any(alive):
                    cand = [i for i in range(2) if alive[i]]
                    i = min(cand, key=lambda i_: cum[i_] / weight[i_])
                    n0 = len(k.P.eng_ops["pe"])
                    try:
                        next(gens[i])
                    except StopIteration:
                        alive[i] = False
                    cum[i] += len(k.P.eng_ops["pe"]) - n0 + 1
            else:
                for _ in gp:
                    pass
                k.P.memset("pool", k.uT[:, 0:4, 2048:2176], 0.0)
        else:
            k.P.memset("pool", k.uT[:, 0:4, :], 0.0)
        k.phase5(l, srcs[l], l == 1)


_CACHE = {}


def kernel(**inp):
    npool = inp["cache_cmp_kv"].shape[1]
    if "nc" not in _CACHE:
        nc = bass.Bass("TRN2", target_bir_lowering=False)
        k = build(nc, npool, stage=STAGE)
        program(k, STAGE)
        k.P.emit()
        _CACHE["nc"] = nc
    nc = _CACHE["nc"]
    consts = make_consts()
    in_maps = []
    for c in range(8):
        m = core_inputs(inp, c)
        if STAGE < 6:
            m.pop("cc")
            m.pop("cs")
        m.update(consts)
        in_maps.append(m)
    res = run_bass_kernel_spmd(nc, in_maps, core_ids=list(range(8)))
    r = res.results
    y_p = np.stack([r[c]["y"][:TP] for c in range(8)])
    y_s = np.concatenate([r[c]["y"][TP:].reshape(16, 8, D) for c in range(8)])

    def kvp(name):
        return np.stack([r[c][name][:, :TP] for c in range(8)], 1).reshape(2, 8, TP, 2, 2, 64)

    def kvs(name):
        return np.concatenate([r[c][name][:, TP:].reshape(2, 16, 8, 256) for c in range(8)], 1).reshape(2, 128, 8, 2, 2, 64)
    win_p = np.stack([r[c]["nwin_p"] for c in range(8)], 1).reshape(2, 8, 512, 2, 2, 64)
    win_s = np.concatenate([r[c]["nwin_s"] for c in range(8)], 1).reshape(2, 128, 512, 2, 2, 64)
    conv_p = np.stack([r[c]["nconv_p"] for c in range(8)], 1)
    conv_s = np.concatenate([r[c]["nconv_s"] for c in range(8)], 1)
    f = lambda a: np.ascontiguousarray(a, dtype=np.float32)
    return (f(y_p), f(y_s), f(kvp("ncmp")), f(kvs("ncmp")), f(kvp("nslc")), f(kvs("nslc")),
            f(win_p), f(win_s), f(conv_p), f(conv_s))
```

```python
import contextlib
import numpy as np
import ml_dtypes
import concourse.bass as bass
import concourse.mybir as mybir
from concourse.bass_utils import run_bass_kernel_spmd

F32 = mybir.dt.float32
BF16 = mybir.dt.bfloat16
I32 = mybir.dt.int32
ALU = mybir.AluOpType
AF = mybir.ActivationFunctionType

ENGS = ("pe", "act", "dve", "pool", "sp")
NSLOT = {"sp": 28, "pool": 16, "act": 4}
BIG = 30000.0


def _dsize(dt):
    return mybir.dt.size(dt)


def _region(ap):
    t = ap.tensor
    space = str(ap.space).upper()
    name = t.name
    if "DRAM" in space or "HBM" in space:
        f0 = ap.offset
        ext = 0
        for st_, cnt in ap.ap:
            ext += (cnt - 1) * abs(st_)
        return ("D", name, 0, 1, f0, f0 + ext + 1)
    shp = list(t.shape)
    pstride = 1
    for s in shp[1:]:
        pstride *= s
    off = ap.offset
    p0 = off // pstride
    f0 = off % pstride
    aps = list(ap.ap)
    pcount = 1
    ext = 0
    for i, (st, cnt) in enumerate(aps):
        if i == 0:
            pcount = cnt if st == pstride or cnt == 1 else cnt
            if st != pstride and cnt > 1 and st != 0:
                pcount = 1
                ext += (cnt - 1) * abs(st)
            continue
        ext += (cnt - 1) * abs(st)
    f1 = f0 + ext + 1
    if "PSUM" in space:
        per_bank = 2048 // _dsize(t.dtype)
        f0 = (f0 // per_bank) * per_bank
        f1 = ((f1 + per_bank - 1) // per_bank) * per_bank
        return ("P", name, 0, 128, f0, f1)
    return ("S", name, p0, p0 + pcount, f0, f1)


def _overlap(a, b):
    return a[1] == b[1] and a[2] < b[3] and b[2] < a[3] and a[4] < b[5] and b[4] < a[5]


def _covers(a, b):
    return a[1] == b[1] and a[2] <= b[2] and a[3] >= b[3] and a[4] <= b[4] and a[5] >= b[5]


class Op:
    __slots__ = ("eng", "fn", "idx", "eidx", "deps", "is_dma", "dma_k", "signal", "waits", "sigval")


class Prog:
    def __init__(self, nc):
        self.nc = nc
        self.ops = []
        self.eng_ops = {e: [] for e in ENGS}
        self.track = {}
        self.dma_count = {e: 0 for e in ENGS}

    def add(self, eng, fn, reads=(), writes=(), is_dma=False):
        op = Op()
        op.eng = eng
        op.fn = fn
        op.idx = len(self.ops)
        op.eidx = len(self.eng_ops[eng])
        op.deps = set()
        op.is_dma = is_dma
        op.dma_k = None
        if is_dma:
            op.dma_k = self.dma_count[eng]
            self.dma_count[eng] += 1
        op.signal = False
        self.ops.append(op)
        self.eng_ops[eng].append(op)
        for a in reads:
            if a is None or isinstance(a, (int, float)):
                continue
            r = _region(a)
            self._access(op, r, r[0] == "P")
        for a in writes:
            if a is None:
                continue
            self._access(op, _region(a), True)
        return op

    def _access(self, op, reg, is_write):
        lst = self.track.get(reg[1])
        if lst is None:
            self.track[reg[1]] = [[reg, op, is_write]]
            return
        new = []
        for ent in lst:
            ereg, eop, ew = ent
            if eop is op:
                new.append(ent)
                continue
            if _overlap(ereg, reg):
                if is_write or ew:
                    op.deps.add(eop)
                    if is_write and _covers(reg, ereg):
                        continue
                    new.append(ent)
                else:
                    if (not eop.is_dma) and (not op.is_dma) and eop.eng == op.eng and ereg == reg:
                        continue
                    new.append(ent)
            else:
                new.append(ent)
        new.append([reg, op, is_write])
        self.track[reg[1]] = new

    def matmul(self, out, lhsT, rhs, start=True, stop=True, **kw):
        return self.add("pe", lambda e: e.matmul(out, lhsT, rhs, start=start, stop=stop, **kw),
                        reads=[lhsT, rhs], writes=[out])

    def transpose(self, out, in_, ident):
        return self.add("pe", lambda e: e.transpose(out, in_, ident), reads=[in_, ident], writes=[out])

    def act(self, out, in_, func, bias=None, scale=None, accum_out=None):
        kw = {}
        rd = [in_]
        if bias is not None:
            kw["bias"] = bias
            rd.append(bias)
        if scale is not None:
            kw["scale"] = scale
            rd.append(scale)
        wr = [out]
        if accum_out is not None:
            kw["accum_out"] = accum_out
            wr.append(accum_out)
        return self.add("act", lambda e: e.activation(out, in_, func, **kw), reads=rd, writes=wr)

    def tt(self, eng, out, in0, in1, op):
        return self.add(eng, lambda e: e.tensor_tensor(out, in0, in1, op), reads=[in0, in1], writes=[out])

    def ts(self, eng, out, in0, s1, s2, op0, op1=None):
        kw = {}
        if op1 is not None:
            kw["op1"] = op1
        return self.add(eng, lambda e: e.tensor_scalar(out, in0, s1, s2, op0, **kw),
                        reads=[in0, s1, s2], writes=[out])

    def stt(self, out, in0, scalar, in1, op0, op1):
        return self.add("dve", lambda e: e.scalar_tensor_tensor(out, in0, scalar, in1, op0, op1),
                        reads=[in0, in1, scalar], writes=[out])

    def copy(self, eng, out, in_):
        if eng == "act":
            return self.add(eng, lambda e: e.copy(out, in_), reads=[in_], writes=[out])
        return self.add(eng, lambda e: e.tensor_copy(out, in_), reads=[in_], writes=[out])

    def memset(self, eng, out, val):
        return self.add(eng, lambda e: e.memset(out, val), reads=[], writes=[out])

    def dma(self, queue, out, in_, **kw):
        return self.add(queue, lambda e: e.dma_start(out=out, in_=in_, **kw),
                        reads=[in_], writes=[out], is_dma=True)

    def emit(self):
        nc = self.nc
        waited = {f: {e: -1 for e in ENGS} for f in ENGS}
        for op in self.ops:
            op.waits = []
        for op in self.ops:
            f = op.eng
            need = {}
            for d in op.deps:
                if d.is_dma:
                    op.waits.append(d)
                    continue
                e = d.eng
                if e == f and not op.is_dma:
                    if e in ("pe", "sp"):
                        continue
                    if op.eidx - d.eidx > 3:
                        continue
                if d.eidx > waited[f][e]:
                    if e not in need or need[e].eidx < d.eidx:
                        need[e] = d
            for e, d in need.items():
                waited[f][e] = d.eidx
                d.signal = True
                op.waits.append(d)
        for e in ENGS:
            c = 0
            for op in self.eng_ops[e]:
                if op.is_dma:
                    continue
                if op.signal:
                    c += 1
                    op.sigval = c
        dma_waited = {f: set() for f in ENGS}
        with contextlib.ExitStack() as st:
            sems = {e: st.enter_context(nc.semaphore("sem_" + e)) for e in ENGS}
            dsems = {}
            for q, n in NSLOT.items():
                if self.dma_count[q] > 0:
                    dsems[q] = [st.enter_context(nc.semaphore("dsem_%s_%d" % (q, i))) for i in range(n)]
            block = st.enter_context(nc.Block())

            def dma_slot(d):
                n = NSLOT[d.eng]
                return dsems[d.eng][d.dma_k % n], 16 * (d.dma_k // n + 1)

            def run_engine(ename, eng):
                final_wait = {}
                for op in self.eng_ops[ename]:
                    for d in op.waits:
                        if d.is_dma:
                            if d in dma_waited[ename]:
                                continue
                            dma_waited[ename].add(d)
                            s, v = dma_slot(d)
                            eng.wait_ge(s, v)
                        else:
                            eng.wait_ge(sems[d.eng], d.sigval)
                    if op.is_dma:
                        n = NSLOT[ename]
                        if op.dma_k >= n:
                            eng.wait_ge(dsems[ename][op.dma_k % n], 16 * (op.dma_k // n))
                        ins = op.fn(eng)
                        s, v = dma_slot(op)
                        ins.then_inc(s, 16)
                        final_wait[op.dma_k % n] = (s, v)
                    else:
                        ins = op.fn(eng)
                        if op.signal:
                            ins.then_inc(sems[ename], 1)
                for k, (s, v) in final_wait.items():
                    eng.wait_ge(s, v)

            @block.sync
            def _(eng):
                run_engine("sp", eng)

            @block.tensor
            def _(eng):
                run_engine("pe", eng)

            @block.scalar
            def _(eng):
                run_engine("act", eng)

            @block.vector
            def _(eng):
                run_engine("dve", eng)

            @block.gpsimd
            def _(eng):
                run_engine("pool", eng)


T = 2176
TP = 2048
NT = 17
D = 1024
DIN = 3352
EPS = 1e-6
NB5 = [(0, 512), (512, 512), (1024, 512), (1536, 512), (2048, 128)]


class K:
    pass


def build(nc, npool, nlayers=2, stage=99):
    k = K()
    k.nc = nc
    P = Prog(nc)
    k.P = P
    st = contextlib.ExitStack()
    k.st = st

    def din(name, shape, dt=F32):
        return nc.dram_tensor(name, list(shape), dt, kind="ExternalInput").ap()

    def dout(name, shape, dt=F32):
        return nc.dram_tensor(name, list(shape), dt, kind="ExternalOutput").ap()

    def dscr(name, shape, dt=F32):
        return nc.dram_tensor(name, list(shape), dt, kind="Internal").ap()

    def sb(name, shape, dt):
        return st.enter_context(nc.sbuf_tensor(name, list(shape), dt))

    def ps(name, shape, dt=F32):
        return st.enter_context(nc.psum_tensor(name, list(shape), dt))

    x_tok = din("x_tok", [T, D])
    ple_tok = din("ple_tok", [2, T, 256])
    if stage >= 6:
        cc = din("cc", [2, npool * 128, 256])
        cs = din("cs", [2, npool * 128, 256])
    ptab = din("ptab", [1, 256], I32)
    swin = din("swin", [2, 16, 512, 256])
    sconv = din("sconv", [2, 480, 512])
    norm_g = din("norm_g", [2, D])
    w_in = din("w_in", [2, D, DIN])
    convp = din("convp", [2, 34, 512])
    cmp_pe = din("cmp_pe", [2, 2, 32, 64])
    cmp_w1 = din("cmp_w1", [2, 2, 2048, 128])
    cmp_w2 = din("cmp_w2", [2, 2, 128, 64])
    w_out = din("w_out", [2, D, D])
    w_ple = din("w_ple", [2, 256, D])
    w_pg = din("w_pg", [2, D, D])
    rel_bias = din("rel_bias", [32, 8])
    fng = din("fng", [1, D])
    c_ident = din("c_ident", [128, 128])
    c_oh = din("c_oh", [33, 383])
    c_expand = din("c_expand", [33, T], BF16)
    c_cover = din("c_cover", [127, 34], BF16)
    c_zpad = din("c_zpad", [16, 270], BF16)
    c_m1p = din("c_m1p", [128, 16 * 33])
    c_c1p = din("c_c1p", [128, 16 * 33])
    c_m1s = din("c_m1s", [8, 33])
    c_c1s = din("c_c1s", [8, 33])
    c_iota = din("c_iota", [128, 1])
    c_ac = din("c_ac", [128, 128], BF16)

    y = dout("y", [T, D])
    ncmp = dout("ncmp", [2, T, 256])
    nslc = dout("nslc", [2, T, 256])
    nwin_p = dout("nwin_p", [2, 512, 256])
    nwin_s = dout("nwin_s", [2, 16, 512, 256])
    nconv_p = dout("nconv_p", [2, 30, 512])
    nconv_s = dout("nconv_s", [2, 16, 30, 512])
    h1 = dscr("h1", [T, D])
    G1 = dscr("G1", [8, 128, 512])
    G16 = dscr("G16", [8, 16, 640])

    ident_f = sb("ident_f", [128, 128], F32)
    ident_b = sb("ident_b", [128, 128], BF16)
    ones_f = sb("ones_f", [128, 128], F32)
    epsb = sb("epsb", [128, 1], F32)
    uT = sb("uT", [128, 8, T], BF16)
    WA = sb("WA", [128, 8, 1024], BF16)
    WB = sb("WB", [128, 8, 1024], BF16)
    bigG = sb("bigG", [128, 4, T], BF16)
    bigZ = sb("bigZ", [128, 4 * T], BF16)
    mixc = sb("mixc", [128, 4, T], BF16)
    KT = sb("KT", [128, 3, T], BF16)
    Vt = sb("Vt", [128, 3, 17, 130], BF16)
    gs = sb("gs", [128, 17, 24], F32)
    aF = sb("aF", [128, 7168], F32)
    aB = sb("aB", [128, 8192], BF16)
    ssq = sb("ssq", [128, NT], F32)
    rs = sb("rs", [128, NT], F32)
    cw = sb("cw", [128, 4, 34], F32)
    gluP = bigG
    gluS = KT[:].rearrange("p a t -> p (a t)")[:, 0:2432].rearrange("p (c b r) -> p c b r", c=4, b=16)
    WC = bigG[:, 0, 0:2048].rearrange("p (c n) -> p c n", c=2)
    VcT = aB[:, 0:2048]
    cwraw = aF[0:34, 6656:7168]
    zc = bigZ[:].rearrange("p (c t) -> p c t", c=4)
    sza = bigZ[:].rearrange("p (t e) -> p t e", e=512)
    qT = bigG
    xin = [aF[:, 0:1024], aF[:, 1024:2048]]
    gbc = aF[:, 2048:3072]
    sg = [aF[:, 3072:3584], aF[:, 3584:4096]]
    glu_tail = aF[:, 4096:5120].rearrange("p (c n) -> p c n", c=4)
    stage_t = [aF[:, 5120:5888], aF[:, 5888:6656]]
    dw = aF[:, 0:2048].rearrange("p (c n) -> p c n", c=4)
    sq = aF[:, 2048:2560]
    st_m = aF[:, 2560:3072]
    st_v = aF[:, 3072:3584]
    st_t = aF[:, 3584:4096]
    scraw = aF[:, 5120:5632]
    tailtok = aF[:, 5632:6656].rearrange("p (c n) -> p c n", c=2)
    Dk = aB[:, 0:3968].rearrange("p (k n) -> p k n", k=31)
    scb = aB[:, 3968:6016].rearrange("p (c n) -> p c n", c=4)
    ub = aB[:, 6016:7040]
    junk = aB[:, 7040:8064]

    psA = ps("psA", [128, 512])
    psB = ps("psB", [128, 512])
    psC = ps("psC", [128, 512])
    psT = ps("psT", [128, 1024], BF16)
    psD = ps("psD", [128, 512])
    psE = ps("psE", [128, 512])
    psF = ps("psF", [128, 512])
    psT2 = ps("psT2", [128, 1024], BF16)
    psG = psF
    rot3 = [psA, psB, psC]
    tcnt = [0]

    def ptb():
        tcnt[0] += 1
        return [psT, psT2][tcnt[0] % 2]

    P.dma("sp", ident_f[:], c_ident)
    P.copy("dve", ident_b[:], ident_f[:])
    P.memset("dve", ones_f[:], 1.0)
    P.memset("dve", epsb[:], EPS)
    P.memset("pool", Vt[:], 1.0)

    wrow = lambda w, l: w[l].rearrange("(c p) n -> p c n", p=128)
    cnt = [0]

    def rr():
        cnt[0] += 1
        return rot3[cnt[0] % 3]

    def alt():
        cnt[0] += 1
        return "act" if cnt[0] % 2 else "dve"

    def recip(out, in_):
        return P.add("dve", lambda e: e.reciprocal(out, in_), reads=[in_], writes=[out])

    def phase1(l, src):
        P.dma("sp", gbc[:], norm_g[l:l + 1, :].broadcast_to([128, D]))
        for t in range(NT):
            xt = xin[t % 2]
            P.dma("sp", xt, src[t * 128:(t + 1) * 128, :])
            P.act(junk[:], xt[:], AF.Square, accum_out=ssq[:, t:t + 1])
            P.act(rs[:, t:t + 1], ssq[:, t:t + 1], AF.Sqrt, bias=epsb[:, 0:1], scale=1.0 / D)
            recip(rs[:, t:t + 1], rs[:, t:t + 1])
            P.stt(ub[:], xt[:], rs[:, t:t + 1], gbc[:], ALU.mult, ALU.mult)
            pT = ptb()
            for c in range(8):
                P.transpose(pT[:, c * 128:(c + 1) * 128], ub[:, c * 128:(c + 1) * 128], ident_b[:])
            P.copy(alt(), uT[:, :, t * 128:(t + 1) * 128], pT[:].rearrange("p (c n) -> p c n", c=8))

    def fm_mm(pst, lhs_of_c, nb, m=128):
        n0, nn = NB5[nb]
        for c in range(8):
            P.matmul(pst[0:m, 0:nn], lhs_of_c(c), uT[:, c, n0:n0 + nn], start=(c == 0), stop=(c == 7))

    def phase2a(l):
        wl = wrow(w_in, l)
        P.memset("pool", gluP[:, :, 0:30], 0.0)
        P.dma("pool", WB[:, :, 0:512], wl[:, :, 0:512])
        P.dma("pool", WB[:, :, 512:1024], wl[:, :, 512:1024])
        P.dma("pool", WA[:, :, 0:512], wl[:, :, 1024:1536])
        for two in range(2):
            for c in range(8):
                P.dma("pool", WA[:, c, 512:1024].rearrange("p (j two d) -> p j two d", two=2, d=64)[:, :, two, :],
                      wl[:, c, 1536 + two * 256:1536 + two * 256 + 256].rearrange("p (j d) -> p j d", d=64))
        for ch in range(4):
            for nb in range(5):
                n0, nn = NB5[nb]
                pa, pb = [(psA, psB), (psC, psD)][nb % 2]
                fm_mm(pa, lambda c: WB[:, c, ch * 128:(ch + 1) * 128], nb)
                fm_mm(pb, lambda c: WB[:, c, 512 + ch * 128:512 + (ch + 1) * 128], nb)
                s = sg[nb % 2]
                P.act(s[:, 0:nn], pb[:, 0:nn], AF.Sigmoid)
                if nb < 4:
                    P.tt("dve", gluP[:, ch, 30 + n0:30 + n0 + nn], pa[:, 0:nn], s[:, 0:nn], ALU.mult)
                    if nb == 3:
                        P.tt("dve", glu_tail[:, ch, 0:128], pa[:, 384:512], s[:, 384:512], ALU.mult)
                else:
                    P.tt("dve", gluS[:, ch, :, 30:38], pa[:, 0:128].rearrange("p (b q) -> p b q", q=8),
                         s[:, 0:128].rearrange("p (b q) -> p b q", q=8), ALU.mult)
                    P.tt("dve", glu_tail[:, ch, 128:256], pa[:, 0:128], s[:, 0:128], ALU.mult)
        P.dma("pool", WB[:, :, 0:256], wl[:, :, 2048:2304])
        P.dma("pool", WB[:, :, 256:280], wl[:, :, 3328:3352])
        P.dma("pool", WB[:, :, 512:768], wl[:, :, 2304:2560])
        for ch in range(4):
            for nb in range(5):
                n0, nn = NB5[nb]
                pst = rr()
                fm_mm(pst, lambda c: WA[:, c, ch * 128:(ch + 1) * 128], nb)
                P.act(zc[:, ch, n0:n0 + nn], pst[:, 0:nn], AF.Silu)

    def tm_group(l, half, ncols, evac):
        for t in range(17):
            sl = slice(t * 128, (t + 1) * 128)
            pst = rr()
            for c in range(8):
                P.matmul(pst[:, 0:ncols], uT[:, c, sl], WB[:, c, half * 512:half * 512 + ncols],
                         start=(c == 0), stop=(c == 7))
            evac(t, sl, pst)

    def phase2b(l):
        wl = wrow(w_in, l)
        P.memset("pool", Vs16[:, :, 128:132], 1.0)
        for j in range(4):
            for nb in range(5):
                n0, nn = NB5[nb]
                pst = rr()
                fm_mm(pst, lambda c: WA[:, c, 512 + j * 128:512 + (j + 1) * 128], nb)
                P.act(qT[:, j, n0:n0 + nn], pst[:, 0:nn], AF.Copy, scale=0.125)
        outs = [ncmp, nslc, None]
        for br in range(3):
            half = br % 2
            c0 = 2048 + br * 256
            if br == 2:
                P.dma("pool", WB[:, :, 0:256], wl[:, :, c0:c0 + 256])
            for nb in range(5):
                n0, nn = NB5[nb]
                pst = rr()
                fm_mm(pst, lambda c: WB[:, c, half * 512:half * 512 + 128], nb)
                P.copy(alt(), KT[:, br, n0:n0 + nn], pst[:, 0:nn])
            if br == 0:
                for nb in range(4):
                    n0, nn = NB5[nb]
                    pst = rr()
                    fm_mm(pst, lambda c: WB[:, c, 128:256], nb)
                    P.copy(alt(), VcT[:, n0:n0 + nn], pst[:, 0:nn])

            def evac(t, sl, pst, br=br):
                stg = stage_t[t % 2]
                P.copy("act", stg[:, 0:256], pst[:, 0:256])
                P.copy("pool", Vt[:, br, t, :].rearrange("p (g e) -> p g e", g=2)[:, :, 0:64],
                       stg[:, 128:256].rearrange("p (g d) -> p g d", g=2))
                if t == 16:
                    P.copy("pool", Vs16[:, br, 0:128], stg[:, 128:256])
                if br == 0:
                    P.act(gs[:, t, :], pst[:, 256:280], AF.Sigmoid)
                if br < 2:
                    P.dma("sp", outs[br][l, sl, :], stg[:, 0:256])
                else:
                    if 12 <= t < 16:
                        P.dma("sp", nwin_p[l, (t - 12) * 128:(t - 11) * 128, :], stg[:, 0:256])
                    if t == 16:
                        for b in range(16):
                            P.dma("sp", nwin_s[l, b, 504:512, :], stg[b * 8:(b + 1) * 8, 0:256])
            tm_group(l, half, 280 if br == 0 else 256, evac)
            if br == 1:
                P.dma("pool", WB[:, :, 512:1024], wl[:, :, 2816:3328])
        P.dma("sp", nwin_s[l][:, 0:504, :], swin[l][:, 8:512, :])
        tm_group(l, 1, 512, lambda t, sl, pst: P.act(sza[:, t, :], pst[:, 0:512], AF.Silu))

    ones_b = sb("ones_b", [128, 128], BF16)
    P.memset("pool", ones_b[:], 1.0)

    def phase3(l):
        for ch in range(4):
            for hlf in range(2):
                P.transpose(psD[:, (hlf * 4 + ch) % 4 * 128:((hlf * 4 + ch) % 4 + 1) * 128],
                            glu_tail[:, ch, hlf * 128:(hlf + 1) * 128], ident_f[:])
                P.copy("dve", tailtok[:, hlf, ch * 128:(ch + 1) * 128],
                       psD[:, (hlf * 4 + ch) % 4 * 128:((hlf * 4 + ch) % 4 + 1) * 128])
        P.dma("sp", nconv_p[l], tailtok[98:128, 0, :])
        for b in range(16):
            P.dma("sp", nconv_s[l, b, 22:30, :], tailtok[b * 8:(b + 1) * 8, 1, :])
        P.dma("sp", nconv_s[l][:, 0:22, :], sconv[l].rearrange("(b r) c -> b r c", r=30)[:, 8:30, :])
        P.dma("sp", cwraw, convp[l])
        for ch in range(4):
            P.transpose(psD[:, 0:34], cwraw[0:34, ch * 128:(ch + 1) * 128], ident_f[0:34, 0:34])
            P.copy("dve", cw[:, ch, :], psD[:, 0:34])
        for r in range(4):
            nr = 128 if r < 3 else 96
            P.dma("sp", scraw[0:nr, :], sconv[l, r * 128:r * 128 + nr, :])
            P.copy("pool", scb[0:nr, r, :], scraw[0:nr, :])
        for ch in range(4):
            pT = ptb()
            for r in range(4):
                nr = 128 if r < 3 else 96
                P.transpose(pT[:, r * 128:r * 128 + nr], scb[0:nr, r, ch * 128:(ch + 1) * 128], ident_b[0:nr, 0:nr])
            P.copy("dve", gluS[:, ch, :, 0:30], pT[:, 0:480].rearrange("p (b r) -> p b r", r=30))
        for ch in range(4):
            for kk in range(31):
                P.ts("dve" if kk % 2 else "pool", Dk[:, kk, :], ident_f[:], cw[:, ch, kk:kk + 1], None, ALU.mult)
            for nb in range(5):
                n0, nn = NB5[nb]
                pst = rr()
                for kk in range(31):
                    if nb < 4:
                        rhs = gluP[:, ch, n0 + kk:n0 + kk + 512]
                    else:
                        rhs = gluS[:, ch, :, kk:kk + 8]
                    P.matmul(pst[:, 0:nn], Dk[:, kk, :], rhs, start=(kk == 0), stop=(kk == 30))
                P.act(mixc[:, ch, n0:n0 + nn], pst[:, 0:nn], AF.Identity, bias=cw[:, ch, 31:32])
        for nb in range(5):
            n0, nn = NB5[nb]
            for ch in range(4):
                P.act(sq[:, 0:nn], mixc[:, ch, n0:n0 + nn], AF.Square)
                P.matmul(psD[:, 0:nn], ones_b[:], mixc[:, ch, n0:n0 + nn], start=(ch == 0), stop=(ch == 3))
                P.matmul(psE[:, 0:nn], ones_f[:], sq[:, 0:nn], start=(ch == 0), stop=(ch == 3))
            P.ts("dve", st_m[:, 0:nn], psD[:, 0:nn], 1.0 / 512, None, ALU.mult)
            P.tt("dve", st_v[:, 0:nn], st_m[:, 0:nn], st_m[:, 0:nn], ALU.mult)
            P.stt(st_v[:, 0:nn], psE[:, 0:nn], 1.0 / 512, st_v[:, 0:nn], ALU.mult, ALU.subtract)
            P.act(st_t[:, 0:nn], st_v[:, 0:nn], AF.Sqrt, bias=epsb[:, 0:1])
            recip(st_t[:, 0:nn], st_t[:, 0:nn])
            for ch in range(4):
                P.tt("dve", sq[:, 0:nn], mixc[:, ch, n0:n0 + nn], st_m[:, 0:nn], ALU.subtract)
                P.tt("dve", sq[:, 0:nn], sq[:, 0:nn], st_t[:, 0:nn], ALU.mult)
                P.act(sq[:, 0:nn], sq[:, 0:nn], AF.Silu, bias=cw[:, ch, 33:34], scale=cw[:, ch, 32:33])
                P.tt("dve", mixc[:, ch, n0:n0 + nn], sq[:, 0:nn], zc[:, ch, n0:n0 + nn], ALU.mult)

    mixa = uT

    def phase5(l, src, last):
        P.dma("pool", WA[:], wrow(w_out, l))
        P.dma("pool", WB[:], wrow(w_pg, l))
        P.dma("pool", WC, wrow(w_ple, l))
        if last:
            P.dma("sp", gbc, fng[0:1, :].broadcast_to([128, D]))
        hms = [aF[:, 3072:4096], aF[:, 5376:6400]]
        plts = [aF[:, 5120:5376], aF[:, 6400:6656]]
        plbs = [aB[:, 0:256], aB[:, 4096:4352]]
        plTs = [aB[:, 256:512].rearrange("p (c n) -> p c n", c=2), aB[:, 4352:4608].rearrange("p (c n) -> p c n", c=2)]
        hmTs = [aB[:, 512:1536].rearrange("p (c n) -> p c n", c=8), aB[:, 3072:4096].rearrange("p (c n) -> p c n", c=8)]
        ubs = [ub, aB[:, 2048:3072]]
        def stageA(t):
            sl = slice(t * 128, (t + 1) * 128)
            p2 = t % 2
            xt = xin[p2]
            hm, plt, plb, plT, hmT, ubp = hms[p2], plts[p2], plbs[p2], plTs[p2], hmTs[p2], ubs[p2]
            P.dma("sp", xt, src[sl, :])
            P.dma("sp", plt, ple_tok[l, sl, :])
            for hf in range(2):
                pst = [psA, psB][hf]
                for e in range(8):
                    lhs = mixc[:, e, sl] if e < 4 else mixa[:, e - 4, sl]
                    P.matmul(pst[:, 0:512], lhs, WA[:, e, hf * 512:(hf + 1) * 512], start=(e == 0), stop=(e == 7))
                P.tt("dve", hm[:, hf * 512:(hf + 1) * 512], pst[:, 0:512], xt[:, hf * 512:(hf + 1) * 512], ALU.add)
            P.copy("act", ubp, hm)
            P.copy("pool", plb, plt)

        def stageA2(t):
            p2 = t % 2
            hm, plt, plb, plT, hmT, ubp = hms[p2], plts[p2], plbs[p2], plTs[p2], hmTs[p2], ubs[p2]
            pT = ptb()
            for c in range(8):
                P.transpose(pT[:, c * 128:(c + 1) * 128], ubp[:, c * 128:(c + 1) * 128], ident_b[:])
            P.copy("act", hmT, pT[:].rearrange("p (c n) -> p c n", c=8))
            pT = ptb()
            for c in range(2):
                P.transpose(pT[:, c * 128:(c + 1) * 128], plb[:, c * 128:(c + 1) * 128], ident_b[:])
            P.copy("act", plT, pT[:, 0:256].rearrange("p (c n) -> p c n", c=2))

        def stageB(t):
            sl = slice(t * 128, (t + 1) * 128)
            p2 = t % 2
            xt = xin[p2]
            hm, plt, plb, plT, hmT, ubp = hms[p2], plts[p2], plbs[p2], plTs[p2], hmTs[p2], ubs[p2]
            gt = xt
            for hf in range(2):
                pg = [psC, psD][hf]
                pp = [psE, psF][hf]
                for c in range(8):
                    P.matmul(pg[:, 0:512], hmT[:, c, :], WB[:, c, hf * 512:(hf + 1) * 512], start=(c == 0), stop=(c == 7))
                for c in range(2):
                    P.matmul(pp[:, 0:512], plT[:, c, :], WC[:, c, hf * 512:(hf + 1) * 512], start=(c == 0), stop=(c == 1))
                P.act(gt[:, hf * 512:(hf + 1) * 512], pg[:, 0:512], AF.Sigmoid)
                P.tt("dve", gt[:, hf * 512:(hf + 1) * 512], gt[:, hf * 512:(hf + 1) * 512], pp[:, 0:512], ALU.mult)
            P.tt("dve", hm, hm, gt, ALU.add)
            if not last:
                P.dma("sp", h1[sl, :], hm)
            else:
                P.act(junk, hm, AF.Square, accum_out=ssq[:, t:t + 1])
                P.act(rs[:, t:t + 1], ssq[:, t:t + 1], AF.Sqrt, bias=epsb[:, 0:1], scale=1.0 / D)
                recip(rs[:, t:t + 1], rs[:, t:t + 1])
                P.stt(gt, hm, rs[:, t:t + 1], gbc, ALU.mult, ALU.mult)
                P.dma("sp", y[sl, :], gt)

        stageA(0)
        stageA2(0)
        for t in range(NT):
            if t + 1 < NT:
                stageA(t + 1)
            stageB(t)
            if t + 1 < NT:
                stageA2(t + 1)

    strip = sb("strip", [128, 8, 256], BF16)
    band = sb("band", [128, 8, 128], BF16)
    bands = sb("bands", [128, 8, 8], BF16)
    expand = sb("expand", [128, T], BF16)
    cover = sb("cover", [127, 34], BF16)
    zpad = sb("zpad", [128, 270], BF16)
    m1c1 = aF[:, 5704:5836].rearrange("p (a b) -> p a b", a=2)
    m1s = sb("m1s", [8, 33], F32)
    c1s = sb("c1s", [8, 33], F32)
    acm = sb("acm", [128, 128], BF16)
    tb = sb("tb", [33, 8], F32)
    r31 = sb("r31", [32, 8], F32)
    ohs = aF[0:33, 6000:6383]
    tbh = aF[0:33, 6400:6528]
    w2 = sb("w2", [128, 2, 64], BF16)
    cvec = sb("cvec", [128, 2], F32)
    peraw = aF[0:32, 5576:5704].rearrange("p (kv d) -> p kv d", kv=2)
    peT = sb("peT", [64, 2, 32], BF16)
    Vs16 = aB[:, 7424:7820].rearrange("p (a b) -> p a b", a=3)
    idx = sb("idx", [128, 256], I32)
    idxf = aF[:, 6400:6656]
    iot = sb("iot", [128, 1], F32)
    smallf = aF[:, 5836:5964]

    def AP(t, off, pat):
        return bass.AP(t.tensor, off, pat)

    def setup_attn():
        P.memset("pool", expand[32:64, :], 0.0)
        P.memset("pool", expand[64:128, :], 0.0)
        P.memset("pool", zpad[:], 0.0)
        P.memset("pool", band[:], 0.0)
        P.memset("pool", bands[:], 0.0)
        P.dma("sp", expand[0:33, :], c_expand)
        P.dma("sp", cover[:], c_cover)
        P.dma("sp", zpad[0:16, :], c_zpad)
        P.dma("sp", m1s[:], c_m1s)
        P.dma("sp", c1s[:], c_c1s)
        P.dma("sp", acm[:], c_ac)
        P.dma("sp", ohs, c_oh)
        P.dma("sp", iot[:], c_iota)
        P.dma("sp", tb[0:32, :], rel_bias)
        P.dma("sp", r31[:], rel_bias[31:32, :].broadcast_to([32, 8]))
        P.tt("dve", tb[0:32, :], tb[0:32, :], r31[:], ALU.subtract)
        P.memset("dve", tb[32:33, :], -BIG)
        gsb = aF[:, 3000:3383]
        for h in range(8):
            P.ts("dve", tbh, ones_f[0:33, :], tb[:, h:h + 1], None, ALU.mult)
            P.matmul(psD[:, 0:383], tbh, ohs, start=True, stop=True)
            P.copy("act", gsb, psD[:, 0:383])
            P.dma("sp", AP(G1, h * 128 * 512, [[513, 128], [1, 383]]), gsb)
            P.dma("sp", AP(G16, h * 16 * 640, [[656, 16], [1, 383]]), gsb[0:16, :])
        strip_f = aF[:, 3400:5448].rearrange("p (h x) -> p h x", h=8)
        band_f = aF[0:16, 5448:6472].rearrange("p (h x) -> p h x", h=8)
        bands_f = aF[0:16, 6472:6536].rearrange("p (h x) -> p h x", h=8)
        P.dma("sp", strip_f, AP(G1, 127, [[512, 128], [65536, 8], [1, 256]]))
        P.dma("sp", band_f, AP(G16, 240, [[640, 16], [10240, 8], [1, 128]]))
        P.dma("sp", bands_f, AP(G16, 352, [[640, 16], [10240, 8], [1, 8]]))
        P.copy("dve", strip[:], strip_f)
        P.copy("dve", band[0:16], band_f)
        P.copy("dve", bands[0:16], bands_f)
        P.dma("sp", idx[:], ptab[0:1, :].broadcast_to([128, 256]))
        P.copy("dve", idxf, idx[:])
        P.ts("dve", idxf, idxf, 128.0, iot[:, 0:1], ALU.mult, ALU.add)
        P.copy("dve", idx[:], idxf)

    w1 = WA[:].rearrange("p c n -> p (c n)").rearrange("p (kv pos h) -> p kv pos h", kv=2, pos=32)
    kcT = aB[:, 6400:6528]
    vcaug = aB[0:127, 6528:6724].rearrange("p (g e) -> p g e", g=2)
    hid = aB[:, 6724:6852]
    hid = aB[:, 7820:8076]

    def load_cmp_weights(l):
        for kv in range(2):
            for cp in range(2):
                P.dma("pool", w1[cp * 64:(cp + 1) * 64, kv, :, :],
                      cmp_w1[l, kv].rearrange("(pos d) h -> d pos h", d=64))
        P.dma("pool", w2[:], cmp_w2[l].rearrange("kv h d -> h kv d"))
        P.dma("sp", peraw, cmp_pe[l].rearrange("kv pos d -> pos kv d"))
        for kv in range(2):
            P.transpose(psD[0:64, kv * 32:(kv + 1) * 32], peraw[0:32, kv, :], ident_f[0:32, 0:32])
        P.copy("dve", peT[:], psD[0:64, 0:64].rearrange("p (kv pos) -> p kv pos", kv=2))
        for kv in range(2):
            for pos in range(32):
                P.matmul(psD[:, 64 + kv:65 + kv], w1[0:64, kv, pos, :], peT[:, kv, pos:pos + 1],
                         start=(pos == 0), stop=(pos == 31))
        P.copy("dve", cvec[:], psD[:, 64:66])

    cmp_tiles = [kcT, vcaug, hid]

    def compress(Ksrc, Vsrc):
        kcT, vcaug, hid = cmp_tiles
        for g in range(2):
            P.copy("pool", vcaug[:, g, 64:98], cover[:, :])
        for kv in range(2):
            src = Ksrc if kv == 0 else Vsrc
            pg = [psA, psB] if kv == 0 else [psC, psA]
            for pos in range(32):
                for g in range(2):
                    gb = g * 64
                    P.matmul(pg[g][:, 0:127], w1[gb:gb + 64, kv, pos, :], src[gb:gb + 64, pos:pos + 16 * 126 + 1:16],
                             start=(pos == 0), stop=(pos == 31))
            for g in range(2):
                P.act(hid[:, g * 127:(g + 1) * 127], pg[g][:, 0:127], AF.Silu, bias=cvec[:, kv:kv + 1])
            for g in range(2):
                gb = g * 64
                hg = hid[:, g * 127:(g + 1) * 127]
                pw = rr()
                if kv == 0:
                    P.matmul(pw[gb:gb + 64, 0:127], w2[:, 0, :], hg, start=True, stop=True)
                    P.copy("dve", kcT[gb:gb + 64, 0:127], pw[gb:gb + 64, 0:127])
                else:
                    P.matmul(pw[0:127, 128 + g * 64:192 + g * 64], hg, w2[:, 1, :], start=True, stop=True)
                    P.copy("dve", vcaug[:, g, 0:64], pw[0:127, 128 + g * 64:192 + g * 64])

    es = [aB[:, i * 512:(i + 1) * 512] for i in range(3)]
    ecmp = aB[:, 1536:2048]
    negT = aB[:, 2048:6400].rearrange("p (g t) -> p g t", g=2)
    negb2 = [aB[:, 6852:6885], aB[:, 8076:8109]]
    attnb = aB[:, 6912:7424]
    accb = aF[:, 0:2048].rearrange("p (q e) -> p q e", q=4)
    cmpo = aF[:, 2048:2440].rearrange("p (r e) -> p r e", r=4)
    tmpo = aF[:, 2440:2960].rearrange("p (q e) -> p q e", q=4)
    rinv = smallf[:, 0:4]
    coef = smallf[:, 4:8]
    imp = smallf[:, 8:41]
    m8 = smallf[:, 48:56]
    ecnt = [0]
    ecmp_ref = [ecmp]

    def cmp_branch(rows, nc_, g, qcols, band_lhsT, band_rhs_of_h, gate_ap, acc_of_h, m1, c1, negT_out, negb, deferred):
        gb = g * 64
        kcT, vcaug, hid = cmp_tiles
        for r in range(4):
            h = g * 4 + r
            P.matmul(psG[0:nc_, r * rows:(r + 1) * rows], kcT[gb:gb + 64, 0:nc_], qcols(r), start=True, stop=False)
            P.matmul(psG[0:nc_, r * rows:(r + 1) * rows], band_lhsT, band_rhs_of_h(h), start=False, stop=True)
        ecm = ecmp_ref[0]
        P.act(ecm[0:nc_, 0:4 * rows], psG[0:nc_, 0:4 * rows], AF.Exp)
        for r in range(4):
            P.matmul(psF[0:rows, r * 98:(r + 1) * 98], ecm[0:nc_, r * rows:(r + 1) * rows], vcaug[0:nc_, g, :],
                     start=True, stop=True)
        P.copy("act", cmpo[0:rows], psF[0:rows, 0:392].rearrange("p (r e) -> p r e", r=4))
        P.ts("dve", rinv[0:rows], cmpo[0:rows, :, 97], 1e-30, None, ALU.add)
        recip(rinv[0:rows], rinv[0:rows])
        P.ts("dve", imp[0:rows], cmpo[0:rows, 0, 64:97], rinv[0:rows, 0:1], None, ALU.mult)
        for r in range(1, 4):
            P.stt(imp[0:rows], cmpo[0:rows, r, 64:97], rinv[0:rows, r:r + 1], imp[0:rows], ALU.mult, ALU.add)
        P.tt("dve", coef[0:rows], rinv[0:rows], gate_ap, ALU.mult)
        for r in range(4):
            P.ts("dve", acc_of_h(g * 4 + r), cmpo[0:rows, r, 0:64], coef[0:rows, r:r + 1], None, ALU.mult)
        P.tt("dve", imp[0:rows], imp[0:rows], m1, ALU.mult)
        P.tt("dve", imp[0:rows], imp[0:rows], c1, ALU.add)
        P.add("dve", lambda e: e.max(out=m8[0:rows], in_=imp[0:rows]), reads=[imp[0:rows]], writes=[m8[0:rows]])
        P.ts("dve", negb[0:rows], imp[0:rows], m8[0:rows, 7:8], -BIG, ALU.is_lt, ALU.mult)
        def fin():
            pT = ptb()
            P.transpose(pT[0:33, 0:rows], negb[0:rows], ident_b[0:rows, 0:rows])
            P.copy("dve", negT_out[0:33], pT[0:33, 0:rows])
        deferred.append(fin)

    def finish_head(rows, nq, pso, width, ocol, scol, gate_of_q, acc_of_q):
        P.copy("act", tmpo[0:rows, 0:nq, 0:width], pso[0:rows, 0:nq * width].rearrange("p (q e) -> p q e", q=nq))
        P.ts("dve", rinv[0:rows, 0:nq], tmpo[0:rows, 0:nq, scol], 1e-30, None, ALU.add)
        recip(rinv[0:rows, 0:nq], rinv[0:rows, 0:nq])
        P.tt("dve", coef[0:rows, 0:nq], rinv[0:rows, 0:nq], gate_of_q, ALU.mult)
        for qi in range(nq):
            a = acc_of_q(qi)
            P.stt(a, tmpo[0:rows, qi, ocol:ocol + 64], coef[0:rows, qi:qi + 1], a, ALU.mult, ALU.add)

    def attention_prompt(l):
        cmp_tiles[:] = [kcT, vcaug, hid]
        ecmp_ref[0] = ecmp
        compress(KT[:, 0, 0:TP], VcT[:, 0:TP])
        yield
        P.memset("pool", negT[32:64], 0.0)
        P.memset("pool", negT[64:128], 0.0)
        pso2 = [psD, psE]
        dfr = []
        for qb in range(4):
            q0 = qb * 512
            for qi in range(4):
                qt = qb * 4 + qi
                qs = slice(qt * 128, (qt + 1) * 128)
                nc_ = min(127, 8 * qt + 7)
                c_off = 8 * qt - 9
                mc = m1c1[:, qt % 2, :]
                P.dma("sp", mc[:, 0:33], c_m1p[:, qt * 33:(qt + 1) * 33])
                P.dma("sp", mc[:, 33:66], c_c1p[:, qt * 33:(qt + 1) * 33])
                for g in range(2):
                    cmp_tiles[:] = [kcT, vcaug, hid]
                    ecmp_ref[0] = ecmp
                    cmp_branch(128, nc_, g,
                               lambda r, g=g, qs=qs: qT[g * 64:(g + 1) * 64, r, qs],
                               zpad[:, 127 - c_off:127 - c_off + nc_],
                               lambda h: band[:, h, :],
                               gs[:, qt, g * 4:g * 4 + 4],
                               lambda h, qi=qi: accb[:, qi, h * 64:(h + 1) * 64],
                               mc[:, 0:33], mc[:, 33:66],
                               negT[:, g, qs], negb2[g], dfr)
                yield
                for fn_ in dfr:
                    fn_()
                del dfr[:]
            for h in range(8):
                g, r = h // 4, h % 4
                gb = g * 64
                for br in (1, 2):
                    pso = pso2[br - 1]
                    P.memset("dve", pso[:, 0:260], 0.0)
                    kt_lo = 0 if br == 1 else max(0, 4 * qb - 4)
                    pend = None
                    for kt in range(kt_lo, 4 * qb + 4):
                        qi_lo = max(0, kt - 4 * qb)
                        qi_hi = 3 if br == 1 else min(3, kt + 4 - 4 * qb)
                        c_lo, c_hi = qi_lo * 128, (qi_hi + 1) * 128
                        pss = rr()
                        ks = slice(kt * 128, (kt + 1) * 128)
                        extra = []
                        if br == 1:
                            extra.append((c_lo, c_hi, expand[:, ks], negT[:, g, q0 + c_lo:q0 + c_hi]))
                        d0 = kt - 4 * qb
                        if 0 <= d0 <= 2:
                            extra.append((d0 * 128, d0 * 128 + 256, ident_b[:], strip[:, h, 0:256]))
                        elif d0 == 3:
                            extra.append((384, 512, ident_b[:], strip[:, h, 0:128]))
                        elif d0 == -1:
                            extra.append((0, 128, ident_b[:], strip[:, h, 128:256]))
                        if br == 2 and 0 <= d0 + 4 <= 3:
                            extra.append(((d0 + 4) * 128, (d0 + 5) * 128, ident_b[:], acm[:]))
                        P.matmul(pss[:, c_lo:c_hi], KT[gb:gb + 64, br, ks], qT[gb:gb + 64, r, q0 + c_lo:q0 + c_hi],
                                 start=True, stop=(len(extra) == 0))
                        for i, (a, b_, lt, rh) in enumerate(extra):
                            P.matmul(pss[:, a:b_], lt, rh, start=False, stop=(i == len(extra) - 1))
                        ecnt[0] += 1
                        e = es[ecnt[0] % 3]
                        P.act(e[:, c_lo:c_hi], pss[:, c_lo:c_hi], AF.Exp)

                        def pv(e=e, kt=kt, qi_lo=qi_lo, qi_hi=qi_hi):
                            for qi in range(qi_lo, qi_hi + 1):
                                P.matmul(pso[:, qi * 65:(qi + 1) * 65], e[:, qi * 128:(qi + 1) * 128],
                                         Vt[:, br, kt, g * 65:(g + 1) * 65], start=False, stop=False, skip_group_check=True)
                        if pend is not None:
                            pend()
                        pend = pv
                        if (kt - kt_lo) % 4 == 3:
                            yield
                    pend()
                    finish_head(128, 4, pso, 65, 0, 64,
                                gs[:, qb * 4:qb * 4 + 4, br * 8 + h],
                                lambda qi, h=h: accb[:, qi, h * 64:(h + 1) * 64])
                    yield
            for qi in range(4):
                qt = qb * 4 + qi
                P.tt("dve", attnb, accb[:, qi, :], sza[:, qt, :], ALU.mult)
                pT = ptb()
                for c in range(4):
                    P.transpose(pT[:, c * 128:(c + 1) * 128], attnb[:, c * 128:(c + 1) * 128], ident_b[:])
                P.copy("act", mixa[:, 0:4, qt * 128:(qt + 1) * 128], pT[:, 0:512].rearrange("p (c n) -> p c n", c=4))

    def attention_sample(l):
        WBf = WB[:].rearrange("p c n -> p (c n)")
        KT0 = KT[:, 0, :]
        cmpq = [KT0[:, i * 512:(i + 1) * 512].rearrange("p (j c) -> p j c", j=2) for i in range(4)]
        slcc = WBf[:, 0:4160].rearrange("p (j c) -> p j c", j=16)
        winc = WBf[:, 4160:5200].rearrange("p (j c) -> p j c", j=4)
        KcT_s = uT[:, 4, 0:2048]
        VcT_s = uT[:, 5, 0:2048]
        KsT = uT[:, 6, 0:2048]
        KwT = uT[:, 7, 0:512]
        e_s = WBf[:, 5200:5744]
        ecmp_s = WBf[:, 5744:5808]
        negT_s = WBf[:, 5808:5872].rearrange("p (g r q) -> p g r q", g=2, r=4)
        ac32 = WBf[:, 5872:5904].rearrange("p (r q) -> p r q", r=4)
        Vnew = WBf[0:8, 5904:6300].rearrange("p (a b) -> p a b", a=3)
        szab = WBf[0:8, 6300:6812]
        attn_s = WBf[0:8, 6812:7324]
        kcT_s = WBf[:, 7324:7452]
        vcaug_s = WBf[0:127, 7452:7648].rearrange("p (g e) -> p g e", g=2)
        hid_s = WBf[:, 7648:7904]
        negb_s = [WBf[:, 7904:7937], WBf[:, 7940:7973]]
        dfs = []
        accs = aF[0:8, 3200:3712]
        gsb_b = aF[0:8, 2960:2984]
        o32 = aF[0:32, 2984:3113]
        on32 = aF[0:32, 3113:3177]
        r32 = aF[0:32, 3177:3178]
        if l == 1:
            P.copy("dve", idxf, idx[:])
            P.ts("dve", idxf, idxf, float(npool * 128), None, ALU.add)
            P.copy("dve", idx[:], idxf)
        P.memset("pool", slcc[:, :, 256:260], 1.0)
        P.memset("pool", winc[:, :, 256:260], 1.0)
        P.memset("pool", negT_s[32:64], 0.0)
        P.memset("pool", negT_s[64:128], 0.0)
        for r in range(4):
            P.copy("pool", ac32[:, r, :], acm[:, 0:8])
        for b in range(16):
            tk = slice(TP + b * 8, TP + b * 8 + 8)
            def gather(dst_ap, srcd, col):
                def f(e):
                    return e.indirect_dma_start(out=dst_ap, out_offset=None, in_=srcd.rearrange("l r c -> (l r) c"),
                                                in_offset=bass.IndirectOffsetOnAxis(idx[:, col:col + 1], 0))
                P.add("pool", f, reads=[idx[:, col:col + 1]], writes=[dst_ap], is_dma=True)
            for j in range(16):
                gather(slcc[:, j, 0:256], cs, b * 16 + j)
            P.dma("pool", winc[:, :, 0:256], swin[l, b].rearrange("(j p) c -> p j c", p=128))
            P.dma("sp", Vnew, Vs16[b * 8:(b + 1) * 8])
            P.dma("sp", szab, sza[b * 8:(b + 1) * 8, 16, :])
            P.dma("sp", gsb_b, gs[b * 8:(b + 1) * 8, 16, :])
            if b == 0:
                for st_ in range(4):
                    for jj in range(2):
                        gather(cmpq[st_][:, jj, :], cc, b * 16 + st_ * 2 + jj)
            for st_ in range(8):
                cq = cmpq[st_ % 4]
                if st_ % 2 == 0:
                    pT = ptb()
                for jj in range(2):
                    sl_ = (st_ % 2) * 2 + jj
                    P.transpose(pT[:, sl_ * 128:(sl_ + 1) * 128], cq[:, jj, 0:128], ident_b[:])
                    P.transpose(pT[:, (4 + sl_) * 128:(5 + sl_) * 128], cq[:, jj, 128:256], ident_b[:])
                if st_ + 4 < 8:
                    for jj in range(2):
                        gather(cq[:, jj, :], cc, b * 16 + (st_ + 4) * 2 + jj)
                if st_ % 2 == 1:
                    pr = st_ // 2
                    P.copy(alt(), KcT_s[:, pr * 512:(pr + 1) * 512], pT[:, 0:512])
                    P.copy(alt(), VcT_s[:, pr * 512:(pr + 1) * 512], pT[:, 512:1024])
            for (src_t, c0, dstT, nj) in ((slcc, 0, KsT, 16), (winc, 0, KwT, 4)):
                for j0 in range(0, nj, 8):
                    n8 = min(8, nj - j0)
                    pT = ptb()
                    for j in range(j0, j0 + n8):
                        P.transpose(pT[:, (j - j0) * 128:(j - j0 + 1) * 128], src_t[:, j, c0:c0 + 128], ident_b[:])
                    P.copy(alt(), dstT[:, j0 * 128:(j0 + n8) * 128], pT[:, 0:n8 * 128])
            yield
            cmp_tiles[:] = [kcT_s, vcaug_s, hid_s]
            ecmp_ref[0] = ecmp_s
            compress(KcT_s, VcT_s)
            yield
            cmp_tiles[:] = [kcT_s, vcaug_s, hid_s]
            ecmp_ref[0] = ecmp_s
            for g in range(2):
                cmp_branch(8, 127, g,
                           lambda r, g=g: qT[g * 64:(g + 1) * 64, r, tk],
                           zpad[:, 15:142],
                           lambda h: bands[:, h, :],
                           gsb_b[:, g * 4:g * 4 + 4],
                           lambda h: accs[:, h * 64:(h + 1) * 64],
                           m1s[:], c1s[:],
                           negT_s[:, g, 0, :], negb_s[g], dfs)
            if b + 1 < 16:
                for st_ in range(4):
                    for jj in range(2):
                        gather(cmpq[st_][:, jj, :], cc, (b + 1) * 16 + st_ * 2 + jj)
            yield
            for fn_ in dfs:
                fn_()
            del dfs[:]
            for g in range(2):
                for r in range(1, 4):
                    P.copy("pool", negT_s[0:33, g, r, :], negT_s[0:33, g, 0, :])
            for g in range(2):
                gb = g * 64
                q32 = qT[gb:gb + 64, 0:4, tk]
                for br in (1, 2):
                    pss = rr()
                    KpT = KsT if br == 1 else KwT
                    cch = slcc if br == 1 else winc
                    nj = 16 if br == 1 else 4
                    for j in range(nj):
                        cs_ = slice(j * 32, (j + 1) * 32)
                        extra = []
                        if br == 1:
                            extra.append((expand[:, j * 128:(j + 1) * 128], negT_s[:, g, :, :]))
                        if br == 2 and j == 0:
                            extra.append((ident_b[:], ac32))
                        if j == nj - 1:
                            extra.append((ident_b[:], strip[:, g * 4:(g + 1) * 4, 128:136]))
                        P.matmul(pss[:, cs_], KpT[gb:gb + 64, j * 128:(j + 1) * 128], q32, start=True, stop=(len(extra) == 0))
                        for i, (lt, rh) in enumerate(extra):
                            P.matmul(pss[:, cs_], lt, rh, start=False, stop=(i == len(extra) - 1))
                    P.matmul(psG[0:8, 0:32], KT[gb:gb + 64, br, tk], q32, start=True, stop=False)
                    P.matmul(psG[0:8, 0:32], ident_b[:, 0:8], strip[:, g * 4:(g + 1) * 4, 0:8], start=False, stop=True)
                    P.act(e_s[:, 0:nj * 32], pss[:, 0:nj * 32], AF.Exp)
                    P.act(e_s[0:8, 512:544], psG[0:8, 0:32], AF.Exp)
                    pso = rr()
                    for j in range(nj):
                        P.matmul(pso[0:32, 0:129], e_s[:, j * 32:(j + 1) * 32], cch[:, j, 128:257], start=(j == 0), stop=False)
                    P.matmul(pso[0:32, 0:129], e_s[0:8, 512:544], Vnew[:, br, 0:129], start=False, stop=True)
                    P.copy("act", o32, pso[0:32, 0:129])
                    P.ts("dve", r32, o32[:, 128:129], 1e-30, None, ALU.add)
                    recip(r32, r32)
                    P.ts("dve", on32, o32[:, g * 64:(g + 1) * 64], r32[:, 0:1], None, ALU.mult)
                    for r in range(4):
                        P.matmul(psF[0:8, r * 64:(r + 1) * 64], ident_f[0:32, r * 8:(r + 1) * 8], on32, start=True, stop=True)
                    for r in range(4):
                        h = g * 4 + r
                        a = accs[:, h * 64:(h + 1) * 64]
                        P.stt(a, psF[0:8, r * 64:(r + 1) * 64], gsb_b[:, br * 8 + h:br * 8 + h + 1], a, ALU.mult, ALU.add)
                    yield
            P.tt("dve", attn_s, accs, szab, ALU.mult)
            pT = ptb()
            for c in range(4):
                P.transpose(pT[:, c * 8:(c + 1) * 8], attn_s[:, c * 128:(c + 1) * 128], ident_b[0:8, 0:8])
            P.copy("act", mixa[:, 0:4, tk], pT[:, 0:32].rearrange("p (c n) -> p c n", c=4))
            yield

    k.phase1 = phase1
    k.__dict__.update(locals())
    return k


def rel_bucket_np(dist):
    n = np.maximum(dist, 0)
    nf = np.maximum(n, 1).astype(np.float32)
    large = 16 + (np.log(nf / np.float32(16)) / np.float32(np.log(128 / 16)) * np.float32(16)).astype(np.int32)
    large = np.minimum(large, 31)
    return np.where(n < 16, n, large)


def make_consts():
    bf = ml_dtypes.bfloat16
    c = {}
    c["c_ident"] = np.eye(128, dtype=np.float32)
    d = np.arange(-127, 256)
    oh = np.zeros((33, 383), np.float32)
    b = rel_bucket_np(d)
    for i, dd in enumerate(d):
        if dd < 0:
            oh[32, i] = 1.0
        else:
            oh[b[i], i] = 1.0
    c["c_oh"] = oh
    pos = np.arange(T)
    ex = np.zeros((33, T), np.float32)
    ex[np.minimum(pos // 64, 32), pos] = 1.0
    c["c_expand"] = ex.astype(bf)
    cidx = np.arange(127)
    c_start = cidx * 16
    c_end = c_start + 31
    s_start = np.arange(33) * 64
    cover = ((c_start[:, None] < s_start[None, :] + 64) & (c_end[:, None] >= s_start[None, :])).astype(np.float32)
    c["c_cover"] = np.concatenate([cover, np.ones((127, 1), np.float32)], axis=1).astype(bf)
    zp = np.zeros((16, 270), np.float32)
    zp[np.arange(16), 127 + np.arange(16)] = 1.0
    c["c_zpad"] = zp.astype(bf)
    qpos = np.arange(2048)
    cur = qpos // 64
    blk = np.arange(33)
    forced = (blk[None] == 0) | (blk[None] == cur[:, None]) | (blk[None] == cur[:, None] - 1)
    allowed = blk[None] <= cur[:, None]
    m1 = (allowed & ~forced).astype(np.float32)
    c1 = np.where(allowed, np.where(forced, 1e4, 0.0), -1e4).astype(np.float32)
    c["c_m1p"] = m1.reshape(16, 128, 33).transpose(1, 0, 2).reshape(128, 16 * 33).copy()
    c["c_c1p"] = c1.reshape(16, 128, 33).transpose(1, 0, 2).reshape(128, 16 * 33).copy()
    qs = 2048 + np.arange(8)
    curs = qs // 64
    forced = (blk[None] == 0) | (blk[None] == curs[:, None]) | (blk[None] == curs[:, None] - 1)
    allowed = blk[None] <= curs[:, None]
    c["c_m1s"] = (allowed & ~forced).astype(np.float32)
    c["c_c1s"] = np.where(allowed, np.where(forced, 1e4, 0.0), -1e4).astype(np.float32)
    c["c_iota"] = np.arange(128, dtype=np.float32).reshape(128, 1)
    kl = np.arange(128)
    c["c_ac"] = np.where(kl[None, :] >= kl[:, None], -BIG, 0.0).astype(np.float32).astype(bf)
    return c


def core_inputs(inp, c, npool_rows=None):
    m = {}
    m["x_tok"] = np.concatenate([inp["x_prompt"][c], inp["x_sample"][16 * c:16 * c + 16].reshape(128, D)], 0)
    m["ple_tok"] = np.concatenate([inp["p_prompt"][:, c], inp["p_sample"][:, 16 * c:16 * c + 16].reshape(2, 128, 256)], 1)
    m["cc"] = inp["cache_cmp_kv"].reshape(2, -1, 256)
    m["cs"] = inp["cache_slc_kv"].reshape(2, -1, 256)
    m["ptab"] = inp["page_table"][16 * c:16 * c + 16].reshape(1, 256).astype(np.int32)
    m["swin"] = inp["state_win_kv"][:, 16 * c:16 * c + 16].reshape(2, 16, 512, 256)
    m["sconv"] = inp["state_conv"][:, 16 * c:16 * c + 16].reshape(2, 480, 512)
    m["norm_g"] = inp["norm_g"]
    m["w_in"] = inp["w_in"]
    m["convp"] = np.concatenate([inp["conv_w"], inp["conv_b"][:, None], inp["conv_ln_g"][:, None],
                                 inp["conv_ln_b"][:, None]], 1)
    m["cmp_pe"] = inp["cmp_pe"]
    m["cmp_w1"] = inp["cmp_w1"].reshape(2, 2, 2048, 128)
    m["cmp_w2"] = inp["cmp_w2"]
    m["w_out"] = inp["w_out"]
    m["w_ple"] = inp["w_ple"]
    m["w_pg"] = inp["w_ple_gate"]
    m["rel_bias"] = inp["rel_bias"]
    m["fng"] = inp["final_norm_g"].reshape(1, D)
    return {k_: np.ascontiguousarray(v) for k_, v in m.items()}


STAGE = 7


def program(k, stage=99):
    srcs = [k.x_tok, k.h1]
    for l in range(2):
        k.phase1(l, srcs[l])
        k.phase2a(l)
        k.phase3(l)
        k.phase2b(l)
        if stage >= 6:
            if l == 0:
                k.setup_attn()
            k.load_cmp_weights(l)
            gp = k.attention_prompt(l)
            if stage >= 7:
                gsm = k.attention_sample(l)
                next(gp)
                gens = [gp, gsm]
                weight = [4500.0, 5800.0]
                cum = [0.0, 0.0]
                alive = [True, True]
                while any(alive):
                    cand = [i for i in range(2) if alive[i]]
                    i = min(cand, key=lambda i_: cum[i_] / weight[i_])
                    n0 = len(k.P.eng_ops["pe"])
                    try:
                        next(gens[i])
                    except StopIteration:
                        alive[i] = False
                    cum[i] += len(k.P.eng_ops["pe"]) - n0 + 1
            else:
                for _ in gp:
                    pass
                k.P.memset("pool", k.uT[:, 0:4, 2048:2176], 0.0)
        else:
            k.P.memset("pool", k.uT[:, 0:4, :], 0.0)
        k.phase5(l, srcs[l], l == 1)


_CACHE = {}


def kernel(**inp):
    npool = inp["cache_cmp_kv"].shape[1]
    if "nc" not in _CACHE:
        nc = bass.Bass("TRN2", target_bir_lowering=False)
        k = build(nc, npool, stage=STAGE)
        program(k, STAGE)
        k.P.emit()
        _CACHE["nc"] = nc
    nc = _CACHE["nc"]
    consts = make_consts()
    in_maps = []
    for c in range(8):
        m = core_inputs(inp, c)
        if STAGE < 6:
            m.pop("cc")
            m.pop("cs")
        m.update(consts)
        in_maps.append(m)
    res = run_bass_kernel_spmd(nc, in_maps, core_ids=list(range(8)))
    r = res.results
    y_p = np.stack([r[c]["y"][:TP] for c in range(8)])
    y_s = np.concatenate([r[c]["y"][TP:].reshape(16, 8, D) for c in range(8)])

    def kvp(name):
        return np.stack([r[c][name][:, :TP] for c in range(8)], 1).reshape(2, 8, TP, 2, 2, 64)

    def kvs(name):
        return np.concatenate([r[c][name][:, TP:].reshape(2, 16, 8, 256) for c in range(8)], 1).reshape(2, 128, 8, 2, 2, 64)
    win_p = np.stack([r[c]["nwin_p"] for c in range(8)], 1).reshape(2, 8, 512, 2, 2, 64)
    win_s = np.concatenate([r[c]["nwin_s"] for c in range(8)], 1).reshape(2, 128, 512, 2, 2, 64)
    conv_p = np.stack([r[c]["nconv_p"] for c in range(8)], 1)
    conv_s = np.concatenate([r[c]["nconv_s"] for c in range(8)], 1)
    f = lambda a: np.ascontiguousarray(a, dtype=np.float32)
    return (f(y_p), f(y_s), f(kvp("ncmp")), f(kvs("ncmp")), f(kvp("nslc")), f(kvs("nslc")),
            f(win_p), f(win_s), f(conv_p), f(conv_s))
```
